# Optimizing a Trainium2 kernel written in Bass

```python
import math
import jax, jax.numpy as jnp
from jax import lax
import numpy as np

D_MODEL = 1024
BATCH = 8
SEQ = 4096
DEPTH = 2

D_SSM = D_MODEL // 2
SSM_GROUP = 16
SSM_GROUPS = D_SSM // SSM_GROUP
SSM_STATE = 64
DT_MIN = 1e-3
DT_MAX = 1e-1
D_ATTN = D_MODEL - D_SSM
HEAD_DIM = 64
N_HEADS = D_ATTN // HEAD_DIM
MOBA_BLOCK = 256
MOBA_TOPK = 3
Q_CHUNK = 32
REL_BUCKETS = 32
REL_MAX_DIST = 128
CONV_WIDTH = 31
D_CONV = D_MODEL
FFN_HIDDEN = 2816
FFN_CONV_WIDTH = 3
N_EVEN = (DEPTH + 1) // 2
N_ODD = DEPTH // 2
EPS = 1e-6

kernel_name = 'hybrid_s5_moba_conformer_convffn'


def rms_norm(x, g):
    xf = x.astype(jnp.float32)
    y = xf * lax.rsqrt(jnp.mean(xf * xf, axis=-1, keepdims=True) + EPS)
    return (y * g.astype(jnp.float32)).astype(x.dtype)


def layer_norm(x, g, b):
    xf = x.astype(jnp.float32)
    mu = jnp.mean(xf, axis=-1, keepdims=True)
    xc = xf - mu
    y = xc * lax.rsqrt(jnp.mean(xc * xc, axis=-1, keepdims=True) + EPS)
    return (y * g.astype(jnp.float32) + b.astype(jnp.float32)).astype(x.dtype)


def causal_dwconv(x, w, b):
    k = w.shape[0]
    xp = jnp.pad(x, ((0, 0), (k - 1, 0), (0, 0)))
    y = lax.conv_general_dilated(xp, w[:, None, :], (1,), 'VALID',
                                 dimension_numbers=('NWC', 'WIO', 'NWC'),
                                 feature_group_count=x.shape[-1])
    return y + b


def modulate(h, shift, scale):
    return h * (1.0 + scale[:, None, :]) + shift[:, None, :]


def s5_mixer(u, a_re, a_im, log_dt, b_re, b_im, c_re, c_im, d_skip, glu_w, glu_b):
    bsz, seq_len, _ = u.shape
    f32 = jnp.float32
    uf = u.astype(f32).reshape(bsz, seq_len, SSM_GROUPS, SSM_GROUP)
    lam_re = a_re.astype(f32)
    lam_im = a_im.astype(f32)
    dt = jnp.exp(log_dt.astype(f32))[:, None]
    mag = jnp.exp(lam_re * dt)
    ab_re = mag * jnp.cos(lam_im * dt)
    ab_im = mag * jnp.sin(lam_im * dt)
    den = lam_re * lam_re + lam_im * lam_im
    nr = ab_re - 1.0
    coef_re = (nr * lam_re + ab_im * lam_im) / den
    coef_im = (ab_im * lam_re - nr * lam_im) / den
    br = b_re.astype(f32)
    bi = b_im.astype(f32)
    bb_re = coef_re[..., None] * br - coef_im[..., None] * bi
    bb_im = coef_re[..., None] * bi + coef_im[..., None] * br
    bu_re = jnp.einsum('blgh,gph->blgp', uf, bb_re)
    bu_im = jnp.einsum('blgh,gph->blgp', uf, bb_im)
    a_full_re = jnp.broadcast_to(ab_re, (1, seq_len) + ab_re.shape)
    a_full_im = jnp.broadcast_to(ab_im, (1, seq_len) + ab_im.shape)

    def combine(e1, e2):
        a1r, a1i, b1r, b1i = e1
        a2r, a2i, b2r, b2i = e2
        return (a1r * a2r - a1i * a2i,
                a1r * a2i + a1i * a2r,
                a2r * b1r - a2i * b1i + b2r,
                a2r * b1i + a2i * b1r + b2i)

    _, _, xr, xi = lax.associative_scan(combine, (a_full_re, a_full_im, bu_re, bu_im), axis=1)
    y = (jnp.einsum('blgp,ghp->blgh', xr, c_re.astype(f32))
         - jnp.einsum('blgp,ghp->blgh', xi, c_im.astype(f32))
         + d_skip.astype(f32) * uf)
    y = jax.nn.gelu(y.reshape(bsz, seq_len, D_SSM)).astype(u.dtype)
    return y * jax.nn.sigmoid(y @ glu_w + glu_b)


def rel_bucket(dist):
    n = jnp.maximum(dist, 0)
    max_exact = REL_BUCKETS // 2
    nf = jnp.maximum(n, 1).astype(jnp.float32)
    large = max_exact + (jnp.log(nf / max_exact) / math.log(REL_MAX_DIST / max_exact)
                         * (REL_BUCKETS - max_exact)).astype(jnp.int32)
    large = jnp.minimum(large, REL_BUCKETS - 1)
    return jnp.where(n < max_exact, n, large)


def moba_attention(q, k, v, rel_bias):
    bsz, seq_len = q.shape[:2]
    f32 = jnp.float32
    n_blocks = -(-seq_len // MOBA_BLOCK)
    l_pad = n_blocks * MOBA_BLOCK
    pad = ((0, 0), (0, l_pad - seq_len), (0, 0), (0, 0))
    q, k, v = [jnp.pad(t, pad).transpose(0, 2, 1, 3) for t in (q, k, v)]
    kb = k.reshape(bsz, N_HEADS, n_blocks, MOBA_BLOCK, HEAD_DIM)
    vb = v.reshape(bsz, N_HEADS, n_blocks, MOBA_BLOCK, HEAD_DIM)
    k_mean = jnp.mean(kb.astype(f32), axis=3).astype(q.dtype)
    top = min(MOBA_TOPK, n_blocks)
    scale = HEAD_DIM ** -0.5
    bias_tab = rel_bias.astype(f32)
    b_idx = jnp.arange(bsz)[:, None, None, None]
    h_idx = jnp.arange(N_HEADS)[None, :, None, None]
    h_bias = jnp.arange(N_HEADS)[None, :, None, None, None]
    offs = jnp.arange(MOBA_BLOCK)

    def chunk(ci):
        start = ci * Q_CHUNK
        blk = start // MOBA_BLOCK
        q_c = lax.dynamic_slice_in_dim(q, start, Q_CHUNK, axis=2)
        q_pos = start + jnp.arange(Q_CHUNK)
        gate = jnp.einsum('bhqd,bhnd->bhqn', q_c, k_mean).astype(f32)
        gate = jnp.where(jnp.arange(n_blocks) < blk, gate, -jnp.inf)
        _, sel = lax.top_k(gate, top)
        sel_valid = jnp.arange(top) < blk
        k_sel = kb[b_idx, h_idx, sel]
        v_sel = vb[b_idx, h_idx, sel]
        k_own = lax.dynamic_index_in_dim(kb, blk, axis=2, keepdims=False)
        v_own = lax.dynamic_index_in_dim(vb, blk, axis=2, keepdims=False)
        s_sel = jnp.einsum('bhqd,bhqrtd->bhqrt', q_c, k_sel).astype(f32) * scale
        k_pos_sel = sel[..., None] * MOBA_BLOCK + offs
        s_sel = s_sel + bias_tab[rel_bucket(q_pos[:, None, None] - k_pos_sel), h_bias]
        s_sel = jnp.where(sel_valid[:, None], s_sel, -jnp.inf)
        dist_own = q_pos[:, None] - (blk * MOBA_BLOCK + offs)[None, :]
        s_own = jnp.einsum('bhqd,bhtd->bhqt', q_c, k_own).astype(f32) * scale
        s_own = s_own + bias_tab[rel_bucket(dist_own)].transpose(2, 0, 1)
        s_own = jnp.where(dist_own >= 0, s_own, -jnp.inf)
        scores = jnp.concatenate(
            [s_sel.reshape(bsz, N_HEADS, Q_CHUNK, top * MOBA_BLOCK), s_own], axis=-1)
        p = jax.nn.softmax(scores, axis=-1).astype(v.dtype)
        p_sel = p[..., :top * MOBA_BLOCK].reshape(bsz, N_HEADS, Q_CHUNK, top, MOBA_BLOCK)
        p_own = p[..., top * MOBA_BLOCK:]
        return (jnp.einsum('bhqrt,bhqrtd->bhqd', p_sel, v_sel)
                + jnp.einsum('bhqt,bhtd->bhqd', p_own, v_own))

    out = lax.map(chunk, jnp.arange(l_pad // Q_CHUNK))
    out = out.transpose(1, 0, 3, 2, 4).reshape(bsz, l_pad, N_HEADS * HEAD_DIM)
    return out[:, :seq_len]


def conformer_conv(h, w_in, b_in, dw_w, dw_b, ln_g, ln_b, w_out, b_out):
    a = h @ w_in + b_in
    a = a[..., :D_CONV] * jax.nn.sigmoid(a[..., D_CONV:])
    a = causal_dwconv(a, dw_w, dw_b)
    a = jax.nn.silu(layer_norm(a, ln_g, ln_b))
    return a @ w_out + b_out


def conv_ffn(h, w_up, w_gate, dw_w, dw_b, w_down):
    g = causal_dwconv(h @ w_gate, dw_w, dw_b)
    return (jax.nn.silu(g) * (h @ w_up)) @ w_down


def setup_inputs(seed: int = 0) -> dict:
    key = jax.random.key(seed)
    ks = iter(jax.random.split(key, 40))

    def nrm(shape, scale):
        return scale * jax.random.normal(next(ks), shape, jnp.float32)

    G, H, P = SSM_GROUPS, SSM_GROUP, SSM_STATE
    n_idx = jnp.arange(P, dtype=jnp.float32)
    return {
        'x': nrm((BATCH, SEQ, D_MODEL), 1.0),
        'c': nrm((BATCH, D_MODEL), 1.0),
        'mod_w': nrm((DEPTH, D_MODEL, 6 * D_MODEL), 0.5 * D_MODEL ** -0.5),
        'mod_b': nrm((DEPTH, 6 * D_MODEL), 0.01),
        'norm_g': 1.0 + nrm((DEPTH, 2, D_MODEL), 0.01),
        'final_g': 1.0 + nrm((D_MODEL,), 0.01),
        'ab_w_in': nrm((N_EVEN, D_MODEL, D_SSM + 3 * D_ATTN), D_MODEL ** -0.5),
        'ssm_a_re': -0.5 + nrm((N_EVEN, G, P), 0.01),
        'ssm_a_im': math.pi * n_idx + nrm((N_EVEN, G, P), 0.01),
        'ssm_log_dt': jax.random.uniform(next(ks), (N_EVEN, G), jnp.float32,
                                         math.log(DT_MIN), math.log(DT_MAX)),
        'ssm_b_re': nrm((N_EVEN, G, P, H), (2 * H) ** -0.5),
        'ssm_b_im': nrm((N_EVEN, G, P, H), (2 * H) ** -0.5),
        'ssm_c_re': nrm((N_EVEN, G, H, P), P ** -0.5),
        'ssm_c_im': nrm((N_EVEN, G, H, P), P ** -0.5),
        'ssm_d': nrm((N_EVEN, G, H), 1.0),
        'ssm_glu_w': nrm((N_EVEN, D_SSM, D_SSM), D_SSM ** -0.5),
        'ssm_glu_b': nrm((N_EVEN, D_SSM), 0.01),
        'ab_w_out': nrm((N_EVEN, D_MODEL, D_MODEL), D_MODEL ** -0.5),
        'rel_bias': nrm((REL_BUCKETS, N_HEADS), 0.5),
        'cm_w_in': nrm((N_ODD, D_MODEL, 2 * D_CONV), D_MODEL ** -0.5),
        'cm_b_in': nrm((N_ODD, 2 * D_CONV), 0.01),
        'cm_dw_w': nrm((N_ODD, CONV_WIDTH, D_CONV), CONV_WIDTH ** -0.5),
        'cm_dw_b': nrm((N_ODD, D_CONV), 0.01),
        'cm_ln_g': 1.0 + nrm((N_ODD, D_CONV), 0.01),
        'cm_ln_b': nrm((N_ODD, D_CONV), 0.01),
        'cm_w_out': nrm((N_ODD, D_CONV, D_MODEL), D_CONV ** -0.5),
        'cm_b_out': nrm((N_ODD, D_MODEL), 0.01),
        'ffn_w_up': nrm((DEPTH, D_MODEL, FFN_HIDDEN), D_MODEL ** -0.5),
        'ffn_w_gate': nrm((DEPTH, D_MODEL, FFN_HIDDEN), D_MODEL ** -0.5),
        'ffn_dw_w': nrm((DEPTH, FFN_CONV_WIDTH, FFN_HIDDEN), FFN_CONV_WIDTH ** -0.5),
        'ffn_dw_b': nrm((DEPTH, FFN_HIDDEN), 0.01),
        'ffn_w_down': nrm((DEPTH, FFN_HIDDEN, D_MODEL), FFN_HIDDEN ** -0.5),
    }


def reference(x, c, mod_w, mod_b, norm_g, final_g, ab_w_in, ssm_a_re, ssm_a_im, ssm_log_dt,
              ssm_b_re, ssm_b_im, ssm_c_re, ssm_c_im, ssm_d, ssm_glu_w, ssm_glu_b, ab_w_out,
              rel_bias, cm_w_in, cm_b_in, cm_dw_w, cm_dw_b, cm_ln_g, cm_ln_b, cm_w_out, cm_b_out,
              ffn_w_up, ffn_w_gate, ffn_dw_w, ffn_dw_b, ffn_w_down):
    bsz, seq_len, _ = x.shape
    cs = jax.nn.silu(c)
    for layer in range(DEPTH):
        mod = cs @ mod_w[layer] + mod_b[layer]
        sh1, sc1, g1, sh2, sc2, g2 = jnp.split(mod, 6, axis=-1)
        h = modulate(rms_norm(x, norm_g[layer, 0]), sh1, sc1)
        i = layer // 2
        if layer % 2 == 0:
            proj = h @ ab_w_in[i]
            u = proj[..., :D_SSM]
            qkv = proj[..., D_SSM:].reshape(bsz, seq_len, 3, N_HEADS, HEAD_DIM)
            y_ssm = s5_mixer(u, ssm_a_re[i], ssm_a_im[i], ssm_log_dt[i], ssm_b_re[i],
                             ssm_b_im[i], ssm_c_re[i], ssm_c_im[i], ssm_d[i],
                             ssm_glu_w[i], ssm_glu_b[i])
            y_att = moba_attention(qkv[:, :, 0], qkv[:, :, 1], qkv[:, :, 2], rel_bias)
            y = jnp.concatenate([y_ssm, y_att], axis=-1) @ ab_w_out[i]
        else:
            y = conformer_conv(h, cm_w_in[i], cm_b_in[i], cm_dw_w[i], cm_dw_b[i],
                               cm_ln_g[i], cm_ln_b[i], cm_w_out[i], cm_b_out[i])
        x = x + g1[:, None, :] * y
        h = modulate(rms_norm(x, norm_g[layer, 1]), sh2, sc2)
        x = x + g2[:, None, :] * conv_ffn(h, ffn_w_up[layer], ffn_w_gate[layer],
                                          ffn_dw_w[layer], ffn_dw_b[layer], ffn_w_down[layer])
    return rms_norm(x, final_g)
```

```python
import math
import numpy as np
from contextlib import ExitStack
import concourse.bass as bass
import concourse.mybir as mybir
from concourse.bass_utils import run_bass_kernel_spmd

F32 = mybir.dt.float32
BF16 = mybir.dt.bfloat16
AF = mybir.ActivationFunctionType
ALU = mybir.AluOpType
AX = mybir.AxisListType

L = 4096
D = 1024
EPS = 1e-6
NEG = -30000.0


class Prog:
    ENGS = ("pe", "dve", "act", "pool", "sp")

    def __init__(self, nc):
        self.nc = nc
        self.ops = []
        self.relax = True
        self.root = ExitStack()
        self.stacks = [self.root]

    def sb(self, name, shape, dtype=F32):
        self.uid = getattr(self, "uid", 0) + 1
        return self.stacks[-1].enter_context(self.nc.sbuf_tensor("%s_s%d" % (name, self.uid), list(shape), dtype))

    def ps(self, name, shape, dtype=F32):
        self.uid = getattr(self, "uid", 0) + 1
        full = 512 if dtype == F32 else 1024
        t = self.stacks[-1].enter_context(self.nc.psum_tensor("%s_p%d" % (name, self.uid), [128, full], dtype))
        self.last_ps_raw = t
        n = 1
        for d in shape[1:]:
            n *= d
        v = t[0:shape[0], 0:n]
        if len(shape) == 3:
            v = v.rearrange("p (a b) -> p a b", a=shape[1])
        elif len(shape) == 4:
            v = v.rearrange("p (a b c) -> p a b c", a=shape[1], b=shape[2])
        return v

    def push(self):
        self.stacks.append(ExitStack())

    def pop(self):
        self.ops.append(dict(barrier=True))
        self.stacks.pop().close()

    def op(self, eng, fn, reads=(), writes=(), drain=False):
        self.ops.append(dict(eng=eng, fn=fn, reads=tuple(reads), writes=tuple(writes), dma=None, drain=drain))

    def dma(self, eng, out, in_, r=(), w=(), key=None):
        if key is None:
            key = w[0]
        r = [x for x in r if not x.startswith("d_")]
        w = [x for x in w if not x.startswith("d_")]
        self.ops.append(dict(eng=eng, fn=lambda e: e.dma_start(out=out, in_=in_),
                             reads=tuple(r), writes=tuple(w), dma=key))

    def mm(self, out, lhsT, rhs, start, stop, r, w, drain=False, **kw):
        self.op("pe", lambda e: e.matmul(out, lhsT, rhs, start=start, stop=stop, **kw), r, w, drain=drain)

    def tr(self, out, in_, ident, r, w):
        self.op("pe", lambda e: e.transpose(out, in_, ident), r, w)

    def act(self, out, in_, func, r, w, scale=1.0, bias=0.0):
        self.op("act", lambda e: e.activation(out=out, in_=in_, func=func, bias=bias, scale=scale), r, w)

    def tt(self, eng, out, in0, in1, op, r, w):
        self.op(eng, lambda e: e.tensor_tensor(out=out, in0=in0, in1=in1, op=op), r, w)

    def ts(self, eng, out, in0, s1, op0, r, w, s2=None, op1=None):
        if op1 is None:
            self.op(eng, lambda e: e.tensor_scalar(out=out, in0=in0, scalar1=s1, scalar2=None, op0=op0), r, w)
        else:
            self.op(eng, lambda e: e.tensor_scalar(out=out, in0=in0, scalar1=s1, scalar2=s2, op0=op0, op1=op1), r, w)

    def stt(self, out, in0, scalar, in1, op0, op1, r, w):
        self.op("dve", lambda e: e.scalar_tensor_tensor(out=out, in0=in0, scalar=scalar, in1=in1, op0=op0, op1=op1), r, w)

    def cp(self, eng, out, in_, r, w):
        if eng == "act":
            self.op("act", lambda e: e.copy(out=out, in_=in_), r, w)
        else:
            self.op(eng, lambda e: e.tensor_copy(out=out, in_=in_), r, w)

    def memset(self, eng, ap, val, w):
        self.op(eng, lambda e: e.memset(ap, val), (), w)

    def build(self):
        nc = self.nc
        ops = self.ops
        n = len(ops)
        last_writer, readers, last_dma_on_key, last_of_stream = {}, {}, {}, {}
        deps = [None] * n
        needed = [False] * n
        pending_bar, bar_seen = set(), set(self.ENGS)
        last_pe = None
        eseq = {}
        for i, o in enumerate(ops):
            if "barrier" in o:
                pending_bar = set(last_of_stream.values())
                bar_seen = set()
                deps[i] = set()
                continue
            o["stream"] = ("dma", o["dma"]) if o["dma"] is not None else ("eng", o["eng"])
            d = set()
            for b in o["reads"]:
                if b in last_writer:
                    d.add(last_writer[b])
            for b in o["writes"]:
                if b in last_writer:
                    d.add(last_writer[b])
                d.update(readers.get(b, ()))
            if o["dma"] is not None:
                k = o["dma"]
                if k in last_dma_on_key:
                    d.add(last_dma_on_key[k])
                last_dma_on_key[k] = i
                needed[i] = True
            if o["eng"] not in bar_seen:
                d |= pending_bar
                bar_seen.add(o["eng"])
            d.discard(i)
            if o["dma"] is None:
                eseq[o["eng"]] = eseq.get(o["eng"], 0) + 1
                o["seq"] = eseq[o["eng"]]
                if o["eng"] in ("dve", "act") and self.relax:
                    d = {j for j in d if not ("barrier" not in ops[j] and ops[j]["dma"] is None
                                              and ops[j]["eng"] == o["eng"] and o["seq"] - ops[j]["seq"] >= 3)}
            if o["eng"] == "pe" and o["dma"] is None:
                d = {j for j in d if not (ops[j]["eng"] == "pe" and ops[j]["dma"] is None)}
                if o.get("drain") and last_pe is not None:
                    d.add(last_pe)
                last_pe = i
            deps[i] = d
            for j in d:
                needed[j] = True
            for b in o["writes"]:
                last_writer[b] = i
                readers[b] = []
            for b in o["reads"]:
                if b not in o["writes"]:
                    readers.setdefault(b, []).append(i)
            last_of_stream[o["stream"]] = i
        for i in last_of_stream.values():
            needed[i] = True
        stream_count, sig = {}, [None] * n
        for i, o in enumerate(ops):
            if "barrier" in o:
                continue
            if needed[i]:
                s = o["stream"]
                inc = 16 if s[0] == "dma" else 1
                stream_count[s] = stream_count.get(s, 0) + inc
                sig[i] = (s, stream_count[s], inc)
        sems = {}
        for k, s in enumerate(stream_count):
            sems[s] = self.root.enter_context(nc.semaphore("sem%d" % k))
        self.n_sems = len(sems)
        per_eng = {e: [] for e in self.ENGS}
        waited = {e: {} for e in self.ENGS}
        for i, o in enumerate(ops):
            if "barrier" in o:
                continue
            e = o["eng"]
            wt = {}
            for j in deps[i]:
                s, c, _ = sig[j]
                if waited[e].get(s, 0) >= c:
                    continue
                wt[s] = max(wt.get(s, 0), c)
            for s, c in wt.items():
                waited[e][s] = c
            per_eng[e].append((i, wt))
        finals = dict(stream_count)
        self.n_ops = {e: len(v) for e, v in per_eng.items()}

        with nc.Block() as blk:
            def emit(engobj, ename):
                for i, wt in per_eng[ename]:
                    for s, c in wt.items():
                        engobj.wait_ge(sems[s], c)
                    ins = ops[i]["fn"](engobj)
                    if sig[i] is not None:
                        s, c, inc = sig[i]
                        ins.then_inc(sems[s], inc)
                if ename == "sp":
                    for s, c in finals.items():
                        engobj.wait_ge(sems[s], c)

            @blk.tensor
            def _(e):
                emit(e, "pe")

            @blk.vector
            def _(e):
                emit(e, "dve")

            @blk.scalar
            def _(e):
                emit(e, "act")

            @blk.gpsimd
            def _(e):
                emit(e, "pool")

            @blk.sync
            def _(e):
                emit(e, "sp")
        while len(self.stacks) > 1:
            self.stacks.pop().close()
        self.root.close()


PV = {}
_off = 0
for _n, _w in [("c", 8), ("modb", 96), ("normg", 32), ("finalg", 8), ("glub", 4), ("cmbin", 16),
               ("cmdww", 248), ("cmdwb", 8), ("cmlng", 8), ("cmlnb", 8), ("cmbout", 8),
               ("ffdww", 132), ("ffdwb", 44), ("ssmd", 4), ("b31", 8)]:
    PV[_n] = (_off, _w)
    _off += _w
NPV = _off


def build_program(stop_after=None, debug=False):
    nc = bass.Bass("TRN2", target_bir_lowering=False)
    P = Prog(nc)

    def din(name, shape, dt=F32):
        return nc.dram_tensor(name, list(shape), dt, kind="ExternalInput").ap()

    def dscr(name, shape, dt=F32):
        kind = "ExternalOutput" if debug else "Internal"
        return nc.dram_tensor(name, list(shape), dt, kind=kind).ap()

    xT = din("xT", [D, L])
    pvec_d = din("pvec", [128, NPV])
    ident_d = din("ident", [128, 128])
    mod_w = din("mod_w", [2, D, 6 * D])
    w_in0 = din("w_in0", [D, 2048])
    glu_w = din("glu_w", [512, 512])
    w_out0 = din("w_out0", [D, D])
    cm_w_in = din("cm_w_in", [D, 2048])
    cm_w_out = din("cm_w_out", [D, D])
    ffn_up = din("ffn_up", [2, D, 2816])
    ffn_gate = din("ffn_gate", [2, D, 2816])
    ffn_down = din("ffn_down", [2, 2816, D])
    s5p_d = din("s5p", [128, 48])
    s5bc_d = din("s5bc", [128, 4, 16, 16])
    biasg_d = din("biasg", [8, 128, 4, 256])
    maskc_d = din("maskc", [128, 2, 256])
    maskv_d = din("maskv", [128, 32, 16])
    notown_d = din("notown", [128, 32, 16])
    onehot_d = din("onehot", [16, L])
    outT = nc.dram_tensor("outT", [D, L], F32, kind="ExternalOutput").ap()

    uT_d = dscr("uT_d", [512, L], BF16)
    qT_d = dscr("qT_d", [512, L], BF16)
    kT_d = dscr("kT_d", [512, L], BF16)
    vaug_d = dscr("vaug_d", [L, 768], BF16)
    ycat_d = dscr("ycat_d", [D, L], BF16)
    xa_d = dscr("xa_d", [D, L])
    xb_d = dscr("xb_d", [D, L])

    def xview(ap):
        return ap.rearrange("(kc p) t -> p kc t", p=128)

    ones_bf = P.sb("ones_bf", [128, 128], BF16)
    P.memset("dve", ones_bf[:], 1.0, ["ones_bf"])
    identf = P.sb("identf", [128, 128])
    P.dma("sp", identf[:], ident_d, w=["identf"])
    identb = P.sb("identb", [128, 128], BF16)
    P.cp("dve", identb[:], identf[:], ["identf"], ["identb"])
    pv = P.sb("pv", [128, NPV])
    P.dma("sp", pv[:], pvec_d, w=["pv"])

    def pvs(name, a=0, b=None):
        o, w = PV[name]
        if b is None:
            b = w
        return pv[:, o + a:o + b]

    modT = P.sb("modT", [128, 2, 48])
    AB = P.sb("AB", [128, 2, 2, 8])
    rr = {"i": 0}

    cs2 = P.sb("cs2", [128, 8, 2])
    for dup in range(2):
        P.act(cs2[:, :, dup], pvs("c"), AF.Silu, ["pv"], ["cs2"])
    ng = pvs("normg").rearrange("p (l s k) -> p l s k", l=2, s=2)

    def mod_block(l, blk, mwbuf, mwname, mps, mpsname):
        mwv = mod_w[l].rearrange("(kc p) n -> p kc n", p=128)
        P.dma("sp", mwbuf[:], mwv[:, :, blk * 512:(blk + 1) * 512], w=[mwname])
        for jj in range(4):
            j = blk * 4 + jj
            for kc in range(8):
                P.mm(mps[:, j, :], mwbuf[:, kc, jj * 128:(jj + 1) * 128], cs2[:, kc, :],
                     kc == 0, kc == 7, [mwname, "cs2"], [mpsname])

    def mod_finish(l, mps, mpsname):
        P.tt("dve", modT[:, l, :], mps[:, :, 0], pvs("modb").rearrange("p (l j) -> p l j", l=2)[:, l, :], ALU.add,
             [mpsname, "pv"], ["modT"])
        for s_ in range(2):
            sc = modT[:, l, 8 + 24 * s_:16 + 24 * s_]
            P.stt(AB[:, l, s_, :], sc, 1.0, ng[:, l, s_, :], ALU.add, ALU.mult, ["modT", "pv"], ["AB"])

    P.push()
    mw = [P.sb("mw%d" % i, [128, 8, 512]) for i in range(6)]
    modps = P.ps("modps", [128, 48, 2])
    for blk in range(12):
        mod_block(0, blk, mw[blk % 6], "mw%d" % (blk % 6), modps, "modps")
    mod_finish(0, modps, "modps")
    P.pop()
    if stop_after == "M":
        dbgm = dscr("dbgm", [128, 96])
        P.dma("sp", dbgm, modT[:].rearrange("p l j -> p (l j)"), r=["modT"], w=[], key="st_dbgm")
        P.build()
        return nc

    def A_(l, s):
        return AB[:, l, s, :]

    def B_(l, s):
        return modT[:, l, 24 * s:24 * s + 8]

    def G_(l, s):
        return modT[:, l, 16 + 24 * s:24 + 24 * s]

    def rmsnorm(xt, xname, N, A, B, ht, hname, xsq, ssps, rstd, tmps, final=False):
        for kc in range(8):
            P.act(xsq[:, kc, :N], xt[:, kc, :N], AF.Square, [xname], ["xsq"])
        for kc in range(8):
            P.mm(ssps[:, :N], ones_bf[:], xsq[:, kc, :N], kc == 0, kc == 7, ["ones_bf", "xsq"], ["ssps"])
        P.act(rstd[:, :N], ssps[:, :N], AF.Ln, ["ssps", "epsb"], ["rstd"], scale=1.0 / D, bias=epsb[:, 0:1])
        P.act(rstd[:, :N], rstd[:, :N], AF.Exp, ["rstd"], ["rstd"], scale=-0.5)
        for kc in range(8):
            if final:
                P.stt(ht[:, kc, :N], xt[:, kc, :N], A[:, kc:kc + 1], rstd[:, :N], ALU.mult, ALU.mult,
                      [xname, "rstd", "AB", "pv"], [hname])
            else:
                ti = rr["i"] % len(tmps)
                rr["i"] += 1
                tn = "tmpn%d" % ti
                P.stt(tmps[ti][:, :N], xt[:, kc, :N], A[:, kc:kc + 1], rstd[:, :N], ALU.mult, ALU.mult,
                      [xname, "rstd", "AB"], [tn])
                P.act(ht[:, kc, :N], tmps[ti][:, :N], AF.Identity, [tn, "modT"], [hname], bias=B[:, kc:kc + 1])

    def load_w_bf(dst, name, src, ncols):
        v = src.rearrange("(kc p) n -> p kc n", p=128)
        c0 = 0
        while c0 < ncols:
            c1 = min(ncols, c0 + 2048)
            P.dma("pool", dst[:, :, c0:c1], v[:, :, c0:c1], w=[name], key="wld_" + name)
            c0 = c1

    def store_x(dst, tile_, name, sl, b):
        for oc in range(8):
            P.dma("sp", dst[oc * 128:(oc + 1) * 128, sl], tile_[:, oc, :], r=[name], w=[], key="st_x%d_%d" % (b, oc))

    epsb = P.sb("epsb", [128, 1])
    P.memset("dve", epsb[:], EPS, ["epsb"])

    P.push()
    win = P.sb("win", [128, 8, 2048], BF16)
    load_w_bf(win, "win", w_in0, 2048)
    xts = [P.sb("xt%d" % i, [128, 8, 512]) for i in range(2)]
    hts = [P.sb("ht%d" % i, [128, 8, 512], BF16) for i in range(2)]
    xsq = P.sb("xsq", [128, 8, 512], BF16)
    rstd = P.sb("rstd", [128, 512])
    tmps = [P.sb("tmpn%d" % i, [128, 512]) for i in range(3)]
    obs = [P.sb("ob%d" % i, [128, 512], BF16) for i in range(4)]
    vst = [P.sb("vst%d" % i, [128, 4, 768], BF16) for i in range(2)]
    ssps = P.ps("ssps", [128, 512])
    pjs = [P.ps("pj%d" % i, [128, 512]) for i in range(3)]
    pvp = [P.ps("pvp%d" % i, [128, 512]) for i in range(2)]
    for i in range(2):
        P.memset("pool", vst[i][:].rearrange("p s (r c) -> p s r c", c=192)[:, :, :, 64:128], 1.0, ["vst%d" % i])
    mwA = [P.sb("mwA%d" % i, [128, 8, 512]) for i in range(2)]
    modps1 = P.ps("modps1", [128, 48, 2])
    mblk = {"i": 0}
    oi = 0
    import os
    SK = os.environ.get("KSKIP", "")
    for tt in range(8 if "1" not in SK else 1):
        xt, xn = xts[tt % 2], "xt%d" % (tt % 2)
        ht, hn = hts[tt % 2], "ht%d" % (tt % 2)
        sl = slice(tt * 512, (tt + 1) * 512)
        P.dma("sp", xt[:], xview(xT)[:, :, sl], w=[xn])
        rmsnorm(xt, xn, 512, A_(0, 0), B_(0, 0), ht, hn, xsq, ssps, rstd, tmps)
        for oc in range(12 if "p" not in SK else 0):
            pj, pn = pjs[oc % 3], "pj%d" % (oc % 3)
            for kc in range(8):
                P.mm(pj[:], win[:, kc, oc * 128:(oc + 1) * 128], ht[:, kc, :], kc == 0, kc == 7, ["win", hn], [pn])
            ob, on = obs[oi % 4], "ob%d" % (oi % 4)
            oi += 1
            P.act(ob[:], pj[:], AF.Identity, [pn], [on], scale=(0.125 if 4 <= oc < 8 else 1.0))
            dst = (uT_d, qT_d, kT_d)[oc // 4]
            P.dma("sp", dst[(oc % 4) * 128:(oc % 4 + 1) * 128, sl], ob[:], r=[on], w=["d_%d_%d" % (oc // 4, tt)], key="st_" + on)
        vs, vn = vst[tt % 2], "vst%d" % (tt % 2)
        for sub in range(4 if "v" not in SK else 0):
            pp, ppn = pvp[sub % 2], "pvp%d" % (sub % 2)
            for kc in range(8):
                P.mm(pp[:], ht[:, kc, sub * 128:(sub + 1) * 128], win[:, kc, 1536:2048], kc == 0, kc == 7, ["win", hn], [ppn])
            v4 = vs[:, sub, :].rearrange("p (r c) -> p r c", c=192)
            p4 = pp[:].rearrange("p (r c) -> p r c", c=128)
            ve = "act" if sub % 2 else "dve"
            P.cp(ve, v4[:, :, 0:64], p4[:, :, 0:64], [ppn], [vn])
            P.cp(ve, v4[:, :, 128:192], p4[:, :, 64:128], [ppn], [vn])
        for sub in range(4 if "s" not in SK else 0):
            P.dma("sp", vaug_d[tt * 512 + sub * 128:tt * 512 + (sub + 1) * 128, :], vs[:, sub, :], r=[vn], w=[], key="st_" + vn)
        for _ in range(2 if tt < 4 else 1):
            bi = mblk["i"]
            mblk["i"] += 1
            mod_block(1, bi, mwA[bi % 2], "mwA%d" % (bi % 2), modps1, "modps1")
    mod_finish(1, modps1, "modps1")
    P.pop()
    if stop_after == "A":
        P.build()
        return nc

    P.push()
    U = P.sb("U", [128, 4, L], BF16)
    for c in range(4):
        P.dma("sp", U[:, c, :], uT_d[c * 128:(c + 1) * 128, :], r=["d_0_%d" % t for t in range(8)], w=["U"], key="ldU")
    W1 = P.sb("W1", [128, 16, 8, 2, 128], BF16)
    CPb = P.sb("CPb", [128, 8, 2, 16, 32], BF16)
    BD = P.sb("BD", [128, 4, 8, 128], BF16)
    DBL = P.sb("DBL", [128, 9, 3, 16])
    gluw = P.sb("gluw", [128, 4, 512], BF16)
    load_w_bf(gluw, "gluw", glu_w, 512)

    P.push()
    s5p = P.sb("s5p", [128, 48])
    P.dma("sp", s5p[:], s5p_d, w=["s5p"])
    s5bc = P.sb("s5bc", [128, 4, 16, 16])
    P.dma("sp", s5bc[:], s5bc_d, w=["s5bc"])
    lamr, lami, logdt = s5p[:, 0:16], s5p[:, 16:32], s5p[:, 32:48]
    sc_ = P.sb("s5t", [128, 24, 16])
    S = "s5t"

    def T(i):
        return sc_[:, i, :]

    def vv(out, a, b, op):
        P.tt("dve", out, a, b, op, [S, "s5p", "PW"], [S])

    halfpi = P.sb("halfpi", [128, 1])
    P.memset("dve", halfpi[:], math.pi / 2, ["halfpi"])
    P.act(T(0), logdt, AF.Exp, ["s5p"], [S])
    vv(T(1), lami, T(0), ALU.mult)
    P.act(T(2), T(1), AF.Sin, [S], [S], scale=0.125)
    P.act(T(3), T(1), AF.Sin, [S, "halfpi"], [S], scale=-0.125, bias=halfpi[:, 0:1])
    for _ in range(3):
        vv(T(4), T(3), T(3), ALU.mult)
        vv(T(5), T(2), T(2), ALU.mult)
        P.stt(T(6), T(2), 2.0, T(3), ALU.mult, ALU.mult, [S], [S])
        vv(T(3), T(4), T(5), ALU.subtract)
        P.cp("dve", T(2), T(6), [S], [S])
    vv(T(4), lamr, T(0), ALU.mult)
    P.act(T(4), T(4), AF.Exp, [S], [S])
    vv(T(7), T(4), T(3), ALU.mult)
    vv(T(8), T(4), T(2), ALU.mult)
    vv(T(9), lamr, lamr, ALU.mult)
    vv(T(10), lami, lami, ALU.mult)
    vv(T(9), T(9), T(10), ALU.add)
    P.op("dve", lambda e: e.reciprocal(out=T(9), in_=T(9)), [S], [S])
    P.ts("dve", T(10), T(7), -1.0, ALU.add, [S], [S])
    vv(T(11), T(10), lamr, ALU.mult)
    vv(T(12), T(8), lami, ALU.mult)
    vv(T(11), T(11), T(12), ALU.add)
    vv(T(11), T(11), T(9), ALU.mult)
    vv(T(12), T(8), lamr, ALU.mult)
    vv(T(13), T(10), lami, ALU.mult)
    vv(T(12), T(12), T(13), ALU.subtract)
    vv(T(12), T(12), T(9), ALU.mult)
    PW = P.sb("PW", [128, 9, 2, 16])
    P.memset("dve", PW[:, 0, 0, :], 1.0, ["PW"])
    P.memset("dve", PW[:, 0, 1, :], 0.0, ["PW"])

    def cmul(o_r, o_i, a_r, a_i, b_r, b_i, names):
        P.tt("dve", T(14), a_r, b_r, ALU.mult, names, [S])
        P.tt("dve", T(15), a_i, b_i, ALU.mult, names, [S])
        P.tt("dve", T(16), a_r, b_i, ALU.mult, names, [S])
        P.tt("dve", T(17), a_i, b_r, ALU.mult, names, [S])
        P.tt("dve", o_r, T(14), T(15), ALU.subtract, [S], names)
        P.tt("dve", o_i, T(16), T(17), ALU.add, [S], names)

    for k in range(1, 9):
        cmul(PW[:, k, 0, :], PW[:, k, 1, :], PW[:, k - 1, 0, :], PW[:, k - 1, 1, :], T(7), T(8), ["PW", S])
    P.cp("dve", DBL[:, 0, 0, :], PW[:, 8, 0, :], ["PW"], ["DBL"])
    P.cp("dve", DBL[:, 0, 1, :], PW[:, 8, 1, :], ["PW"], ["DBL"])
    for j in range(1, 9):
        cmul(DBL[:, j, 0, :], DBL[:, j, 1, :], DBL[:, j - 1, 0, :], DBL[:, j - 1, 1, :],
             DBL[:, j - 1, 0, :], DBL[:, j - 1, 1, :], ["DBL", S])
    for j in range(9):
        P.ts("dve", DBL[:, j, 2, :], DBL[:, j, 1, :], -1.0, ALU.mult, ["DBL"], ["DBL"])

    if stop_after == "B00":
        d1 = dscr("dbg_pw", [128, 9 * 2 * 16]); P.dma("sp", d1, PW[:].rearrange("p a b c -> p (a b c)"), r=["PW"], w=[], key="st_d1")
        d2 = dscr("dbg_dbl", [128, 9 * 3 * 16]); P.dma("sp", d2, DBL[:].rearrange("p a b c -> p (a b c)"), r=["DBL"], w=[], key="st_d2")
        P.build()
        return nc
    wt = P.sb("s5w", [128, 8, 16, 16])
    WN = "s5w"

    def Wt(i):
        return wt[:, i, :, :]

    def bc(ap16):
        return ap16.unsqueeze(2).to_broadcast([128, 16, 16])

    def cmulw(o_r, o_i, s_r, s_i, b_r, b_i, rn, wn, neg_i=False):
        P.tt("dve", Wt(4), bc(s_r), b_r, ALU.mult, rn, [WN])
        P.tt("dve", Wt(5), bc(s_i), b_i, ALU.mult, rn, [WN])
        P.tt("dve", Wt(6), bc(s_r), b_i, ALU.mult, rn, [WN])
        P.tt("dve", Wt(7), bc(s_i), b_r, ALU.mult, rn, [WN])
        P.tt("dve", o_r, Wt(4), Wt(5), ALU.subtract, [WN], wn)
        P.tt("dve", o_i, Wt(6), Wt(7), ALU.add, [WN], wn)

    br, bi_, cr, ci = s5bc[:, 0], s5bc[:, 1], s5bc[:, 2], s5bc[:, 3]
    cmulw(Wt(0), Wt(1), T(11), T(12), br, bi_, [S, "s5bc", WN], [WN])

    BB128 = P.sb("BB128", [128, 2, 16, 128], BF16)
    P.memset("pool", BB128[:], 0.0, ["BB128"])
    PS_BB = 2 * 16 * 128

    def diag_ap(tensor, pstride, base_off, half):
        off = half * 64 * pstride + base_off + 16 * half
        return bass.AP(tensor, off, [[pstride, 64], [512, 4], [160, 4], [1, 16]])

    for ri in range(2):
        for half in range(2):
            hs = slice(64 * half, 64 * half + 64)
            for q in range(16):
                co = 32 * (q % 4) + 16 * half
                P.cp("dve", BB128[hs, ri, q, co:co + 16], Wt(ri)[hs, q, :], [WN], ["BB128"])

    CPf = P.sb("CPf", [128, 8, 2, 16, 32], BF16)
    P.memset("pool", CPf[:], 0.0, ["CPf"])
    P.memset("pool", CPb[:], 0.0, ["CPb"])
    for k in range(9):
        cmulw(Wt(2), Wt(3), PW[:, k, 0, :], PW[:, k, 1, :], cr, ci, ["PW", "s5bc", WN], [WN])
        P.ts("dve", Wt(3), Wt(3), -1.0, ALU.mult, [WN], [WN])
        for ri in range(2):
            for half in range(2):
                hs = slice(64 * half, 64 * half + 64)
                if k < 8:
                    P.cp("dve", CPf[hs, k, ri, :, 16 * half:16 * half + 16], Wt(2 + ri)[hs], [WN], ["CPf"])
                if k >= 1:
                    P.cp("act", CPb[hs, k - 1, ri, :, 16 * half:16 * half + 16], Wt(2 + ri)[hs], [WN], ["CPb"])

    bdps2 = [P.ps("bdps%d" % i, [128, 128]) for i in range(2)]
    P.memset("pool", BD[:], 0.0, ["BD"])
    for c in range(4 if "a" not in SK else 0):
        for tau in range(8):
            bp = bdps2[tau % 2]
            bpn = "bdps%d" % (tau % 2)
            for q4 in range(4):
                q = 4 * c + q4
                for ri in range(2):
                    P.mm(bp[:], BB128[:, ri, q, :], CPf[:, tau, ri, 4 * c:4 * c + 4, :], q4 == 0 and ri == 0, q4 == 3 and ri == 1,
                         ["BB128", "CPf"], [bpn])
            for q4 in range(4):
                P.cp("act", BD[32 * q4:32 * q4 + 32, c, tau, 32 * q4:32 * q4 + 32],
                     bp[32 * q4:32 * q4 + 32, 32 * q4:32 * q4 + 32], [bpn], ["BD"])

    srcs = [P.sb("w1src%d" % i, [128, 16, 128], BF16) for i in range(2)]
    for i in range(2):
        P.memset("pool", srcs[i][:], 0.0, ["w1src%d" % i])
    trps = [P.ps("trps%d" % i, [128, 4, 128], BF16) for i in range(2)]
    ti = 0
    for k in range(8 if "b" not in SK else 0):
        cmulw(Wt(2), Wt(3), PW[:, k, 0, :], PW[:, k, 1, :], Wt(0), Wt(1), ["PW", WN], [WN])
        for ri in range(2):
            sname = "w1src%d" % ri
            for half in range(2):
                hs = slice(64 * half, 64 * half + 64)
                for q in range(16):
                    co = 32 * (q % 4) + 16 * half
                    P.cp("dve" if q % 2 else "pool", srcs[ri][hs, q, co:co + 16], Wt(2 + ri)[hs, q, :], [WN], [sname])
            for c in range(4):
                tp, tpn = trps[ti % 2], "trps%d" % (ti % 2)
                ti += 1
                for q4 in range(4):
                    P.tr(tp[:, q4, :], srcs[ri][:, 4 * c + q4, :], identb[:], [sname, "identb"], [tpn])
                P.cp("act" if c % 2 else "dve", W1[:, 4 * c:4 * c + 4, k, ri, :], tp[:], [tpn], ["W1"])
    if stop_after == "B0":
        d1 = dscr("dbg_pw", [128, 9 * 2 * 16]); P.dma("sp", d1, PW[:].rearrange("p a b c -> p (a b c)"), r=["PW"], w=[], key="st_d1")
        d2 = dscr("dbg_dbl", [128, 9 * 3 * 16]); P.dma("sp", d2, DBL[:].rearrange("p a b c -> p (a b c)"), r=["DBL"], w=[], key="st_d2")
        d3 = dscr("dbg_bd", [128, 4 * 8 * 128], BF16); P.dma("sp", d3, BD[:].rearrange("p a b c -> p (a b c)"), r=["BD"], w=[], key="st_d3")
        d4 = dscr("dbg_w1", [128, 16, 8 * 2 * 128], BF16)
        for qq in range(16):
            P.dma("sp", d4[:, qq, :], W1[:, qq].rearrange("p b c d -> p (b c d)"), r=["W1"], w=[], key="st_d4")
        d5 = dscr("dbg_cpb", [128, 8 * 2 * 16 * 32], BF16); P.dma("sp", d5, CPb[:].rearrange("p a b c d -> p (a b c d)"), r=["CPb"], w=[], key="st_d5")
        P.build()
        return nc
    P.pop()

    ygb = P.sb("ygb", [128, 4, L], BF16)
    Xa = [[P.sb("Xa%d%d" % (b, ri), [128, 512]) for ri in range(2)] for b in range(2)]
    Xpr = [[P.sb("Xpr%d%d" % (q4, ri), [128, 512], BF16) for ri in range(2)] for q4 in range(4)]
    slp = [P.ps("slp%d" % ri, [128, 512]) for ri in range(2)]
    yps = [P.ps("yps%d" % i, [128, 512]) for i in range(2)]
    ypre = [P.sb("ypre%d" % i, [128, 512]) for i in range(2)]
    gtmp = [P.sb("gtmp%d" % i, [128, 512]) for i in range(2)]
    for q4 in range(4):
        for ri in range(2):
            P.memset("pool", Xpr[q4][ri][:, 0:1], 0.0, ["Xpr%d%d" % (q4, ri)])
    for c in range(4):
        Ust = U[:, c, :].rearrange("p (j s) -> p s j", s=8)
        for q4 in range(4):
            q = 4 * c + q4
            for ri in range(2):
                for s8 in range(8):
                    P.mm(slp[ri][:], W1[:, q, 7 - s8, ri, :], Ust[:, s8, :], s8 == 0, s8 == 7, ["W1", "U"], ["slp%d" % ri],
                         drain=(q4 == 0 and ri == 0 and s8 == 0))
                P.cp("act", Xa[0][ri][:], slp[ri][:], ["slp%d" % ri], ["Xa0%d" % ri])
            cur = 0
            for j in range(9):
                d = 1 << j
                nx = 1 - cur
                sr, si = Xa[cur][0], Xa[cur][1]
                dr, di = Xa[nx][0], Xa[nx][1]
                nsr, nsi, ndr, ndi = "Xa%d0" % cur, "Xa%d1" % cur, "Xa%d0" % nx, "Xa%d1" % nx
                ar_, ai_, nai_ = DBL[:, j, 0, q:q + 1], DBL[:, j, 1, q:q + 1], DBL[:, j, 2, q:q + 1]
                P.cp("pool", dr[:, 0:d], sr[:, 0:d], [nsr], [ndr])
                P.cp("pool", di[:, 0:d], si[:, 0:d], [nsi], [ndi])
                P.stt(dr[:, d:], sr[:, 0:512 - d], ar_, sr[:, d:], ALU.mult, ALU.add, [nsr, "DBL"], [ndr])
                P.stt(di[:, d:], si[:, 0:512 - d], ar_, si[:, d:], ALU.mult, ALU.add, [nsi, "DBL"], [ndi])
                P.stt(dr[:, d:], si[:, 0:512 - d], nai_, dr[:, d:], ALU.mult, ALU.add, [nsi, ndr, "DBL"], [ndr])
                P.stt(di[:, d:], sr[:, 0:512 - d], ai_, di[:, d:], ALU.mult, ALU.add, [nsr, ndi, "DBL"], [ndi])
                cur = nx
            for ri in range(2):
                P.cp("act", Xpr[q4][ri][:, 1:512], Xa[cur][ri][:, 0:511], ["Xa%d%d" % (cur, ri)], ["Xpr%d%d" % (q4, ri)])
        for t8 in range(8):
            yp, ypn = yps[t8 % 2], "yps%d" % (t8 % 2)
            for s8 in range(t8 + 1):
                P.mm(yp[:], BD[:, c, t8 - s8, :], Ust[:, s8, :], s8 == 0, False, ["BD", "U"], [ypn], drain=(s8 == 0))
            for q4 in range(4):
                q = 4 * c + q4
                for ri in range(2):
                    last = (q4 == 3 and ri == 1)
                    P.mm(yp[32 * q4:32 * q4 + 32, :], CPb[:, t8, ri, q, :], Xpr[q4][ri][:], False, last,
                         ["CPb", "Xpr%d%d" % (q4, ri)], [ypn], drain=(q4 == 0 and ri == 0), tile_position=(0, 32 * q4))
            ye, yen = ypre[t8 % 2], "ypre%d" % (t8 % 2)
            P.stt(ye[:], Ust[:, t8, :], pvs("ssmd", c, c + 1), yp[:], ALU.mult, ALU.add, ["U", "pv", ypn], [yen])
            gt, gtn = gtmp[t8 % 2], "gtmp%d" % (t8 % 2)
            P.act(gt[:], ye[:], AF.Square, [yen], [gtn])
            P.ts("dve", gt[:], gt[:], 0.044715, ALU.mult, [gtn], [gtn], s2=1.0, op1=ALU.add)
            P.tt("dve", gt[:], gt[:], ye[:], ALU.mult, [gtn, yen], [gtn])
            P.act(gt[:], gt[:], AF.Sigmoid, [gtn], [gtn], scale=1.5957691216057308)
            P.tt("dve", ygb[:, c, :].rearrange("p (j s) -> p s j", s=8)[:, t8, :], ye[:], gt[:], ALU.mult, [yen, gtn], ["ygb%d" % c])
    zps = [P.ps("zps%d" % i, [128, 512]) for i in range(2)]
    sgs = [P.sb("sg%d" % i, [128, 512]) for i in range(2)]
    obs = [P.sb("obB%d" % i, [128, 512], BF16) for i in range(2)]
    it = 0
    for tt in range(8):
        sl = slice(tt * 512, (tt + 1) * 512)
        for oc in range(4):
            zp, zn = zps[it % 2], "zps%d" % (it % 2)
            sg, sn = sgs[it % 2], "sg%d" % (it % 2)
            ob, on = obs[it % 2], "obB%d" % (it % 2)
            it += 1
            for kc in range(4):
                P.mm(zp[:], gluw[:, kc, oc * 128:(oc + 1) * 128], ygb[:, kc, sl], kc == 0, kc == 3,
                     ["gluw"] + ["ygb%d" % kk for kk in range(4)], [zn], drain=(tt == 0 and oc == 0 and kc == 0))
            P.act(sg[:], zp[:], AF.Sigmoid, [zn, "pv"], [sn], bias=pvs("glub", oc, oc + 1))
            P.tt("dve", ob[:], ygb[:, oc, sl], sg[:], ALU.mult, ["ygb%d" % oc, sn], [on])
            P.dma("sp", ycat_d[oc * 128:(oc + 1) * 128, sl], ob[:], r=[on], w=["d_y_%d_%d" % (oc, tt)], key="st_" + on)
    P.pop()
    if stop_after == "B":
        P.build()
        return nc

    P.push()
    qa2 = [[P.sb("qa%d%d" % (p_, i), [80, L], BF16) for i in range(2)] for p_ in range(2)]
    ka2 = [[P.sb("ka%d%d" % (p_, i), [80, L], BF16) for i in range(2)] for p_ in range(2)]
    vg2 = [P.sb("vg%d" % p_, [128, 32, 192], BF16) for p_ in range(2)]
    bt2 = [[P.sb("bt%d%d" % (p_, i), [128, 4, 256]) for i in range(2)] for p_ in range(2)]
    maskc = P.sb("maskc", [128, 2, 256])
    P.dma("sp", maskc[:], maskc_d, w=["maskc"])
    maskv = P.sb("maskv", [128, 32, 16])
    P.dma("sp", maskv[:], maskv_d, w=["maskv"])
    notown = P.sb("notown", [128, 32, 16])
    P.dma("sp", notown[:], notown_d, w=["notown"])
    kmb = [P.sb("kmb%d" % i, [80, 16], BF16) for i in range(2)]
    kmf = P.sb("kmf", [64, 16])
    Mext = P.sb("Mext", [128, 32, 80], BF16)
    gm = P.sb("gm", [128, 32, 16])
    msel = P.sb("msel", [128, 32, 16])
    t8a = P.sb("t8a", [128, 32, 8])
    P.memset("pool", Mext[:], 0.0, ["Mext"])
    for i in range(2):
        P.memset("pool", kmb[i][:], 0.0, ["kmb%d" % i])
    NS = 3
    gbank = P.ps("gbank", [128, 32, 16])
    sps = [P.ps("sps%d" % i, [128, 2, 256]) for i in range(NS)]
    OA = [P.ps("OA%d" % i, [128, 256]) for i in range(2)]
    OB = [P.ps("OB%d" % i, [128, 256]) for i in range(2)]
    tpsv = P.last_ps_raw[0:80, :].bitcast(BF16).rearrange("p (a b) -> p a b", a=8)
    sbb = [P.sb("sbb%d" % i, [128, 2, 256]) for i in range(3)]
    pTs = [P.sb("pT%d" % i, [128, 2, 256], BF16) for i in range(4)]
    Dt = P.sb("Dt", [128, 256])
    oba = [P.sb("oba%d" % i, [128, 256], BF16) for i in range(2)]
    cnt = {"p": 0, "o": 0, "b": 0}

    def att_loads(hp):
        pr = hp % 2
        P.dma("sp", vg2[pr][:], vaug_d[:, hp * 192:(hp + 1) * 192].rearrange("(kc p) c -> p kc c", p=128), w=["vg%d" % pr])
        for i in range(2):
            h = 2 * hp + i
            P.dma("sp", qa2[pr][i][0:64, :], qT_d[64 * h:64 * h + 64, :], w=["qa%d%d_q" % (pr, i)], key="ld_qa%d%d" % (pr, i))
            P.dma("sp", ka2[pr][i][0:64, :], kT_d[64 * h:64 * h + 64, :], w=["ka%d%d" % (pr, i)], key="ld_ka%d%d" % (pr, i))
            for hh in range(2):
                P.dma("pool", ka2[pr][i][64:80, hh * 2048:(hh + 1) * 2048], onehot_d[:, hh * 2048:(hh + 1) * 2048],
                      w=["ka%d%d" % (pr, i)], key="ld_ka%d%d" % (pr, i))
            P.dma("sp", bt2[pr][i][:], biasg_d[h], w=["bt%d%d" % (pr, i)])

    att_loads(0)
    for hp in range(4):
        pr = hp % 2
        qa, ka, vg, bt = qa2[pr], ka2[pr], vg2[pr], bt2[pr]
        nq = lambda i, pr=pr: "qa%d%d_q" % (pr, i)
        nm = lambda i, pr=pr: "qa%d%d_m" % (pr, i)
        nka = lambda i, pr=pr: "ka%d%d" % (pr, i)
        nbt = lambda i, pr=pr: "bt%d%d" % (pr, i)
        nvg = "vg%d" % pr
        if hp + 1 < 4:
            att_loads(hp + 1)
        for i in range(2):
            h = 2 * hp + i
            P.memset("pool", qa[i][64:80, :], 0.0, [nm(i)])
            P.ts("dve", bt[i][:], bt[i][:], pvs("b31", h, h + 1), ALU.subtract, [nbt(i), "pv"], [nbt(i)])
            P.tt("dve", bt[i][:, 0:2, :], bt[i][:, 0:2, :], maskc[:], ALU.add, [nbt(i), "maskc"], [nbt(i)])
            P.op("dve", lambda e, i=i, ka=ka: e.tensor_reduce(out=kmf[:], in_=ka[i][0:64, :].rearrange("p (j t) -> p j t", t=256),
                                                            axis=AX.X, op=ALU.add), [nka(i)], ["kmf"])
            P.ts("dve", kmb[i][0:64, :], kmf[:], 1.0 / 256, ALU.mult, ["kmf"], ["kmb%d" % i])
            for qt in range(32):
                P.mm(gbank[:, qt, :], qa[i][0:80, qt * 128:(qt + 1) * 128], kmb[i][:], True, True,
                     [nq(i), nm(i), "kmb%d" % i], ["gbank"])
            P.tt("dve", gm[:], gbank[:], maskv[:], ALU.add, ["gbank", "maskv"], ["gm"])
            for qt in range(32):
                P.op("dve", lambda e, qt=qt: e.max(out=t8a[:, qt, :], in_=gm[:, qt, :]), ["gm"], ["t8a"])
            P.tt("dve", msel[:], gm[:], t8a[:, :, 2:3].to_broadcast([128, 32, 16]), ALU.is_ge, ["gm", "t8a"], ["msel"])
            P.stt(Mext[:, :, 64:80], msel[:], -1.0, notown[:], ALU.add, ALU.mult, ["msel", "notown"], ["Mext"])
            for g4 in range(4):
                for t in range(8):
                    P.tr(tpsv[:, t, :], Mext[:, 8 * g4 + t, :], identb[:], ["Mext", "identb"], ["OB1"])
                P.cp("act", qa[i][64:80, g4 * 1024:(g4 + 1) * 1024].rearrange("p (a b) -> p a b", a=8), tpsv[64:80, :, :],
                     ["OB1"], [nm(i)])
        tasks = [(n, i, j) for n in range(16) for i in range(2) for j in range(n + 1)]
        LOOK = NS - 1

        def emit_qk(t):
            n, i, j = tasks[t]
            s = t % NS
            for cc in range(2):
                kc = 2 * j + cc
                P.mm(sps[s][:, cc, :], ka[i][0:80, kc * 128:(kc + 1) * 128], qa[i][0:80, n * 256:(n + 1) * 256], True, True,
                     [nka(i), nq(i), nm(i)], ["sps%d" % s])

        def emit_rest(t):
            n, i, j = tasks[t]
            s = t % NS
            sp_, spn = sps[s][:], "sps%d" % s
            pi_ = cnt["p"] % 4
            cnt["p"] += 1
            pT, pTn = pTs[pi_], "pT%d" % pi_
            if j >= n - 1:
                base = 0 if j == n else 2
                bi_ = cnt["b"] % 3
                cnt["b"] += 1
                sb_, sbn = sbb[bi_], "sbb%d" % bi_
                P.tt("dve", sb_[:], sp_, bt[i][:, base:base + 2, :], ALU.add, [spn, nbt(i)], [sbn])
                P.act(pT[:], sb_[:], AF.Exp, [sbn], [pTn])
            else:
                P.act(pT[:], sp_, AF.Exp, [spn], [pTn])
            for cc in range(2):
                kc = 2 * j + cc
                first = (j == 0 and cc == 0)
                last = (j == n and cc == 1)
                P.mm(OA[i][:], vg[:, kc, 0:128], pT[:, cc, :], first, last, [nvg, pTn], ["OA%d" % i])
                P.mm(OB[i][:], vg[:, kc, 64:192], pT[:, cc, :], first, last, [nvg, pTn], ["OB%d" % i])
            if i == 1 and j == n:
                qs = slice(n * 256, (n + 1) * 256)
                P.cp("act", Dt[0:64, :], OB[0][0:64, :], ["OB0"], ["Dt"])
                P.cp("act", Dt[64:128, :], OA[1][64:128, :], ["OA1"], ["Dt"])
                P.op("dve", lambda e: e.reciprocal(out=Dt[:], in_=Dt[:]), ["Dt"], ["Dt"])
                o = cnt["o"] % 2
                cnt["o"] += 1
                P.tt("dve", oba[o][0:64, :], OA[0][0:64, :], Dt[0:64, :], ALU.mult, ["OA0", "Dt"], ["oba%d" % o])
                P.tt("dve", oba[o][64:128, :], OB[1][64:128, :], Dt[64:128, :], ALU.mult, ["OB1", "Dt"], ["oba%d" % o])
                P.dma("sp", ycat_d[512 + hp * 128:512 + (hp + 1) * 128, qs], oba[o][:], r=["oba%d" % o],
                      w=[], key="st_oba%d" % o)

        for t in range(len(tasks) + LOOK):
            if t < len(tasks):
                emit_qk(t)
            if t - LOOK >= 0:
                emit_rest(t - LOOK)
    P.pop()
    if stop_after == "C":
        P.build()
        return nc

    def outproj_phase(issue_ffn_loads):
        P.push()
        N = 256
        wout = P.sb("wout", [128, 8, D], BF16)
        load_w_bf(wout, "wout", w_out0, D)
        issue_ffn_loads()
        xts = [P.sb("xtD%d" % i, [128, 8, N]) for i in range(2)]
        ycs = [P.sb("yc%d" % i, [128, 8, N], BF16) for i in range(2)]
        pjs = [P.ps("pjD%d" % i, [128, N]) for i in range(3)]
        for tt in range(L // N):
            sl = slice(tt * N, (tt + 1) * N)
            b = tt % 2
            P.dma("sp", xts[b][:], xview(xT)[:, :, sl], w=["xtD%d" % b])
            P.dma("sp", ycs[b][:], xview(ycat_d)[:, :, sl], w=["yc%d" % b])
            for oc in range(8):
                pj, pn = pjs[oc % 3], "pjD%d" % (oc % 3)
                for kc in range(8):
                    P.mm(pj[:], wout[:, kc, oc * 128:(oc + 1) * 128], ycs[b][:, kc, :], kc == 0, kc == 7, ["wout", "yc%d" % b], [pn])
                P.stt(xts[b][:, oc, :], pj[:], G_(0, 0)[:, oc:oc + 1], xts[b][:, oc, :], ALU.mult, ALU.add,
                      [pn, "modT", "xtD%d" % b], ["xtD%d" % b])
            store_x(xa_d, xts[b], "xtD%d" % b, sl, b)
        P.pop()

    def ffn_phase(l, xin_d, xout_d, final=False, prologue=None):
        P.push()
        wg = P.sb("wg", [128, 8, 2816], BF16)
        wu = P.sb("wu", [128, 8, 2816], BF16)
        wd = P.sb("wd", [128, 22, D], BF16)
        def issue_loads():
            load_w_bf(wg, "wg", ffn_gate[l], 2816)
            load_w_bf(wu, "wu", ffn_up[l], 2816)
            load_w_bf(wd, "wd", ffn_down[l], D)

        if prologue is not None:
            prologue(issue_loads)
        else:
            issue_loads()
        N = 256
        NT = L // N
        xts = [P.sb("xtF%d" % i, [128, 8, N]) for i in range(3)]
        hts = [P.sb("htF%d" % i, [128, 8, N], BF16) for i in range(2)]
        hids = [P.sb("hid%d" % i, [128, 22, N], BF16) for i in range(2)]
        xsq = P.sb("xsq", [128, 8, N], BF16)
        rstd = P.sb("rstd", [128, N])
        tmps = [P.sb("tmpn%d" % i, [128, N]) for i in range(2)]
        halo = P.sb("halo", [128, 22, 2])
        P.memset("pool", halo[:], 0.0, ["halo%d" % j for j in range(22)])
        gbs = [P.sb("gb%d" % i, [128, N + 2]) for i in range(2)]
        accs = [P.sb("acc%d" % i, [128, N]) for i in range(2)]
        ssps = P.ps("ssps", [128, 512])
        gpss = [P.ps("gpsF%d" % i, [128, N]) for i in range(2)]
        upss = [P.ps("upsF%d" % i, [128, N]) for i in range(2)]
        dpss = [P.ps("dpsF%d" % i, [128, N]) for i in range(2)]
        dww = pvs("ffdww").rearrange("p (l j k) -> p l j k", l=2, j=22)
        dwb = pvs("ffdwb").rearrange("p (l j) -> p l j", l=2)
        cnt = {"it": 0}

        def emit_load(t):
            P.dma("sp", xts[t % 3][:], xview(xin_d)[:, :, t * N:(t + 1) * N], w=["xtF%d" % (t % 3)])

        def emit_norm(t):
            rmsnorm(xts[t % 3], "xtF%d" % (t % 3), N, A_(l, 1), B_(l, 1), hts[t % 2], "htF%d" % (t % 2), xsq, ssps, rstd, tmps)

        def emit_gu(t, j):
            ht, hn = hts[t % 2], "htF%d" % (t % 2)
            hid, hidn = hids[t % 2], "hid%d" % (t % 2)
            it = cnt["it"]
            cnt["it"] += 1
            gb, gbn = gbs[it % 2], "gb%d" % (it % 2)
            acc, acn = accs[it % 2], "acc%d" % (it % 2)
            gp, gpn = gpss[it % 2], "gpsF%d" % (it % 2)
            up, upn = upss[it % 2], "upsF%d" % (it % 2)
            hl = "halo%d" % j
            for kc in range(8):
                P.mm(gp[:], wg[:, kc, j * 128:(j + 1) * 128], ht[:, kc, :], kc == 0, kc == 7, ["wg", hn], [gpn])
            for kc in range(8):
                P.mm(up[:], wu[:, kc, j * 128:(j + 1) * 128], ht[:, kc, :], kc == 0, kc == 7, ["wu", hn], [upn])
            P.cp("act", gb[:, 2:N + 2], gp[:], [gpn], [gbn])
            P.cp("pool", gb[:, 0:2], halo[:, j, :], [hl], [gbn])
            P.act(acc[:], gp[:], AF.Identity, [gpn, "pv"], [acn], scale=dww[:, l, j, 2:3], bias=dwb[:, l, j:j + 1])
            P.stt(acc[:], gb[:, 1:N + 1], dww[:, l, j, 1:2], acc[:], ALU.mult, ALU.add, [gbn, "pv", acn], [acn])
            P.stt(acc[:], gb[:, 0:N], dww[:, l, j, 0:1], acc[:], ALU.mult, ALU.add, [gbn, "pv", acn], [acn])
            P.cp("pool", halo[:, j, :], gb[:, N:N + 2], [gbn], [hl])
            P.act(acc[:], acc[:], AF.Silu, [acn], [acn])
            P.tt("dve", hid[:, j, :], acc[:], up[:], ALU.mult, [acn, upn], [hidn])

        def emit_down(t):
            xt, xn = xts[t % 3], "xtF%d" % (t % 3)
            hid, hidn = hids[t % 2], "hid%d" % (t % 2)
            for oc in range(8):
                dp, dpn = dpss[oc % 2], "dpsF%d" % (oc % 2)
                for j in range(22):
                    P.mm(dp[:], wd[:, j, oc * 128:(oc + 1) * 128], hid[:, j, :], j == 0, j == 21, ["wd", hidn], [dpn])
                P.stt(xt[:, oc, :], dp[:], G_(l, 1)[:, oc:oc + 1], xt[:, oc, :], ALU.mult, ALU.add,
                      [dpn, "modT", xn], [xn])
            if final:
                rmsnorm(xt, xn, N, pvs("finalg"), None, xt, xn, xsq, ssps, rstd, None, final=True)
            store_x(xout_d, xt, xn, slice(t * N, (t + 1) * N), t % 2)

        emit_load(0)
        emit_load(1)
        emit_norm(0)
        for t in range(NT):
            for j in range(22):
                emit_gu(t, j)
                if j == 2 and t >= 1:
                    emit_down(t - 1)
                if j == 8 and t + 2 < NT:
                    emit_load(t + 2)
                if j == 10 and t + 1 < NT:
                    emit_norm(t + 1)
        emit_down(NT - 1)
        P.pop()

    ffn_phase(0, xa_d, xb_d, prologue=outproj_phase)
    if stop_after == "E":
        P.build()
        return nc

    P.push()
    cwin = P.sb("cwin", [128, 8, 2048], BF16)
    cwout = P.sb("cwout", [128, 8, D], BF16)
    load_w_bf(cwin, "cwin", cm_w_in, 2048)
    load_w_bf(cwout, "cwout", cm_w_out, D)
    dg = P.sb("dg", [128, 8, 31, 128], BF16)
    dwv = pvs("cmdww").rearrange("p (c k) -> p c k", c=8)
    for oc in range(8):
        P.tt("pool" if oc % 2 else "dve", dg[:, oc, :, :], identf[:].unsqueeze(1).to_broadcast([128, 31, 128]),
             dwv[:, oc, :].unsqueeze(2).to_broadcast([128, 31, 128]), ALU.mult, ["identf", "pv"], ["dg%d" % oc])
    N = 256
    NT = L // N
    xts = [P.sb("xtG%d" % i, [128, 8, N]) for i in range(3)]
    ht = P.sb("htG", [128, 8, N], BF16)
    xsq = P.sb("xsq", [128, 8, N], BF16)
    rstd = P.sb("rstd", [128, N])
    rstd2 = P.sb("rstd2", [128, N])
    tmps = [P.sb("tmpn%d" % i, [128, N]) for i in range(3)]
    ags = [P.sb("ag%d" % i, [128, 8, N + 30], BF16) for i in range(2)]
    P.memset("pool", ags[1][:, :, N:N + 30], 0.0, ["ag1_%d" % oc for oc in range(8)])
    cvt = P.sb("cvt", [128, 8, N])
    cvb = P.sb("cvb", [128, 8, N], BF16)
    cvq = P.sb("cvq", [128, 8, N], BF16)
    mu = P.sb("mu", [128, N])
    msq = P.sb("msq", [128, N])
    lnb = P.sb("lnb", [128, 8, N], BF16)
    sgs = [P.sb("sgG%d" % i, [128, N]) for i in range(4)]
    ssps = P.ps("ssps", [128, N])
    a1p = [P.ps("a1p%d" % i, [128, N]) for i in range(2)]
    a2p = [P.ps("a2p%d" % i, [128, N]) for i in range(2)]
    cps = [P.ps("cps%d" % i, [128, N]) for i in range(2)]
    sqps = P.ps("sqps", [128, N])
    bin_ = pvs("cmbin")
    hbin = P.sb("hbin", [128, 8])
    P.ts("dve", hbin[:], bin_[:, 8:16], 0.5, ALU.mult, ["pv"], ["hb"])

    def c_load(t):
        P.dma("sp", xts[t % 3][:], xview(xb_d)[:, :, t * N:(t + 1) * N], w=["xtG%d" % (t % 3)])

    def c_norm(t):
        rmsnorm(xts[t % 3], "xtG%d" % (t % 3), N, A_(1, 0), B_(1, 0), ht, "htG", xsq, ssps, rstd, tmps)

    def c_halo(t):
        ag, pag = ags[t % 2], ags[(t + 1) % 2]
        P.cp("pool", ag[:, :, 0:30], pag[:, :, N:N + 30], ["ag%d_%d" % ((t + 1) % 2, oc) for oc in range(8)],
             ["ag%d_%d" % (t % 2, oc) for oc in range(8)])

    def c_s1(t, oc):
        ag, agn = ags[t % 2], "ag%d_%d" % (t % 2, oc)
        p1, p1n = a1p[oc % 2], "a1p%d" % (oc % 2)
        p2, p2n = a2p[oc % 2], "a2p%d" % (oc % 2)
        sg, sgn = sgs[oc % 4], "sgG%d" % (oc % 4)
        for kc in range(8):
            P.mm(p1[:], cwin[:, kc, oc * 128:(oc + 1) * 128], ht[:, kc, :], kc == 0, kc == 7, ["cwin", "htG"], [p1n])
        for kc in range(8):
            P.mm(p2[:], cwin[:, kc, 1024 + oc * 128:1024 + (oc + 1) * 128], ht[:, kc, :], kc == 0, kc == 7, ["cwin", "htG"], [p2n])
        P.act(sg[:], p2[:], AF.Tanh, [p2n, "hb"], [sgn], scale=0.5, bias=hbin[:, oc:oc + 1])
        P.ts("pool", sg[:], sg[:], 0.5, ALU.mult, [sgn], [sgn], s2=0.5, op1=ALU.add)
        P.stt(ag[:, oc, 30:N + 30], p1[:], bin_[:, oc:oc + 1], sg[:], ALU.add, ALU.mult, [p1n, "pv", sgn], [agn])

    def c_s2(t):
        ag = ags[t % 2]
        for oc in range(8):
            agn = "ag%d_%d" % (t % 2, oc)
            cp_, cpn = cps[oc % 2], "cps%d" % (oc % 2)
            for k in range(31):
                P.mm(cp_[:], dg[:, oc, k, :], ag[:, oc, k:k + N], k == 0, k == 30, ["dg%d" % oc, agn], [cpn])
            P.act(cvt[:, oc, :], cp_[:], AF.Identity, [cpn, "pv"], ["cvt%d" % oc], bias=pvs("cmdwb", oc, oc + 1))
            P.cp("dve", cvb[:, oc, :], cvt[:, oc, :], ["cvt%d" % oc], ["cvb"])
            P.act(cvq[:, oc, :], cvt[:, oc, :], AF.Square, ["cvt%d" % oc], ["cvq"])

    def c_stats(t):
        for oc in range(8):
            P.mm(ssps[:], ones_bf[:], cvb[:, oc, :], oc == 0, oc == 7, ["ones_bf", "cvb"], ["ssps"])
        for oc in range(8):
            P.mm(sqps[:], ones_bf[:], cvq[:, oc, :], oc == 0, oc == 7, ["ones_bf", "cvq"], ["sqps"])
        P.act(mu[:], ssps[:], AF.Identity, ["ssps"], ["mu"], scale=1.0 / D)
        P.tt("dve", msq[:], mu[:], mu[:], ALU.mult, ["mu"], ["msq"])
        P.stt(msq[:], sqps[:], 1.0 / D, msq[:], ALU.mult, ALU.subtract, ["sqps", "msq"], ["msq"])
        P.act(rstd2[:], msq[:], AF.Ln, ["msq", "epsb"], ["rstd2"], bias=epsb[:, 0:1])
        P.act(rstd2[:], rstd2[:], AF.Exp, ["rstd2"], ["rstd2"], scale=-0.5)

    def c_normalize(t, oc):
        cn = "cvt%d" % oc
        P.tt("dve", cvt[:, oc, :], cvt[:, oc, :], mu[:], ALU.subtract, [cn, "mu"], [cn])
        P.tt("dve", cvt[:, oc, :], cvt[:, oc, :], rstd2[:], ALU.mult, [cn, "rstd2"], [cn])
        P.act(lnb[:, oc, :], cvt[:, oc, :], AF.Silu, [cn, "pv"], ["lnb"],
              scale=pvs("cmlng", oc, oc + 1), bias=pvs("cmlnb", oc, oc + 1))

    def c_s4(t):
        xt, xn = xts[t % 3], "xtG%d" % (t % 3)
        for oc in range(8):
            p1, p1n = a1p[oc % 2], "a1p%d" % (oc % 2)
            sg, sgn = sgs[oc % 2], "sgG%d" % (oc % 2)
            for kc in range(8):
                P.mm(p1[:], cwout[:, kc, oc * 128:(oc + 1) * 128], lnb[:, kc, :], kc == 0, kc == 7, ["cwout", "lnb"], [p1n])
            P.act(sg[:], p1[:], AF.Identity, [p1n, "pv"], [sgn], bias=pvs("cmbout", oc, oc + 1))
            P.stt(xt[:, oc, :], sg[:], G_(1, 0)[:, oc:oc + 1], xt[:, oc, :], ALU.mult, ALU.add,
                  [sgn, "modT", xn], [xn])
        store_x(xa_d, xt, xn, slice(t * N, (t + 1) * N), t % 2)

    c_load(0)
    c_load(1)
    c_norm(0)
    c_halo(0)
    for oc in range(8):
        c_s1(0, oc)
    for t in range(NT):
        if t + 1 < NT:
            c_norm(t + 1)
        c_s2(t)
        c_stats(t)
        if t + 1 < NT:
            c_halo(t + 1)
        for oc in range(8):
            c_normalize(t, oc)
            if t + 1 < NT:
                c_s1(t + 1, oc)
        c_s4(t)
        if t + 2 < NT:
            c_load(t + 2)
    P.pop()
    if stop_after == "F":
        P.build()
        return nc

    ffn_phase(1, xa_d, outT, final=True)
    P.build()
    return nc


def _rel_bucket(dist):
    n = np.maximum(dist, 0)
    max_exact = 16
    nf = np.maximum(n, 1).astype(np.float32)
    large = max_exact + (np.log(nf / max_exact) / math.log(128 / max_exact) * (32 - max_exact)).astype(np.int32)
    large = np.minimum(large, 31)
    return np.where(n < max_exact, n, large)


def _t128(v):
    v = np.asarray(v, np.float32)
    return np.ascontiguousarray(v.reshape(-1, 128).T)


def prepare_shared(inp):
    f = lambda a: np.ascontiguousarray(np.asarray(a, np.float32))
    sh = {}
    sh["ident"] = np.eye(128, dtype=np.float32)
    sh["mod_w"] = f(inp["mod_w"])
    sh["w_in0"] = f(inp["ab_w_in"][0])
    sh["glu_w"] = f(inp["ssm_glu_w"][0])
    sh["w_out0"] = f(inp["ab_w_out"][0])
    sh["cm_w_in"] = f(inp["cm_w_in"][0])
    sh["cm_w_out"] = f(inp["cm_w_out"][0])
    sh["ffn_up"] = f(inp["ffn_w_up"])
    sh["ffn_gate"] = f(inp["ffn_w_gate"])
    sh["ffn_down"] = f(inp["ffn_w_down"])

    def gq(a):
        a = np.asarray(a, np.float32)
        rest = a.shape[2:]
        a = a.reshape((16, 2, 64) + rest)
        a = np.moveaxis(a, 0, 2)
        return np.ascontiguousarray(a.reshape((128, 16) + rest))

    lamr = gq(inp["ssm_a_re"][0])
    lami = gq(inp["ssm_a_im"][0])
    logdt = gq(np.repeat(np.asarray(inp["ssm_log_dt"][0], np.float32)[:, None], 64, axis=1))
    sh["s5p"] = np.ascontiguousarray(np.concatenate([lamr, lami, logdt], axis=1))
    br = gq(inp["ssm_b_re"][0])
    bi = gq(inp["ssm_b_im"][0])
    cr = gq(np.transpose(np.asarray(inp["ssm_c_re"][0], np.float32), (0, 2, 1)))
    ci = gq(np.transpose(np.asarray(inp["ssm_c_im"][0], np.float32), (0, 2, 1)))
    sh["s5bc"] = np.ascontiguousarray(np.stack([br, bi, cr, ci], axis=1))
    rb = np.asarray(inp["rel_bias"], np.float32)
    kp = np.arange(128)[:, None]
    qp = np.arange(256)[None, :]
    tabs = []
    for t in range(4):
        if t < 2:
            dist = qp - (128 * t + kp)
        else:
            dist = qp + 256 - (128 * (t - 2) + kp)
        tabs.append(_rel_bucket(dist))
    idx = np.stack(tabs, axis=1)
    sh["biasg"] = np.ascontiguousarray(np.transpose(rb[idx], (3, 0, 1, 2)))
    mc = np.zeros((128, 2, 256), np.float32)
    for t in range(2):
        mc[:, t, :] = np.where(qp - (128 * t + kp) >= 0, 0.0, NEG)
    sh["maskc"] = mc
    mv = np.zeros((128, 32, 16), np.float32)
    no = np.ones((128, 32, 16), np.float32)
    for qt in range(32):
        mv[:, qt, qt // 2:] = -1e30
        no[:, qt, qt // 2] = 0.0
    sh["maskv"] = mv
    sh["notown"] = no
    oh = np.zeros((16, L), np.float32)
    for j in range(16):
        oh[j, 256 * j:256 * (j + 1)] = -NEG
    sh["onehot"] = oh
    return sh


def prepare_pvec(inp, b):
    pvv = np.zeros((128, NPV), np.float32)

    def put(name, arr):
        o, w = PV[name]
        arr = np.asarray(arr, np.float32)
        assert arr.shape == (128, w), (name, arr.shape, w)
        pvv[:, o:o + w] = arr

    put("c", _t128(inp["c"][b]))
    put("modb", np.concatenate([_t128(inp["mod_b"][l]) for l in range(2)], axis=1))
    put("normg", np.concatenate([_t128(inp["norm_g"][l, s]) for l in range(2) for s in range(2)], axis=1))
    put("finalg", _t128(inp["final_g"]))
    put("glub", _t128(inp["ssm_glu_b"][0]))
    put("cmbin", _t128(inp["cm_b_in"][0]))
    dw = np.asarray(inp["cm_dw_w"][0], np.float32)
    put("cmdww", np.transpose(dw.reshape(31, 8, 128), (2, 1, 0)).reshape(128, 248))
    put("cmdwb", _t128(inp["cm_dw_b"][0]))
    put("cmlng", _t128(inp["cm_ln_g"][0]))
    put("cmlnb", _t128(inp["cm_ln_b"][0]))
    put("cmbout", _t128(inp["cm_b_out"][0]))
    fw = np.asarray(inp["ffn_dw_w"], np.float32)
    put("ffdww", np.transpose(fw.reshape(2, 3, 22, 128), (3, 0, 2, 1)).reshape(128, 132))
    put("ffdwb", np.concatenate([_t128(inp["ffn_dw_b"][l]) for l in range(2)], axis=1))
    put("ssmd", _t128(np.asarray(inp["ssm_d"][0], np.float32).reshape(-1)))
    put("b31", np.repeat(np.asarray(inp["rel_bias"], np.float32)[31][None, :], 128, axis=0))
    return pvv


_CACHE = {}


def kernel(**inputs):
    x = np.asarray(inputs["x"], np.float32)
    nb = x.shape[0]
    if "nc" not in _CACHE:
        _CACHE["nc"] = build_program()
    nc = _CACHE["nc"]
    sh = prepare_shared(inputs)
    in_maps = []
    for b in range(nb):
        m = dict(sh)
        m["xT"] = np.ascontiguousarray(x[b].T)
        m["pvec"] = prepare_pvec(inputs, b)
        in_maps.append(m)
    res = run_bass_kernel_spmd(nc, in_maps, core_ids=list(range(nb)))
    out = np.stack([np.ascontiguousarray(r["outT"].T) for r in res.results], axis=0)
    return out.astype(np.float32)
```

```python
import math
import numpy as np
from contextlib import ExitStack
import concourse.bass as bass
import concourse.mybir as mybir
from concourse.bass_utils import run_bass_kernel_spmd

F32 = mybir.dt.float32
BF16 = mybir.dt.bfloat16
AF = mybir.ActivationFunctionType
ALU = mybir.AluOpType
AX = mybir.AxisListType

L = 4096
D = 1024
EPS = 1e-6
NEG = -30000.0


class Prog:
    ENGS = ("pe", "dve", "act", "pool", "sp")

    def __init__(self, nc):
        self.nc = nc
        self.ops = []
        self.relax = True
        self.root = ExitStack()
        self.stacks = [self.root]

    def sb(self, name, shape, dtype=F32):
        self.uid = getattr(self, "uid", 0) + 1
        return self.stacks[-1].enter_context(self.nc.sbuf_tensor("%s_s%d" % (name, self.uid), list(shape), dtype))

    def ps(self, name, shape, dtype=F32):
        self.uid = getattr(self, "uid", 0) + 1
        full = 512 if dtype == F32 else 1024
        t = self.stacks[-1].enter_context(self.nc.psum_tensor("%s_p%d" % (name, self.uid), [128, full], dtype))
        self.last_ps_raw = t
        n = 1
        for d in shape[1:]:
            n *= d
        v = t[0:shape[0], 0:n]
        if len(shape) == 3:
            v = v.rearrange("p (a b) -> p a b", a=shape[1])
        elif len(shape) == 4:
            v = v.rearrange("p (a b c) -> p a b c", a=shape[1], b=shape[2])
        return v

    def push(self):
        self.stacks.append(ExitStack())

    def pop(self):
        self.ops.append(dict(barrier=True))
        self.stacks.pop().close()

    def op(self, eng, fn, reads=(), writes=(), drain=False):
        self.ops.append(dict(eng=eng, fn=fn, reads=tuple(reads), writes=tuple(writes), dma=None, drain=drain))

    def dma(self, eng, out, in_, r=(), w=(), key=None):
        if key is None:
            key = w[0]
        r = [x for x in r if not x.startswith("d_")]
        w = [x for x in w if not x.startswith("d_")]
        self.ops.append(dict(eng=eng, fn=lambda e: e.dma_start(out=out, in_=in_),
                             reads=tuple(r), writes=tuple(w), dma=key))

    def mm(self, out, lhsT, rhs, start, stop, r, w, drain=False, **kw):
        self.op("pe", lambda e: e.matmul(out, lhsT, rhs, start=start, stop=stop, **kw), r, w, drain=drain)

    def tr(self, out, in_, ident, r, w):
        self.op("pe", lambda e: e.transpose(out, in_, ident), r, w)

    def act(self, out, in_, func, r, w, scale=1.0, bias=0.0):
        self.op("act", lambda e: e.activation(out=out, in_=in_, func=func, bias=bias, scale=scale), r, w)

    def tt(self, eng, out, in0, in1, op, r, w):
        self.op(eng, lambda e: e.tensor_tensor(out=out, in0=in0, in1=in1, op=op), r, w)

    def ts(self, eng, out, in0, s1, op0, r, w, s2=None, op1=None):
        if op1 is None:
            self.op(eng, lambda e: e.tensor_scalar(out=out, in0=in0, scalar1=s1, scalar2=None, op0=op0), r, w)
        else:
            self.op(eng, lambda e: e.tensor_scalar(out=out, in0=in0, scalar1=s1, scalar2=s2, op0=op0, op1=op1), r, w)

    def stt(self, out, in0, scalar, in1, op0, op1, r, w):
        self.op("dve", lambda e: e.scalar_tensor_tensor(out=out, in0=in0, scalar=scalar, in1=in1, op0=op0, op1=op1), r, w)

    def cp(self, eng, out, in_, r, w):
        if eng == "act":
            self.op("act", lambda e: e.copy(out=out, in_=in_), r, w)
        else:
            self.op(eng, lambda e: e.tensor_copy(out=out, in_=in_), r, w)

    def memset(self, eng, ap, val, w):
        self.op(eng, lambda e: e.memset(ap, val), (), w)

    def build(self):
        nc = self.nc
        ops = self.ops
        n = len(ops)
        last_writer, readers, last_dma_on_key, last_of_stream = {}, {}, {}, {}
        deps = [None] * n
        needed = [False] * n
        pending_bar, bar_seen = set(), set(self.ENGS)
        last_pe = None
        eseq = {}
        for i, o in enumerate(ops):
            if "barrier" in o:
                pending_bar = set(last_of_stream.values())
                bar_seen = set()
                deps[i] = set()
                continue
            o["stream"] = ("dma", o["dma"]) if o["dma"] is not None else ("eng", o["eng"])
            d = set()
            for b in o["reads"]:
                if b in last_writer:
                    d.add(last_writer[b])
            for b in o["writes"]:
                if b in last_writer:
                    d.add(last_writer[b])
                d.update(readers.get(b, ()))
            if o["dma"] is not None:
                k = o["dma"]
                if k in last_dma_on_key:
                    d.add(last_dma_on_key[k])
                last_dma_on_key[k] = i
                needed[i] = True
            if o["eng"] not in bar_seen:
                d |= pending_bar
                bar_seen.add(o["eng"])
            d.discard(i)
            if o["dma"] is None:
                eseq[o["eng"]] = eseq.get(o["eng"], 0) + 1
                o["seq"] = eseq[o["eng"]]
                if o["eng"] in ("dve", "act") and self.relax:
                    d = {j for j in d if not ("barrier" not in ops[j] and ops[j]["dma"] is None
                                              and ops[j]["eng"] == o["eng"] and o["seq"] - ops[j]["seq"] >= 3)}
            if o["eng"] == "pe" and o["dma"] is None:
                d = {j for j in d if not (ops[j]["eng"] == "pe" and ops[j]["dma"] is None)}
                if o.get("drain") and last_pe is not None:
                    d.add(last_pe)
                last_pe = i
            deps[i] = d
            for j in d:
                needed[j] = True
            for b in o["writes"]:
                last_writer[b] = i
                readers[b] = []
            for b in o["reads"]:
                if b not in o["writes"]:
                    readers.setdefault(b, []).append(i)
            last_of_stream[o["stream"]] = i
        for i in last_of_stream.values():
            needed[i] = True
        stream_count, sig = {}, [None] * n
        for i, o in enumerate(ops):
            if "barrier" in o:
                continue
            if needed[i]:
                s = o["stream"]
                inc = 16 if s[0] == "dma" else 1
                stream_count[s] = stream_count.get(s, 0) + inc
                sig[i] = (s, stream_count[s], inc)
        sems = {}
        for k, s in enumerate(stream_count):
            sems[s] = self.root.enter_context(nc.semaphore("sem%d" % k))
        self.n_sems = len(sems)
        per_eng = {e: [] for e in self.ENGS}
        waited = {e: {} for e in self.ENGS}
        for i, o in enumerate(ops):
            if "barrier" in o:
                continue
            e = o["eng"]
            wt = {}
            for j in deps[i]:
                s, c, _ = sig[j]
                if waited[e].get(s, 0) >= c:
                    continue
                wt[s] = max(wt.get(s, 0), c)
            for s, c in wt.items():
                waited[e][s] = c
            per_eng[e].append((i, wt))
        finals = dict(stream_count)
        self.n_ops = {e: len(v) for e, v in per_eng.items()}

        with nc.Block() as blk:
            def emit(engobj, ename):
                for i, wt in per_eng[ename]:
                    for s, c in wt.items():
                        engobj.wait_ge(sems[s], c)
                    ins = ops[i]["fn"](engobj)
                    if sig[i] is not None:
                        s, c, inc = sig[i]
                        ins.then_inc(sems[s], inc)
                if ename == "sp":
                    for s, c in finals.items():
                        engobj.wait_ge(sems[s], c)

            @blk.tensor
            def _(e):
                emit(e, "pe")

            @blk.vector
            def _(e):
                emit(e, "dve")

            @blk.scalar
            def _(e):
                emit(e, "act")

            @blk.gpsimd
            def _(e):
                emit(e, "pool")

            @blk.sync
            def _(e):
                emit(e, "sp")
        while len(self.stacks) > 1:
            self.stacks.pop().close()
        self.root.close()


PV = {}
_off = 0
for _n, _w in [("c", 8), ("modb", 96), ("normg", 32), ("finalg", 8), ("glub", 4), ("cmbin", 16),
               ("cmdww", 248), ("cmdwb", 8), ("cmlng", 8), ("cmlnb", 8), ("cmbout", 8),
               ("ffdww", 132), ("ffdwb", 44), ("ssmd", 4), ("b31", 8)]:
    PV[_n] = (_off, _w)
    _off += _w
NPV = _off


def build_program(stop_after=None, debug=False):
    nc = bass.Bass("TRN2", target_bir_lowering=False)
    P = Prog(nc)

    def din(name, shape, dt=F32):
        return nc.dram_tensor(name, list(shape), dt, kind="ExternalInput").ap()

    def dscr(name, shape, dt=F32):
        kind = "ExternalOutput" if debug else "Internal"
        return nc.dram_tensor(name, list(shape), dt, kind=kind).ap()

    xT = din("xT", [D, L])
    pvec_d = din("pvec", [128, NPV])
    ident_d = din("ident", [128, 128])
    mod_w = din("mod_w", [2, D, 6 * D])
    w_in0 = din("w_in0", [D, 2048])
    glu_w = din("glu_w", [512, 512])
    w_out0 = din("w_out0", [D, D])
    cm_w_in = din("cm_w_in", [D, 2048])
    cm_w_out = din("cm_w_out", [D, D])
    ffn_up = din("ffn_up", [2, D, 2816])
    ffn_gate = din("ffn_gate", [2, D, 2816])
    ffn_down = din("ffn_down", [2, 2816, D])
    s5p_d = din("s5p", [128, 48])
    s5bc_d = din("s5bc", [128, 4, 16, 16])
    biasg_d = din("biasg", [8, 128, 4, 256])
    maskc_d = din("maskc", [128, 2, 256])
    maskv_d = din("maskv", [128, 32, 16])
    notown_d = din("notown", [128, 32, 16])
    onehot_d = din("onehot", [16, L])
    outT = nc.dram_tensor("outT", [D, L], F32, kind="ExternalOutput").ap()

    uT_d = dscr("uT_d", [512, L], BF16)
    qT_d = dscr("qT_d", [512, L], BF16)
    kT_d = dscr("kT_d", [512, L], BF16)
    vaug_d = dscr("vaug_d", [L, 768], BF16)
    ycat_d = dscr("ycat_d", [D, L], BF16)
    xa_d = dscr("xa_d", [D, L])
    xb_d = dscr("xb_d", [D, L])

    def xview(ap):
        return ap.rearrange("(kc p) t -> p kc t", p=128)

    ones_bf = P.sb("ones_bf", [128, 128], BF16)
    P.memset("dve", ones_bf[:], 1.0, ["ones_bf"])
    identf = P.sb("identf", [128, 128])
    P.dma("sp", identf[:], ident_d, w=["identf"])
    identb = P.sb("identb", [128, 128], BF16)
    P.cp("dve", identb[:], identf[:], ["identf"], ["identb"])
    pv = P.sb("pv", [128, NPV])
    P.dma("sp", pv[:], pvec_d, w=["pv"])

    def pvs(name, a=0, b=None):
        o, w = PV[name]
        if b is None:
            b = w
        return pv[:, o + a:o + b]

    modT = P.sb("modT", [128, 2, 48])
    AB = P.sb("AB", [128, 2, 2, 8])
    rr = {"i": 0}

    cs2 = P.sb("cs2", [128, 8, 2])
    for dup in range(2):
        P.act(cs2[:, :, dup], pvs("c"), AF.Silu, ["pv"], ["cs2"])
    ng = pvs("normg").rearrange("p (l s k) -> p l s k", l=2, s=2)

    def mod_block(l, blk, mwbuf, mwname, mps, mpsname):
        mwv = mod_w[l].rearrange("(kc p) n -> p kc n", p=128)
        P.dma("sp", mwbuf[:], mwv[:, :, blk * 512:(blk + 1) * 512], w=[mwname])
        for jj in range(4):
            j = blk * 4 + jj
            for kc in range(8):
                P.mm(mps[:, j, :], mwbuf[:, kc, jj * 128:(jj + 1) * 128], cs2[:, kc, :],
                     kc == 0, kc == 7, [mwname, "cs2"], [mpsname])

    def mod_finish(l, mps, mpsname):
        P.tt("dve", modT[:, l, :], mps[:, :, 0], pvs("modb").rearrange("p (l j) -> p l j", l=2)[:, l, :], ALU.add,
             [mpsname, "pv"], ["modT"])
        for s_ in range(2):
            sc = modT[:, l, 8 + 24 * s_:16 + 24 * s_]
            P.stt(AB[:, l, s_, :], sc, 1.0, ng[:, l, s_, :], ALU.add, ALU.mult, ["modT", "pv"], ["AB"])

    P.push()
    mw = [P.sb("mw%d" % i, [128, 8, 512]) for i in range(6)]
    modps = P.ps("modps", [128, 48, 2])
    for blk in range(12):
        mod_block(0, blk, mw[blk % 6], "mw%d" % (blk % 6), modps, "modps")
    mod_finish(0, modps, "modps")
    P.pop()
    if stop_after == "M":
        dbgm = dscr("dbgm", [128, 96])
        P.dma("sp", dbgm, modT[:].rearrange("p l j -> p (l j)"), r=["modT"], w=[], key="st_dbgm")
        P.build()
        return nc

    def A_(l, s):
        return AB[:, l, s, :]

    def B_(l, s):
        return modT[:, l, 24 * s:24 * s + 8]

    def G_(l, s):
        return modT[:, l, 16 + 24 * s:24 + 24 * s]

    def rmsnorm(xt, xname, N, A, B, ht, hname, xsq, ssps, rstd, tmps, final=False):
        for kc in range(8):
            P.act(xsq[:, kc, :N], xt[:, kc, :N], AF.Square, [xname], ["xsq"])
        for kc in range(8):
            P.mm(ssps[:, :N], ones_bf[:], xsq[:, kc, :N], kc == 0, kc == 7, ["ones_bf", "xsq"], ["ssps"])
        P.act(rstd[:, :N], ssps[:, :N], AF.Ln, ["ssps", "epsb"], ["rstd"], scale=1.0 / D, bias=epsb[:, 0:1])
        P.act(rstd[:, :N], rstd[:, :N], AF.Exp, ["rstd"], ["rstd"], scale=-0.5)
        for kc in range(8):
            if final:
                P.stt(ht[:, kc, :N], xt[:, kc, :N], A[:, kc:kc + 1], rstd[:, :N], ALU.mult, ALU.mult,
                      [xname, "rstd", "AB", "pv"], [hname])
            else:
                ti = rr["i"] % len(tmps)
                rr["i"] += 1
                tn = "tmpn%d" % ti
                P.stt(tmps[ti][:, :N], xt[:, kc, :N], A[:, kc:kc + 1], rstd[:, :N], ALU.mult, ALU.mult,
                      [xname, "rstd", "AB"], [tn])
                P.act(ht[:, kc, :N], tmps[ti][:, :N], AF.Identity, [tn, "modT"], [hname], bias=B[:, kc:kc + 1])

    def load_w_bf(dst, name, src, ncols):
        v = src.rearrange("(kc p) n -> p kc n", p=128)
        c0 = 0
        while c0 < ncols:
            c1 = min(ncols, c0 + 2048)
            P.dma("pool", dst[:, :, c0:c1], v[:, :, c0:c1], w=[name], key="wld_" + name)
            c0 = c1

    def store_x(dst, tile_, name, sl, b):
        for oc in range(8):
            P.dma("sp", dst[oc * 128:(oc + 1) * 128, sl], tile_[:, oc, :], r=[name], w=[], key="st_x%d_%d" % (b, oc))

    epsb = P.sb("epsb", [128, 1])
    P.memset("dve", epsb[:], EPS, ["epsb"])

    P.push()
    win = P.sb("win", [128, 8, 2048], BF16)
    load_w_bf(win, "win", w_in0, 2048)
    xts = [P.sb("xt%d" % i, [128, 8, 512]) for i in range(2)]
    hts = [P.sb("ht%d" % i, [128, 8, 512], BF16) for i in range(2)]
    xsq = P.sb("xsq", [128, 8, 512], BF16)
    rstd = P.sb("rstd", [128, 512])
    tmps = [P.sb("tmpn%d" % i, [128, 512]) for i in range(3)]
    obs = [P.sb("ob%d" % i, [128, 512], BF16) for i in range(4)]
    vst = [P.sb("vst%d" % i, [128, 4, 768], BF16) for i in range(2)]
    ssps = P.ps("ssps", [128, 512])
    pjs = [P.ps("pj%d" % i, [128, 512]) for i in range(3)]
    pvp = [P.ps("pvp%d" % i, [128, 512]) for i in range(2)]
    for i in range(2):
        P.memset("pool", vst[i][:].rearrange("p s (r c) -> p s r c", c=192)[:, :, :, 64:128], 1.0, ["vst%d" % i])
    mwA = [P.sb("mwA%d" % i, [128, 8, 512]) for i in range(2)]
    modps1 = P.ps("modps1", [128, 48, 2])
    mblk = {"i": 0}
    oi = 0
    import os
    SK = os.environ.get("KSKIP", "")
    for tt in range(8 if "1" not in SK else 1):
        xt, xn = xts[tt % 2], "xt%d" % (tt % 2)
        ht, hn = hts[tt % 2], "ht%d" % (tt % 2)
        sl = slice(tt * 512, (tt + 1) * 512)
        if tt == 0:
            P.dma("sp", xt[:], xview(xT)[:, :, sl], w=[xn])
        if tt + 1 < 8:
            P.dma("sp", xts[(tt + 1) % 2][:], xview(xT)[:, :, (tt + 1) * 512:(tt + 2) * 512], w=["xt%d" % ((tt + 1) % 2)])
        rmsnorm(xt, xn, 512, A_(0, 0), B_(0, 0), ht, hn, xsq, ssps, rstd, tmps)
        for oc in range(12 if "p" not in SK else 0):
            pj, pn = pjs[oc % 3], "pj%d" % (oc % 3)
            for kc in range(8):
                P.mm(pj[:], win[:, kc, oc * 128:(oc + 1) * 128], ht[:, kc, :], kc == 0, kc == 7, ["win", hn], [pn])
            ob, on = obs[oi % 4], "ob%d" % (oi % 4)
            oi += 1
            P.act(ob[:], pj[:], AF.Identity, [pn], [on], scale=(0.125 if 4 <= oc < 8 else 1.0))
            dst = (uT_d, qT_d, kT_d)[oc // 4]
            P.dma("sp", dst[(oc % 4) * 128:(oc % 4 + 1) * 128, sl], ob[:], r=[on], w=["d_%d_%d" % (oc // 4, tt)], key="st_" + on)
        vs, vn = vst[tt % 2], "vst%d" % (tt % 2)
        for sub in range(4 if "v" not in SK else 0):
            pp, ppn = pvp[sub % 2], "pvp%d" % (sub % 2)
            for kc in range(8):
                P.mm(pp[:], ht[:, kc, sub * 128:(sub + 1) * 128], win[:, kc, 1536:2048], kc == 0, kc == 7, ["win", hn], [ppn])
            v4 = vs[:, sub, :].rearrange("p (r c) -> p r c", c=192)
            p4 = pp[:].rearrange("p (r c) -> p r c", c=128)
            ve = "act" if sub % 2 else "dve"
            P.cp(ve, v4[:, :, 0:64], p4[:, :, 0:64], [ppn], [vn])
            P.cp(ve, v4[:, :, 128:192], p4[:, :, 64:128], [ppn], [vn])
        for sub in range(4 if "s" not in SK else 0):
            P.dma("sp", vaug_d[tt * 512 + sub * 128:tt * 512 + (sub + 1) * 128, :], vs[:, sub, :], r=[vn], w=[], key="st_" + vn)
        for _ in range(2 if tt < 4 else 1):
            bi = mblk["i"]
            mblk["i"] += 1
            mod_block(1, bi, mwA[bi % 2], "mwA%d" % (bi % 2), modps1, "modps1")
    mod_finish(1, modps1, "modps1")
    P.pop()
    if stop_after == "A":
        P.build()
        return nc

    P.push()
    U = P.sb("U", [128, 4, L], BF16)
    for c in range(4):
        P.dma("sp", U[:, c, :], uT_d[c * 128:(c + 1) * 128, :], r=["d_0_%d" % t for t in range(8)], w=["U"], key="ldU")
    W1 = P.sb("W1", [128, 16, 8, 2, 128], BF16)
    CPb = P.sb("CPb", [128, 8, 2, 16, 32], BF16)
    BD = P.sb("BD", [128, 4, 8, 128], BF16)
    DBL = P.sb("DBL", [128, 9, 3, 16])
    gluw = P.sb("gluw", [128, 4, 512], BF16)
    load_w_bf(gluw, "gluw", glu_w, 512)

    P.push()
    s5p = P.sb("s5p", [128, 48])
    P.dma("sp", s5p[:], s5p_d, w=["s5p"])
    s5bc = P.sb("s5bc", [128, 4, 16, 16])
    P.dma("sp", s5bc[:], s5bc_d, w=["s5bc"])
    lamr, lami, logdt = s5p[:, 0:16], s5p[:, 16:32], s5p[:, 32:48]
    sc_ = P.sb("s5t", [128, 24, 16])
    S = "s5t"

    def T(i):
        return sc_[:, i, :]

    def vv(out, a, b, op):
        P.tt("dve", out, a, b, op, [S, "s5p", "PW"], [S])

    halfpi = P.sb("halfpi", [128, 1])
    P.memset("dve", halfpi[:], math.pi / 2, ["halfpi"])
    P.act(T(0), logdt, AF.Exp, ["s5p"], [S])
    vv(T(1), lami, T(0), ALU.mult)
    P.act(T(2), T(1), AF.Sin, [S], [S], scale=0.125)
    P.act(T(3), T(1), AF.Sin, [S, "halfpi"], [S], scale=-0.125, bias=halfpi[:, 0:1])
    for _ in range(3):
        vv(T(4), T(3), T(3), ALU.mult)
        vv(T(5), T(2), T(2), ALU.mult)
        P.stt(T(6), T(2), 2.0, T(3), ALU.mult, ALU.mult, [S], [S])
        vv(T(3), T(4), T(5), ALU.subtract)
        P.cp("dve", T(2), T(6), [S], [S])
    vv(T(4), lamr, T(0), ALU.mult)
    P.act(T(4), T(4), AF.Exp, [S], [S])
    vv(T(7), T(4), T(3), ALU.mult)
    vv(T(8), T(4), T(2), ALU.mult)
    vv(T(9), lamr, lamr, ALU.mult)
    vv(T(10), lami, lami, ALU.mult)
    vv(T(9), T(9), T(10), ALU.add)
    P.op("dve", lambda e: e.reciprocal(out=T(9), in_=T(9)), [S], [S])
    P.ts("dve", T(10), T(7), -1.0, ALU.add, [S], [S])
    vv(T(11), T(10), lamr, ALU.mult)
    vv(T(12), T(8), lami, ALU.mult)
    vv(T(11), T(11), T(12), ALU.add)
    vv(T(11), T(11), T(9), ALU.mult)
    vv(T(12), T(8), lamr, ALU.mult)
    vv(T(13), T(10), lami, ALU.mult)
    vv(T(12), T(12), T(13), ALU.subtract)
    vv(T(12), T(12), T(9), ALU.mult)
    PW = P.sb("PW", [128, 9, 2, 16])
    P.memset("dve", PW[:, 0, 0, :], 1.0, ["PW"])
    P.memset("dve", PW[:, 0, 1, :], 0.0, ["PW"])

    def cmul(o_r, o_i, a_r, a_i, b_r, b_i, names):
        P.tt("dve", T(14), a_r, b_r, ALU.mult, names, [S])
        P.tt("dve", T(15), a_i, b_i, ALU.mult, names, [S])
        P.tt("dve", T(16), a_r, b_i, ALU.mult, names, [S])
        P.tt("dve", T(17), a_i, b_r, ALU.mult, names, [S])
        P.tt("dve", o_r, T(14), T(15), ALU.subtract, [S], names)
        P.tt("dve", o_i, T(16), T(17), ALU.add, [S], names)

    for k in range(1, 9):
        cmul(PW[:, k, 0, :], PW[:, k, 1, :], PW[:, k - 1, 0, :], PW[:, k - 1, 1, :], T(7), T(8), ["PW", S])
    P.cp("dve", DBL[:, 0, 0, :], PW[:, 8, 0, :], ["PW"], ["DBL"])
    P.cp("dve", DBL[:, 0, 1, :], PW[:, 8, 1, :], ["PW"], ["DBL"])
    for j in range(1, 9):
        cmul(DBL[:, j, 0, :], DBL[:, j, 1, :], DBL[:, j - 1, 0, :], DBL[:, j - 1, 1, :],
             DBL[:, j - 1, 0, :], DBL[:, j - 1, 1, :], ["DBL", S])
    for j in range(9):
        P.ts("dve", DBL[:, j, 2, :], DBL[:, j, 1, :], -1.0, ALU.mult, ["DBL"], ["DBL"])

    if stop_after == "B00":
        d1 = dscr("dbg_pw", [128, 9 * 2 * 16]); P.dma("sp", d1, PW[:].rearrange("p a b c -> p (a b c)"), r=["PW"], w=[], key="st_d1")
        d2 = dscr("dbg_dbl", [128, 9 * 3 * 16]); P.dma("sp", d2, DBL[:].rearrange("p a b c -> p (a b c)"), r=["DBL"], w=[], key="st_d2")
        P.build()
        return nc
    wt = P.sb("s5w", [128, 8, 16, 16])
    WN = "s5w"

    def Wt(i):
        return wt[:, i, :, :]

    def bc(ap16):
        return ap16.unsqueeze(2).to_broadcast([128, 16, 16])

    def cmulw(o_r, o_i, s_r, s_i, b_r, b_i, rn, wn, neg_i=False):
        P.tt("dve", Wt(4), bc(s_r), b_r, ALU.mult, rn, [WN])
        P.tt("dve", Wt(5), bc(s_i), b_i, ALU.mult, rn, [WN])
        P.tt("dve", Wt(6), bc(s_r), b_i, ALU.mult, rn, [WN])
        P.tt("dve", Wt(7), bc(s_i), b_r, ALU.mult, rn, [WN])
        P.tt("dve", o_r, Wt(4), Wt(5), ALU.subtract, [WN], wn)
        P.tt("dve", o_i, Wt(6), Wt(7), ALU.add, [WN], wn)

    br, bi_, cr, ci = s5bc[:, 0], s5bc[:, 1], s5bc[:, 2], s5bc[:, 3]
    cmulw(Wt(0), Wt(1), T(11), T(12), br, bi_, [S, "s5bc", WN], [WN])

    BB128 = P.sb("BB128", [128, 2, 16, 128], BF16)
    P.memset("pool", BB128[:], 0.0, ["BB128"])
    PS_BB = 2 * 16 * 128

    def diag_ap(tensor, pstride, base_off, half):
        off = half * 64 * pstride + base_off + 16 * half
        return bass.AP(tensor, off, [[pstride, 64], [512, 4], [160, 4], [1, 16]])

    for ri in range(2):
        for half in range(2):
            hs = slice(64 * half, 64 * half + 64)
            for q in range(16):
                co = 32 * (q % 4) + 16 * half
                P.cp("dve", BB128[hs, ri, q, co:co + 16], Wt(ri)[hs, q, :], [WN], ["BB128"])

    CPf = P.sb("CPf", [128, 8, 2, 16, 32], BF16)
    P.memset("pool", CPf[:], 0.0, ["CPf"])
    P.memset("pool", CPb[:], 0.0, ["CPb"])
    for k in range(9):
        cmulw(Wt(2), Wt(3), PW[:, k, 0, :], PW[:, k, 1, :], cr, ci, ["PW", "s5bc", WN], [WN])
        P.ts("dve", Wt(3), Wt(3), -1.0, ALU.mult, [WN], [WN])
        for ri in range(2):
            for half in range(2):
                hs = slice(64 * half, 64 * half + 64)
                if k < 8:
                    P.cp("dve", CPf[hs, k, ri, :, 16 * half:16 * half + 16], Wt(2 + ri)[hs], [WN], ["CPf"])
                if k >= 1:
                    P.cp("act", CPb[hs, k - 1, ri, :, 16 * half:16 * half + 16], Wt(2 + ri)[hs], [WN], ["CPb"])

    bdps2 = [P.ps("bdps%d" % i, [128, 128]) for i in range(2)]
    P.memset("pool", BD[:], 0.0, ["BD"])
    for c in range(4 if "a" not in SK else 0):
        for tau in range(8):
            bp = bdps2[tau % 2]
            bpn = "bdps%d" % (tau % 2)
            for q4 in range(4):
                q = 4 * c + q4
                for ri in range(2):
                    P.mm(bp[:], BB128[:, ri, q, :], CPf[:, tau, ri, 4 * c:4 * c + 4, :], q4 == 0 and ri == 0, q4 == 3 and ri == 1,
                         ["BB128", "CPf"], [bpn])
            for q4 in range(4):
                P.cp("act", BD[32 * q4:32 * q4 + 32, c, tau, 32 * q4:32 * q4 + 32],
                     bp[32 * q4:32 * q4 + 32, 32 * q4:32 * q4 + 32], [bpn], ["BD"])

    srcs = [P.sb("w1src%d" % i, [128, 16, 128], BF16) for i in range(2)]
    for i in range(2):
        P.memset("pool", srcs[i][:], 0.0, ["w1src%d" % i])
    trps = [P.ps("trps%d" % i, [128, 4, 128], BF16) for i in range(2)]
    ti = 0
    for k in range(8 if "b" not in SK else 0):
        cmulw(Wt(2), Wt(3), PW[:, k, 0, :], PW[:, k, 1, :], Wt(0), Wt(1), ["PW", WN], [WN])
        for ri in range(2):
            sname = "w1src%d" % ri
            for half in range(2):
                hs = slice(64 * half, 64 * half + 64)
                for q in range(16):
                    co = 32 * (q % 4) + 16 * half
                    P.cp("dve" if q % 2 else "pool", srcs[ri][hs, q, co:co + 16], Wt(2 + ri)[hs, q, :], [WN], [sname])
            for c in range(4):
                tp, tpn = trps[ti % 2], "trps%d" % (ti % 2)
                ti += 1
                for q4 in range(4):
                    P.tr(tp[:, q4, :], srcs[ri][:, 4 * c + q4, :], identb[:], [sname, "identb"], [tpn])
                P.cp("act" if c % 2 else "dve", W1[:, 4 * c:4 * c + 4, k, ri, :], tp[:], [tpn], ["W1"])
    if stop_after == "B0":
        d1 = dscr("dbg_pw", [128, 9 * 2 * 16]); P.dma("sp", d1, PW[:].rearrange("p a b c -> p (a b c)"), r=["PW"], w=[], key="st_d1")
        d2 = dscr("dbg_dbl", [128, 9 * 3 * 16]); P.dma("sp", d2, DBL[:].rearrange("p a b c -> p (a b c)"), r=["DBL"], w=[], key="st_d2")
        d3 = dscr("dbg_bd", [128, 4 * 8 * 128], BF16); P.dma("sp", d3, BD[:].rearrange("p a b c -> p (a b c)"), r=["BD"], w=[], key="st_d3")
        d4 = dscr("dbg_w1", [128, 16, 8 * 2 * 128], BF16)
        for qq in range(16):
            P.dma("sp", d4[:, qq, :], W1[:, qq].rearrange("p b c d -> p (b c d)"), r=["W1"], w=[], key="st_d4")
        d5 = dscr("dbg_cpb", [128, 8 * 2 * 16 * 32], BF16); P.dma("sp", d5, CPb[:].rearrange("p a b c d -> p (a b c d)"), r=["CPb"], w=[], key="st_d5")
        P.build()
        return nc
    P.pop()

    ygb = P.sb("ygb", [128, 4, L], BF16)
    Xa = [[P.sb("Xa%d%d" % (b, ri), [128, 512]) for ri in range(2)] for b in range(2)]
    Xpr = [[P.sb("Xpr%d%d" % (q4, ri), [128, 512], BF16) for ri in range(2)] for q4 in range(4)]
    slp = [P.ps("slp%d" % ri, [128, 512]) for ri in range(2)]
    yps = [P.ps("yps%d" % i, [128, 512]) for i in range(2)]
    ypre = [P.sb("ypre%d" % i, [128, 512]) for i in range(2)]
    gtmp = [P.sb("gtmp%d" % i, [128, 512]) for i in range(2)]
    for q4 in range(4):
        for ri in range(2):
            P.memset("pool", Xpr[q4][ri][:, 0:1], 0.0, ["Xpr%d%d" % (q4, ri)])
    for c in range(4):
        Ust = U[:, c, :].rearrange("p (j s) -> p s j", s=8)
        for q4 in range(4):
            q = 4 * c + q4
            for ri in range(2):
                for s8 in range(8):
                    P.mm(slp[ri][:], W1[:, q, 7 - s8, ri, :], Ust[:, s8, :], s8 == 0, s8 == 7, ["W1", "U"], ["slp%d" % ri],
                         drain=(q4 == 0 and ri == 0 and s8 == 0))
                P.cp("act", Xa[0][ri][:], slp[ri][:], ["slp%d" % ri], ["Xa0%d" % ri])
            cur = 0
            for j in range(9):
                d = 1 << j
                nx = 1 - cur
                sr, si = Xa[cur][0], Xa[cur][1]
                dr, di = Xa[nx][0], Xa[nx][1]
                nsr, nsi, ndr, ndi = "Xa%d0" % cur, "Xa%d1" % cur, "Xa%d0" % nx, "Xa%d1" % nx
                ar_, ai_, nai_ = DBL[:, j, 0, q:q + 1], DBL[:, j, 1, q:q + 1], DBL[:, j, 2, q:q + 1]
                P.cp("pool", dr[:, 0:d], sr[:, 0:d], [nsr], [ndr])
                P.cp("pool", di[:, 0:d], si[:, 0:d], [nsi], [ndi])
                P.stt(dr[:, d:], sr[:, 0:512 - d], ar_, sr[:, d:], ALU.mult, ALU.add, [nsr, "DBL"], [ndr])
                P.stt(di[:, d:], si[:, 0:512 - d], ar_, si[:, d:], ALU.mult, ALU.add, [nsi, "DBL"], [ndi])
                P.stt(dr[:, d:], si[:, 0:512 - d], nai_, dr[:, d:], ALU.mult, ALU.add, [nsi, ndr, "DBL"], [ndr])
                P.stt(di[:, d:], sr[:, 0:512 - d], ai_, di[:, d:], ALU.mult, ALU.add, [nsr, ndi, "DBL"], [ndi])
                cur = nx
            for ri in range(2):
                P.cp("act", Xpr[q4][ri][:, 1:512], Xa[cur][ri][:, 0:511], ["Xa%d%d" % (cur, ri)], ["Xpr%d%d" % (q4, ri)])
        for t8 in range(8):
            yp, ypn = yps[t8 % 2], "yps%d" % (t8 % 2)
            for s8 in range(t8 + 1):
                P.mm(yp[:], BD[:, c, t8 - s8, :], Ust[:, s8, :], s8 == 0, False, ["BD", "U"], [ypn], drain=(s8 == 0))
            for q4 in range(4):
                q = 4 * c + q4
                for ri in range(2):
                    last = (q4 == 3 and ri == 1)
                    P.mm(yp[32 * q4:32 * q4 + 32, :], CPb[:, t8, ri, q, :], Xpr[q4][ri][:], False, last,
                         ["CPb", "Xpr%d%d" % (q4, ri)], [ypn], drain=(q4 == 0 and ri == 0), tile_position=(0, 32 * q4))
            ye, yen = ypre[t8 % 2], "ypre%d" % (t8 % 2)
            P.stt(ye[:], Ust[:, t8, :], pvs("ssmd", c, c + 1), yp[:], ALU.mult, ALU.add, ["U", "pv", ypn], [yen])
            gt, gtn = gtmp[t8 % 2], "gtmp%d" % (t8 % 2)
            P.act(gt[:], ye[:], AF.Square, [yen], [gtn])
            P.ts("dve", gt[:], gt[:], 0.044715, ALU.mult, [gtn], [gtn], s2=1.0, op1=ALU.add)
            P.tt("dve", gt[:], gt[:], ye[:], ALU.mult, [gtn, yen], [gtn])
            P.act(gt[:], gt[:], AF.Sigmoid, [gtn], [gtn], scale=1.5957691216057308)
            P.tt("dve", ygb[:, c, :].rearrange("p (j s) -> p s j", s=8)[:, t8, :], ye[:], gt[:], ALU.mult, [yen, gtn], ["ygb%d" % c])
    zps = [P.ps("zps%d" % i, [128, 512]) for i in range(2)]
    sgs = [P.sb("sg%d" % i, [128, 512]) for i in range(2)]
    obs = [P.sb("obB%d" % i, [128, 512], BF16) for i in range(2)]
    it = 0
    for tt in range(8):
        sl = slice(tt * 512, (tt + 1) * 512)
        for oc in range(4):
            zp, zn = zps[it % 2], "zps%d" % (it % 2)
            sg, sn = sgs[it % 2], "sg%d" % (it % 2)
            ob, on = obs[it % 2], "obB%d" % (it % 2)
            it += 1
            for kc in range(4):
                P.mm(zp[:], gluw[:, kc, oc * 128:(oc + 1) * 128], ygb[:, kc, sl], kc == 0, kc == 3,
                     ["gluw"] + ["ygb%d" % kk for kk in range(4)], [zn], drain=(tt == 0 and oc == 0 and kc == 0))
            P.act(sg[:], zp[:], AF.Sigmoid, [zn, "pv"], [sn], bias=pvs("glub", oc, oc + 1))
            P.tt("dve", ob[:], ygb[:, oc, sl], sg[:], ALU.mult, ["ygb%d" % oc, sn], [on])
            P.dma("sp", ycat_d[oc * 128:(oc + 1) * 128, sl], ob[:], r=[on], w=["d_y_%d_%d" % (oc, tt)], key="st_" + on)
    P.pop()
    if stop_after == "B":
        P.build()
        return nc

    P.push()
    qa2 = [[P.sb("qa%d%d" % (p_, i), [80, L], BF16) for i in range(2)] for p_ in range(2)]
    ka2 = [[P.sb("ka%d%d" % (p_, i), [80, L], BF16) for i in range(2)] for p_ in range(2)]
    vg2 = [P.sb("vg%d" % p_, [128, 32, 192], BF16) for p_ in range(2)]
    bt2 = [[P.sb("bt%d%d" % (p_, i), [128, 4, 256]) for i in range(2)] for p_ in range(2)]
    maskc = P.sb("maskc", [128, 2, 256])
    P.dma("sp", maskc[:], maskc_d, w=["maskc"])
    maskv = P.sb("maskv", [128, 32, 16])
    P.dma("sp", maskv[:], maskv_d, w=["maskv"])
    notown = P.sb("notown", [128, 32, 16])
    P.dma("sp", notown[:], notown_d, w=["notown"])
    kmb = [P.sb("kmb%d" % i, [80, 16], BF16) for i in range(2)]
    kmf = P.sb("kmf", [64, 16])
    Mext = P.sb("Mext", [128, 32, 80], BF16)
    gm = P.sb("gm", [128, 32, 16])
    msel = P.sb("msel", [128, 32, 16])
    t8a = P.sb("t8a", [128, 32, 8])
    P.memset("pool", Mext[:], 0.0, ["Mext"])
    for i in range(2):
        P.memset("pool", kmb[i][:], 0.0, ["kmb%d" % i])
    NS = 3
    gbank = P.ps("gbank", [128, 32, 16])
    sps = [P.ps("sps%d" % i, [128, 2, 256]) for i in range(NS)]
    OA = [P.ps("OA%d" % i, [128, 256]) for i in range(2)]
    OB = [P.ps("OB%d" % i, [128, 256]) for i in range(2)]
    tpsv = P.last_ps_raw[0:80, :].bitcast(BF16).rearrange("p (a b) -> p a b", a=8)
    sbb = [P.sb("sbb%d" % i, [128, 2, 256]) for i in range(3)]
    pTs = [P.sb("pT%d" % i, [128, 2, 256], BF16) for i in range(4)]
    Dt = P.sb("Dt", [128, 256])
    oba = [P.sb("oba%d" % i, [128, 256], BF16) for i in range(2)]
    cnt = {"p": 0, "o": 0, "b": 0}

    def att_loads(hp):
        pr = hp % 2
        P.dma("sp", vg2[pr][:], vaug_d[:, hp * 192:(hp + 1) * 192].rearrange("(kc p) c -> p kc c", p=128), w=["vg%d" % pr])
        for i in range(2):
            h = 2 * hp + i
            P.dma("sp", qa2[pr][i][0:64, :], qT_d[64 * h:64 * h + 64, :], w=["qa%d%d_q" % (pr, i)], key="ld_qa%d%d" % (pr, i))
            P.dma("sp", ka2[pr][i][0:64, :], kT_d[64 * h:64 * h + 64, :], w=["ka%d%d" % (pr, i)], key="ld_ka%d%d" % (pr, i))
            for hh in range(2):
                P.dma("pool", ka2[pr][i][64:80, hh * 2048:(hh + 1) * 2048], onehot_d[:, hh * 2048:(hh + 1) * 2048],
                      w=["ka%d%d" % (pr, i)], key="ld_ka%d%d" % (pr, i))
            P.dma("sp", bt2[pr][i][:], biasg_d[h], w=["bt%d%d" % (pr, i)])

    att_loads(0)
    for hp in range(4):
        pr = hp % 2
        qa, ka, vg, bt = qa2[pr], ka2[pr], vg2[pr], bt2[pr]
        nq = lambda i, pr=pr: "qa%d%d_q" % (pr, i)
        nm = lambda i, pr=pr: "qa%d%d_m" % (pr, i)
        nka = lambda i, pr=pr: "ka%d%d" % (pr, i)
        nbt = lambda i, pr=pr: "bt%d%d" % (pr, i)
        nvg = "vg%d" % pr
        if hp + 1 < 4:
            att_loads(hp + 1)
        for i in range(2):
            h = 2 * hp + i
            P.memset("pool", qa[i][64:80, :], 0.0, [nm(i)])
            P.ts("dve", bt[i][:], bt[i][:], pvs("b31", h, h + 1), ALU.subtract, [nbt(i), "pv"], [nbt(i)])
            P.tt("dve", bt[i][:, 0:2, :], bt[i][:, 0:2, :], maskc[:], ALU.add, [nbt(i), "maskc"], [nbt(i)])
            P.op("dve", lambda e, i=i, ka=ka: e.tensor_reduce(out=kmf[:], in_=ka[i][0:64, :].rearrange("p (j t) -> p j t", t=256),
                                                            axis=AX.X, op=ALU.add), [nka(i)], ["kmf"])
            P.ts("dve", kmb[i][0:64, :], kmf[:], 1.0 / 256, ALU.mult, ["kmf"], ["kmb%d" % i])
            for qt in range(32):
                P.mm(gbank[:, qt, :], qa[i][0:80, qt * 128:(qt + 1) * 128], kmb[i][:], True, True,
                     [nq(i), nm(i), "kmb%d" % i], ["gbank"])
            P.tt("dve", gm[:], gbank[:], maskv[:], ALU.add, ["gbank", "maskv"], ["gm"])
            for qt in range(32):
                P.op("dve", lambda e, qt=qt: e.max(out=t8a[:, qt, :], in_=gm[:, qt, :]), ["gm"], ["t8a"])
            P.tt("dve", msel[:], gm[:], t8a[:, :, 2:3].to_broadcast([128, 32, 16]), ALU.is_ge, ["gm", "t8a"], ["msel"])
            P.stt(Mext[:, :, 64:80], msel[:], -1.0, notown[:], ALU.add, ALU.mult, ["msel", "notown"], ["Mext"])
            for g4 in range(4):
                for t in range(8):
                    P.tr(tpsv[:, t, :], Mext[:, 8 * g4 + t, :], identb[:], ["Mext", "identb"], ["OB1"])
                P.cp("act", qa[i][64:80, g4 * 1024:(g4 + 1) * 1024].rearrange("p (a b) -> p a b", a=8), tpsv[64:80, :, :],
                     ["OB1"], [nm(i)])
        tasks = [(n, i, j) for n in range(16) for i in range(2) for j in range(n + 1)]
        LOOK = NS - 1

        def emit_qk(t):
            n, i, j = tasks[t]
            s = t % NS
            for cc in range(2):
                kc = 2 * j + cc
                P.mm(sps[s][:, cc, :], ka[i][0:80, kc * 128:(kc + 1) * 128], qa[i][0:80, n * 256:(n + 1) * 256], True, True,
                     [nka(i), nq(i), nm(i)], ["sps%d" % s])

        def emit_rest(t):
            n, i, j = tasks[t]
            s = t % NS
            sp_, spn = sps[s][:], "sps%d" % s
            pi_ = cnt["p"] % 4
            cnt["p"] += 1
            pT, pTn = pTs[pi_], "pT%d" % pi_
            if j >= n - 1:
                base = 0 if j == n else 2
                bi_ = cnt["b"] % 3
                cnt["b"] += 1
                sb_, sbn = sbb[bi_], "sbb%d" % bi_
                P.tt("dve", sb_[:], sp_, bt[i][:, base:base + 2, :], ALU.add, [spn, nbt(i)], [sbn])
                P.act(pT[:], sb_[:], AF.Exp, [sbn], [pTn])
            else:
                P.act(pT[:], sp_, AF.Exp, [spn], [pTn])
            for cc in range(2):
                kc = 2 * j + cc
                first = (j == 0 and cc == 0)
                last = (j == n and cc == 1)
                P.mm(OA[i][:], vg[:, kc, 0:128], pT[:, cc, :], first, last, [nvg, pTn], ["OA%d" % i])
                P.mm(OB[i][:], vg[:, kc, 64:192], pT[:, cc, :], first, last, [nvg, pTn], ["OB%d" % i])
            if i == 1 and j == n:
                qs = slice(n * 256, (n + 1) * 256)
                P.cp("act", Dt[0:64, :], OB[0][0:64, :], ["OB0"], ["Dt"])
                P.cp("act", Dt[64:128, :], OA[1][64:128, :], ["OA1"], ["Dt"])
                P.op("dve", lambda e: e.reciprocal(out=Dt[:], in_=Dt[:]), ["Dt"], ["Dt"])
                o = cnt["o"] % 2
                cnt["o"] += 1
                P.tt("dve", oba[o][0:64, :], OA[0][0:64, :], Dt[0:64, :], ALU.mult, ["OA0", "Dt"], ["oba%d" % o])
                P.tt("dve", oba[o][64:128, :], OB[1][64:128, :], Dt[64:128, :], ALU.mult, ["OB1", "Dt"], ["oba%d" % o])
                P.dma("sp", ycat_d[512 + hp * 128:512 + (hp + 1) * 128, qs], oba[o][:], r=["oba%d" % o],
                      w=[], key="st_oba%d" % o)

        for t in range(len(tasks) + LOOK):
            if t < len(tasks):
                emit_qk(t)
            if t - LOOK >= 0:
                emit_rest(t - LOOK)
    P.pop()
    if stop_after == "C":
        P.build()
        return nc

    def ycat_reads(tt):
        r = ["d_y_%d_%d" % (oc, tt) for oc in range(4)]
        r += ["d_ya_%d_%d" % (hp, n) for hp in range(4) for n in (2 * tt, 2 * tt + 1)]
        return r

    P.push()
    wout = P.sb("wout", [128, 8, D], BF16)
    load_w_bf(wout, "wout", w_out0, D)
    xts = [P.sb("xtD%d" % i, [128, 8, 512]) for i in range(2)]
    ycs = [P.sb("yc%d" % i, [128, 8, 512], BF16) for i in range(2)]
    xos = [P.sb("xo%d" % i, [128, 8, 512]) for i in range(2)]
    pjs = [P.ps("pjD%d" % i, [128, 512]) for i in range(3)]
    for tt in range(8):
        sl = slice(tt * 512, (tt + 1) * 512)
        b = tt % 2
        if tt == 0:
            P.dma("sp", xts[b][:], xview(xT)[:, :, sl], w=["xtD%d" % b])
            P.dma("sp", ycs[b][:], xview(ycat_d)[:, :, sl], w=["yc%d" % b])
        if tt + 1 < 8:
            sl2 = slice((tt + 1) * 512, (tt + 2) * 512)
            P.dma("sp", xts[1 - b][:], xview(xT)[:, :, sl2], w=["xtD%d" % (1 - b)])
            P.dma("sp", ycs[1 - b][:], xview(ycat_d)[:, :, sl2], w=["yc%d" % (1 - b)])
        for oc in range(8):
            pj, pn = pjs[oc % 3], "pjD%d" % (oc % 3)
            for kc in range(8):
                P.mm(pj[:], wout[:, kc, oc * 128:(oc + 1) * 128], ycs[b][:, kc, :], kc == 0, kc == 7, ["wout", "yc%d" % b], [pn])
            P.stt(xos[b][:, oc, :], pj[:], G_(0, 0)[:, oc:oc + 1], xts[b][:, oc, :], ALU.mult, ALU.add,
                  [pn, "modT", "xtD%d" % b], ["xo%d" % b])
        store_x(xa_d, xos[b], "xo%d" % b, sl, b)
    P.pop()
    if stop_after == "D":
        P.build()
        return nc

    def ffn_phase(l, xin_d, xout_d, final=False):
        P.push()
        wg = P.sb("wg", [128, 8, 2816], BF16)
        wu = P.sb("wu", [128, 8, 2816], BF16)
        wd = P.sb("wd", [128, 22, D], BF16)
        load_w_bf(wg, "wg", ffn_gate[l], 2816)
        load_w_bf(wu, "wu", ffn_up[l], 2816)
        load_w_bf(wd, "wd", ffn_down[l], D)
        N = 256
        NT = L // N
        xts = [P.sb("xtF%d" % i, [128, 8, N]) for i in range(3)]
        hts = [P.sb("htF%d" % i, [128, 8, N], BF16) for i in range(2)]
        hids = [P.sb("hid%d" % i, [128, 22, N], BF16) for i in range(2)]
        xsq = P.sb("xsq", [128, 8, N], BF16)
        rstd = P.sb("rstd", [128, N])
        tmps = [P.sb("tmpn%d" % i, [128, N]) for i in range(2)]
        halo = P.sb("halo", [128, 22, 2])
        P.memset("pool", halo[:], 0.0, ["halo%d" % j for j in range(22)])
        gbs = [P.sb("gb%d" % i, [128, N + 2]) for i in range(2)]
        accs = [P.sb("acc%d" % i, [128, N]) for i in range(2)]
        ssps = P.ps("ssps", [128, 512])
        gpss = [P.ps("gpsF%d" % i, [128, N]) for i in range(2)]
        upss = [P.ps("upsF%d" % i, [128, N]) for i in range(2)]
        dpss = [P.ps("dpsF%d" % i, [128, N]) for i in range(2)]
        dww = pvs("ffdww").rearrange("p (l j k) -> p l j k", l=2, j=22)
        dwb = pvs("ffdwb").rearrange("p (l j) -> p l j", l=2)
        cnt = {"it": 0}

        def emit_load(t):
            P.dma("sp", xts[t % 3][:], xview(xin_d)[:, :, t * N:(t + 1) * N], w=["xtF%d" % (t % 3)])

        def emit_norm(t):
            rmsnorm(xts[t % 3], "xtF%d" % (t % 3), N, A_(l, 1), B_(l, 1), hts[t % 2], "htF%d" % (t % 2), xsq, ssps, rstd, tmps)

        def emit_gu(t, j):
            ht, hn = hts[t % 2], "htF%d" % (t % 2)
            hid, hidn = hids[t % 2], "hid%d" % (t % 2)
            it = cnt["it"]
            cnt["it"] += 1
            gb, gbn = gbs[it % 2], "gb%d" % (it % 2)
            acc, acn = accs[it % 2], "acc%d" % (it % 2)
            gp, gpn = gpss[it % 2], "gpsF%d" % (it % 2)
            up, upn = upss[it % 2], "upsF%d" % (it % 2)
            hl = "halo%d" % j
            for kc in range(8):
                P.mm(gp[:], wg[:, kc, j * 128:(j + 1) * 128], ht[:, kc, :], kc == 0, kc == 7, ["wg", hn], [gpn])
            for kc in range(8):
                P.mm(up[:], wu[:, kc, j * 128:(j + 1) * 128], ht[:, kc, :], kc == 0, kc == 7, ["wu", hn], [upn])
            P.cp("act", gb[:, 2:N + 2], gp[:], [gpn], [gbn])
            P.cp("pool", gb[:, 0:2], halo[:, j, :], [hl], [gbn])
            P.act(acc[:], gp[:], AF.Identity, [gpn, "pv"], [acn], scale=dww[:, l, j, 2:3], bias=dwb[:, l, j:j + 1])
            P.stt(acc[:], gb[:, 1:N + 1], dww[:, l, j, 1:2], acc[:], ALU.mult, ALU.add, [gbn, "pv", acn], [acn])
            P.stt(acc[:], gb[:, 0:N], dww[:, l, j, 0:1], acc[:], ALU.mult, ALU.add, [gbn, "pv", acn], [acn])
            P.cp("pool", halo[:, j, :], gb[:, N:N + 2], [gbn], [hl])
            P.act(acc[:], acc[:], AF.Silu, [acn], [acn])
            P.tt("dve", hid[:, j, :], acc[:], up[:], ALU.mult, [acn, upn], [hidn])

        def emit_down(t):
            xt, xn = xts[t % 3], "xtF%d" % (t % 3)
            hid, hidn = hids[t % 2], "hid%d" % (t % 2)
            for oc in range(8):
                dp, dpn = dpss[oc % 2], "dpsF%d" % (oc % 2)
                for j in range(22):
                    P.mm(dp[:], wd[:, j, oc * 128:(oc + 1) * 128], hid[:, j, :], j == 0, j == 21, ["wd", hidn], [dpn])
                P.stt(xt[:, oc, :], dp[:], G_(l, 1)[:, oc:oc + 1], xt[:, oc, :], ALU.mult, ALU.add,
                      [dpn, "modT", xn], [xn])
            if final:
                rmsnorm(xt, xn, N, pvs("finalg"), None, xt, xn, xsq, ssps, rstd, None, final=True)
            store_x(xout_d, xt, xn, slice(t * N, (t + 1) * N), t % 2)

        emit_load(0)
        emit_load(1)
        emit_norm(0)
        for t in range(NT):
            for j in range(22):
                emit_gu(t, j)
                if j == 2 and t >= 1:
                    emit_down(t - 1)
                if j == 8 and t + 2 < NT:
                    emit_load(t + 2)
                if j == 10 and t + 1 < NT:
                    emit_norm(t + 1)
        emit_down(NT - 1)
        P.pop()

    ffn_phase(0, xa_d, xb_d)
    if stop_after == "E":
        P.build()
        return nc

    P.push()
    cwin = P.sb("cwin", [128, 8, 2048], BF16)
    cwout = P.sb("cwout", [128, 8, D], BF16)
    load_w_bf(cwin, "cwin", cm_w_in, 2048)
    load_w_bf(cwout, "cwout", cm_w_out, D)
    dg = P.sb("dg", [128, 8, 31, 128], BF16)
    dwv = pvs("cmdww").rearrange("p (c k) -> p c k", c=8)
    for oc in range(8):
        P.tt("pool" if oc % 2 else "dve", dg[:, oc, :, :], identf[:].unsqueeze(1).to_broadcast([128, 31, 128]),
             dwv[:, oc, :].unsqueeze(2).to_broadcast([128, 31, 128]), ALU.mult, ["identf", "pv"], ["dg%d" % oc])
    N = 256
    NT = L // N
    xts = [P.sb("xtG%d" % i, [128, 8, N]) for i in range(3)]
    ht = P.sb("htG", [128, 8, N], BF16)
    xsq = P.sb("xsq", [128, 8, N], BF16)
    rstd = P.sb("rstd", [128, N])
    rstd2 = P.sb("rstd2", [128, N])
    tmps = [P.sb("tmpn%d" % i, [128, N]) for i in range(3)]
    ags = [P.sb("ag%d" % i, [128, 8, N + 30], BF16) for i in range(2)]
    P.memset("pool", ags[1][:, :, N:N + 30], 0.0, ["ag1_%d" % oc for oc in range(8)])
    cvt = P.sb("cvt", [128, 8, N])
    cvb = P.sb("cvb", [128, 8, N], BF16)
    cvq = P.sb("cvq", [128, 8, N], BF16)
    mu = P.sb("mu", [128, N])
    msq = P.sb("msq", [128, N])
    lnb = P.sb("lnb", [128, 8, N], BF16)
    sgs = [P.sb("sgG%d" % i, [128, N]) for i in range(4)]
    ssps = P.ps("ssps", [128, N])
    a1p = [P.ps("a1p%d" % i, [128, N]) for i in range(2)]
    a2p = [P.ps("a2p%d" % i, [128, N]) for i in range(2)]
    cps = [P.ps("cps%d" % i, [128, N]) for i in range(2)]
    sqps = P.ps("sqps", [128, N])
    bin_ = pvs("cmbin")
    hbin = P.sb("hbin", [128, 8])
    P.ts("dve", hbin[:], bin_[:, 8:16], 0.5, ALU.mult, ["pv"], ["hb"])

    def c_load(t):
        P.dma("sp", xts[t % 3][:], xview(xb_d)[:, :, t * N:(t + 1) * N], w=["xtG%d" % (t % 3)])

    def c_norm(t):
        rmsnorm(xts[t % 3], "xtG%d" % (t % 3), N, A_(1, 0), B_(1, 0), ht, "htG", xsq, ssps, rstd, tmps)

    def c_halo(t):
        ag, pag = ags[t % 2], ags[(t + 1) % 2]
        P.cp("pool", ag[:, :, 0:30], pag[:, :, N:N + 30], ["ag%d_%d" % ((t + 1) % 2, oc) for oc in range(8)],
             ["ag%d_%d" % (t % 2, oc) for oc in range(8)])

    def c_s1(t, oc):
        ag, agn = ags[t % 2], "ag%d_%d" % (t % 2, oc)
        p1, p1n = a1p[oc % 2], "a1p%d" % (oc % 2)
        p2, p2n = a2p[oc % 2], "a2p%d" % (oc % 2)
        sg, sgn = sgs[oc % 4], "sgG%d" % (oc % 4)
        for kc in range(8):
            P.mm(p1[:], cwin[:, kc, oc * 128:(oc + 1) * 128], ht[:, kc, :], kc == 0, kc == 7, ["cwin", "htG"], [p1n])
        for kc in range(8):
            P.mm(p2[:], cwin[:, kc, 1024 + oc * 128:1024 + (oc + 1) * 128], ht[:, kc, :], kc == 0, kc == 7, ["cwin", "htG"], [p2n])
        P.act(sg[:], p2[:], AF.Tanh, [p2n, "hb"], [sgn], scale=0.5, bias=hbin[:, oc:oc + 1])
        P.ts("pool", sg[:], sg[:], 0.5, ALU.mult, [sgn], [sgn], s2=0.5, op1=ALU.add)
        P.stt(ag[:, oc, 30:N + 30], p1[:], bin_[:, oc:oc + 1], sg[:], ALU.add, ALU.mult, [p1n, "pv", sgn], [agn])

    def c_s2(t):
        ag = ags[t % 2]
        for oc in range(8):
            agn = "ag%d_%d" % (t % 2, oc)
            cp_, cpn = cps[oc % 2], "cps%d" % (oc % 2)
            for k in range(31):
                P.mm(cp_[:], dg[:, oc, k, :], ag[:, oc, k:k + N], k == 0, k == 30, ["dg%d" % oc, agn], [cpn])
            P.act(cvt[:, oc, :], cp_[:], AF.Identity, [cpn, "pv"], ["cvt%d" % oc], bias=pvs("cmdwb", oc, oc + 1))
            P.cp("dve", cvb[:, oc, :], cvt[:, oc, :], ["cvt%d" % oc], ["cvb"])
            P.act(cvq[:, oc, :], cvt[:, oc, :], AF.Square, ["cvt%d" % oc], ["cvq"])

    def c_stats(t):
        for oc in range(8):
            P.mm(ssps[:], ones_bf[:], cvb[:, oc, :], oc == 0, oc == 7, ["ones_bf", "cvb"], ["ssps"])
        for oc in range(8):
            P.mm(sqps[:], ones_bf[:], cvq[:, oc, :], oc == 0, oc == 7, ["ones_bf", "cvq"], ["sqps"])
        P.act(mu[:], ssps[:], AF.Identity, ["ssps"], ["mu"], scale=1.0 / D)
        P.tt("dve", msq[:], mu[:], mu[:], ALU.mult, ["mu"], ["msq"])
        P.stt(msq[:], sqps[:], 1.0 / D, msq[:], ALU.mult, ALU.subtract, ["sqps", "msq"], ["msq"])
        P.act(rstd2[:], msq[:], AF.Ln, ["msq", "epsb"], ["rstd2"], bias=epsb[:, 0:1])
        P.act(rstd2[:], rstd2[:], AF.Exp, ["rstd2"], ["rstd2"], scale=-0.5)

    def c_normalize(t, oc):
        cn = "cvt%d" % oc
        P.tt("dve", cvt[:, oc, :], cvt[:, oc, :], mu[:], ALU.subtract, [cn, "mu"], [cn])
        P.tt("dve", cvt[:, oc, :], cvt[:, oc, :], rstd2[:], ALU.mult, [cn, "rstd2"], [cn])
        P.act(lnb[:, oc, :], cvt[:, oc, :], AF.Silu, [cn, "pv"], ["lnb"],
              scale=pvs("cmlng", oc, oc + 1), bias=pvs("cmlnb", oc, oc + 1))

    def c_s4(t):
        xt, xn = xts[t % 3], "xtG%d" % (t % 3)
        for oc in range(8):
            p1, p1n = a1p[oc % 2], "a1p%d" % (oc % 2)
            sg, sgn = sgs[oc % 2], "sgG%d" % (oc % 2)
            for kc in range(8):
                P.mm(p1[:], cwout[:, kc, oc * 128:(oc + 1) * 128], lnb[:, kc, :], kc == 0, kc == 7, ["cwout", "lnb"], [p1n])
            P.act(sg[:], p1[:], AF.Identity, [p1n, "pv"], [sgn], bias=pvs("cmbout", oc, oc + 1))
            P.stt(xt[:, oc, :], sg[:], G_(1, 0)[:, oc:oc + 1], xt[:, oc, :], ALU.mult, ALU.add,
                  [sgn, "modT", xn], [xn])
        store_x(xa_d, xt, xn, slice(t * N, (t + 1) * N), t % 2)

    c_load(0)
    c_load(1)
    c_norm(0)
    c_halo(0)
    for oc in range(8):
        c_s1(0, oc)
    for t in range(NT):
        if t + 1 < NT:
            c_norm(t + 1)
        c_s2(t)
        c_stats(t)
        if t + 1 < NT:
            c_halo(t + 1)
        for oc in range(8):
            c_normalize(t, oc)
            if t + 1 < NT:
                c_s1(t + 1, oc)
        c_s4(t)
        if t + 2 < NT:
            c_load(t + 2)
    P.pop()
    if stop_after == "F":
        P.build()
        return nc

    ffn_phase(1, xa_d, outT, final=True)
    P.build()
    return nc


def _rel_bucket(dist):
    n = np.maximum(dist, 0)
    max_exact = 16
    nf = np.maximum(n, 1).astype(np.float32)
    large = max_exact + (np.log(nf / max_exact) / math.log(128 / max_exact) * (32 - max_exact)).astype(np.int32)
    large = np.minimum(large, 31)
    return np.where(n < max_exact, n, large)


def _t128(v):
    v = np.asarray(v, np.float32)
    return np.ascontiguousarray(v.reshape(-1, 128).T)


def prepare_shared(inp):
    f = lambda a: np.ascontiguousarray(np.asarray(a, np.float32))
    sh = {}
    sh["ident"] = np.eye(128, dtype=np.float32)
    sh["mod_w"] = f(inp["mod_w"])
    sh["w_in0"] = f(inp["ab_w_in"][0])
    sh["glu_w"] = f(inp["ssm_glu_w"][0])
    sh["w_out0"] = f(inp["ab_w_out"][0])
    sh["cm_w_in"] = f(inp["cm_w_in"][0])
    sh["cm_w_out"] = f(inp["cm_w_out"][0])
    sh["ffn_up"] = f(inp["ffn_w_up"])
    sh["ffn_gate"] = f(inp["ffn_w_gate"])
    sh["ffn_down"] = f(inp["ffn_w_down"])

    def gq(a):
        a = np.asarray(a, np.float32)
        rest = a.shape[2:]
        a = a.reshape((16, 2, 64) + rest)
        a = np.moveaxis(a, 0, 2)
        return np.ascontiguousarray(a.reshape((128, 16) + rest))

    lamr = gq(inp["ssm_a_re"][0])
    lami = gq(inp["ssm_a_im"][0])
    logdt = gq(np.repeat(np.asarray(inp["ssm_log_dt"][0], np.float32)[:, None], 64, axis=1))
    sh["s5p"] = np.ascontiguousarray(np.concatenate([lamr, lami, logdt], axis=1))
    br = gq(inp["ssm_b_re"][0])
    bi = gq(inp["ssm_b_im"][0])
    cr = gq(np.transpose(np.asarray(inp["ssm_c_re"][0], np.float32), (0, 2, 1)))
    ci = gq(np.transpose(np.asarray(inp["ssm_c_im"][0], np.float32), (0, 2, 1)))
    sh["s5bc"] = np.ascontiguousarray(np.stack([br, bi, cr, ci], axis=1))
    rb = np.asarray(inp["rel_bias"], np.float32)
    kp = np.arange(128)[:, None]
    qp = np.arange(256)[None, :]
    tabs = []
    for t in range(4):
        if t < 2:
            dist = qp - (128 * t + kp)
        else:
            dist = qp + 256 - (128 * (t - 2) + kp)
        tabs.append(_rel_bucket(dist))
    idx = np.stack(tabs, axis=1)
    sh["biasg"] = np.ascontiguousarray(np.transpose(rb[idx], (3, 0, 1, 2)))
    mc = np.zeros((128, 2, 256), np.float32)
    for t in range(2):
        mc[:, t, :] = np.where(qp - (128 * t + kp) >= 0, 0.0, NEG)
    sh["maskc"] = mc
    mv = np.zeros((128, 32, 16), np.float32)
    no = np.ones((128, 32, 16), np.float32)
    for qt in range(32):
        mv[:, qt, qt // 2:] = -1e30
        no[:, qt, qt // 2] = 0.0
    sh["maskv"] = mv
    sh["notown"] = no
    oh = np.zeros((16, L), np.float32)
    for j in range(16):
        oh[j, 256 * j:256 * (j + 1)] = -NEG
    sh["onehot"] = oh
    return sh


def prepare_pvec(inp, b):
    pvv = np.zeros((128, NPV), np.float32)

    def put(name, arr):
        o, w = PV[name]
        arr = np.asarray(arr, np.float32)
        assert arr.shape == (128, w), (name, arr.shape, w)
        pvv[:, o:o + w] = arr

    put("c", _t128(inp["c"][b]))
    put("modb", np.concatenate([_t128(inp["mod_b"][l]) for l in range(2)], axis=1))
    put("normg", np.concatenate([_t128(inp["norm_g"][l, s]) for l in range(2) for s in range(2)], axis=1))
    put("finalg", _t128(inp["final_g"]))
    put("glub", _t128(inp["ssm_glu_b"][0]))
    put("cmbin", _t128(inp["cm_b_in"][0]))
    dw = np.asarray(inp["cm_dw_w"][0], np.float32)
    put("cmdww", np.transpose(dw.reshape(31, 8, 128), (2, 1, 0)).reshape(128, 248))
    put("cmdwb", _t128(inp["cm_dw_b"][0]))
    put("cmlng", _t128(inp["cm_ln_g"][0]))
    put("cmlnb", _t128(inp["cm_ln_b"][0]))
    put("cmbout", _t128(inp["cm_b_out"][0]))
    fw = np.asarray(inp["ffn_dw_w"], np.float32)
    put("ffdww", np.transpose(fw.reshape(2, 3, 22, 128), (3, 0, 2, 1)).reshape(128, 132))
    put("ffdwb", np.concatenate([_t128(inp["ffn_dw_b"][l]) for l in range(2)], axis=1))
    put("ssmd", _t128(np.asarray(inp["ssm_d"][0], np.float32).reshape(-1)))
    put("b31", np.repeat(np.asarray(inp["rel_bias"], np.float32)[31][None, :], 128, axis=0))
    return pvv


_CACHE = {}


def kernel(**inputs):
    x = np.asarray(inputs["x"], np.float32)
    nb = x.shape[0]
    if "nc" not in _CACHE:
        _CACHE["nc"] = build_program()
    nc = _CACHE["nc"]
    sh = prepare_shared(inputs)
    in_maps = []
    for b in range(nb):
        m = dict(sh)
        m["xT"] = np.ascontiguousarray(x[b].T)
        m["pvec"] = prepare_pvec(inputs, b)
        in_maps.append(m)
    res = run_bass_kernel_spmd(nc, in_maps, core_ids=list(range(nb)))
    out = np.stack([np.ascontiguousarray(r["outT"].T) for r in res.results], axis=0)
    return out.astype(np.float32)
```

```python
import math
import numpy as np
from contextlib import ExitStack
import concourse.bass as bass
import concourse.mybir as mybir
from concourse.bass_utils import run_bass_kernel_spmd

F32 = mybir.dt.float32
BF16 = mybir.dt.bfloat16
AF = mybir.ActivationFunctionType
ALU = mybir.AluOpType
AX = mybir.AxisListType

L = 4096
D = 1024
EPS = 1e-6
NEG = -30000.0


class Prog:
    ENGS = ("pe", "dve", "act", "pool", "sp")

    def __init__(self, nc):
        self.nc = nc
        self.ops = []
        self.relax = True
        self.root = ExitStack()
        self.stacks = [self.root]

    def sb(self, name, shape, dtype=F32):
        self.uid = getattr(self, "uid", 0) + 1
        return self.stacks[-1].enter_context(self.nc.sbuf_tensor("%s_s%d" % (name, self.uid), list(shape), dtype))

    def ps(self, name, shape, dtype=F32):
        self.uid = getattr(self, "uid", 0) + 1
        full = 512 if dtype == F32 else 1024
        t = self.stacks[-1].enter_context(self.nc.psum_tensor("%s_p%d" % (name, self.uid), [128, full], dtype))
        self.last_ps_raw = t
        n = 1
        for d in shape[1:]:
            n *= d
        v = t[0:shape[0], 0:n]
        if len(shape) == 3:
            v = v.rearrange("p (a b) -> p a b", a=shape[1])
        elif len(shape) == 4:
            v = v.rearrange("p (a b c) -> p a b c", a=shape[1], b=shape[2])
        return v

    def push(self):
        self.stacks.append(ExitStack())

    def pop(self):
        self.ops.append(dict(barrier=True))
        self.stacks.pop().close()

    def op(self, eng, fn, reads=(), writes=(), drain=False):
        self.ops.append(dict(eng=eng, fn=fn, reads=tuple(reads), writes=tuple(writes), dma=None, drain=drain))

    def dma(self, eng, out, in_, r=(), w=(), key=None):
        if key is None:
            key = w[0]
        r = [x for x in r if not x.startswith("d_")]
        w = [x for x in w if not x.startswith("d_")]
        self.ops.append(dict(eng=eng, fn=lambda e: e.dma_start(out=out, in_=in_),
                             reads=tuple(r), writes=tuple(w), dma=key))

    def mm(self, out, lhsT, rhs, start, stop, r, w, drain=False, **kw):
        self.op("pe", lambda e: e.matmul(out, lhsT, rhs, start=start, stop=stop, **kw), r, w, drain=drain)

    def tr(self, out, in_, ident, r, w):
        self.op("pe", lambda e: e.transpose(out, in_, ident), r, w)

    def act(self, out, in_, func, r, w, scale=1.0, bias=0.0):
        self.op("act", lambda e: e.activation(out=out, in_=in_, func=func, bias=bias, scale=scale), r, w)

    def tt(self, eng, out, in0, in1, op, r, w):
        self.op(eng, lambda e: e.tensor_tensor(out=out, in0=in0, in1=in1, op=op), r, w)

    def ts(self, eng, out, in0, s1, op0, r, w, s2=None, op1=None):
        if op1 is None:
            self.op(eng, lambda e: e.tensor_scalar(out=out, in0=in0, scalar1=s1, scalar2=None, op0=op0), r, w)
        else:
            self.op(eng, lambda e: e.tensor_scalar(out=out, in0=in0, scalar1=s1, scalar2=s2, op0=op0, op1=op1), r, w)

    def stt(self, out, in0, scalar, in1, op0, op1, r, w):
        self.op("dve", lambda e: e.scalar_tensor_tensor(out=out, in0=in0, scalar=scalar, in1=in1, op0=op0, op1=op1), r, w)

    def cp(self, eng, out, in_, r, w):
        if eng == "act":
            self.op("act", lambda e: e.copy(out=out, in_=in_), r, w)
        else:
            self.op(eng, lambda e: e.tensor_copy(out=out, in_=in_), r, w)

    def memset(self, eng, ap, val, w):
        self.op(eng, lambda e: e.memset(ap, val), (), w)

    def build(self):
        nc = self.nc
        ops = self.ops
        n = len(ops)
        last_writer, readers, last_dma_on_key, last_of_stream = {}, {}, {}, {}
        deps = [None] * n
        needed = [False] * n
        pending_bar, bar_seen = set(), set(self.ENGS)
        last_pe = None
        eseq = {}
        for i, o in enumerate(ops):
            if "barrier" in o:
                pending_bar = set(last_of_stream.values())
                bar_seen = set()
                deps[i] = set()
                continue
            o["stream"] = ("dma", o["dma"]) if o["dma"] is not None else ("eng", o["eng"])
            d = set()
            for b in o["reads"]:
                if b in last_writer:
                    d.add(last_writer[b])
            for b in o["writes"]:
                if b in last_writer:
                    d.add(last_writer[b])
                d.update(readers.get(b, ()))
            if o["dma"] is not None:
                k = o["dma"]
                if k in last_dma_on_key:
                    d.add(last_dma_on_key[k])
                last_dma_on_key[k] = i
                needed[i] = True
            if o["eng"] not in bar_seen:
                d |= pending_bar
                bar_seen.add(o["eng"])
            d.discard(i)
            if o["dma"] is None:
                eseq[o["eng"]] = eseq.get(o["eng"], 0) + 1
                o["seq"] = eseq[o["eng"]]
                if o["eng"] in ("dve", "act") and self.relax:
                    d = {j for j in d if not ("barrier" not in ops[j] and ops[j]["dma"] is None
                                              and ops[j]["eng"] == o["eng"] and o["seq"] - ops[j]["seq"] >= 3)}
            if o["eng"] == "pe" and o["dma"] is None:
                d = {j for j in d if not (ops[j]["eng"] == "pe" and ops[j]["dma"] is None)}
                if o.get("drain") and last_pe is not None:
                    d.add(last_pe)
                last_pe = i
            deps[i] = d
            for j in d:
                needed[j] = True
            for b in o["writes"]:
                last_writer[b] = i
                readers[b] = []
            for b in o["reads"]:
                if b not in o["writes"]:
                    readers.setdefault(b, []).append(i)
            last_of_stream[o["stream"]] = i
        for i in last_of_stream.values():
            needed[i] = True
        stream_count, sig = {}, [None] * n
        for i, o in enumerate(ops):
            if "barrier" in o:
                continue
            if needed[i]:
                s = o["stream"]
                inc = 16 if s[0] == "dma" else 1
                stream_count[s] = stream_count.get(s, 0) + inc
                sig[i] = (s, stream_count[s], inc)
        sems = {}
        for k, s in enumerate(stream_count):
            sems[s] = self.root.enter_context(nc.semaphore("sem%d" % k))
        self.n_sems = len(sems)
        per_eng = {e: [] for e in self.ENGS}
        waited = {e: {} for e in self.ENGS}
        for i, o in enumerate(ops):
            if "barrier" in o:
                continue
            e = o["eng"]
            wt = {}
            for j in deps[i]:
                s, c, _ = sig[j]
                if waited[e].get(s, 0) >= c:
                    continue
                wt[s] = max(wt.get(s, 0), c)
            for s, c in wt.items():
                waited[e][s] = c
            per_eng[e].append((i, wt))
        finals = dict(stream_count)
        self.n_ops = {e: len(v) for e, v in per_eng.items()}

        with nc.Block() as blk:
            def emit(engobj, ename):
                for i, wt in per_eng[ename]:
                    for s, c in wt.items():
                        engobj.wait_ge(sems[s], c)
                    ins = ops[i]["fn"](engobj)
                    if sig[i] is not None:
                        s, c, inc = sig[i]
                        ins.then_inc(sems[s], inc)
                if ename == "sp":
                    for s, c in finals.items():
                        engobj.wait_ge(sems[s], c)

            @blk.tensor
            def _(e):
                emit(e, "pe")

            @blk.vector
            def _(e):
                emit(e, "dve")

            @blk.scalar
            def _(e):
                emit(e, "act")

            @blk.gpsimd
            def _(e):
                emit(e, "pool")

            @blk.sync
            def _(e):
                emit(e, "sp")
        while len(self.stacks) > 1:
            self.stacks.pop().close()
        self.root.close()


PV = {}
_off = 0
for _n, _w in [("c", 8), ("modb", 96), ("normg", 32), ("finalg", 8), ("glub", 4), ("cmbin", 16),
               ("cmdww", 248), ("cmdwb", 8), ("cmlng", 8), ("cmlnb", 8), ("cmbout", 8),
               ("ffdww", 132), ("ffdwb", 44), ("ssmd", 4), ("b31", 8)]:
    PV[_n] = (_off, _w)
    _off += _w
NPV = _off


def build_program(stop_after=None, debug=False):
    nc = bass.Bass("TRN2", target_bir_lowering=False)
    P = Prog(nc)

    def din(name, shape, dt=F32):
        return nc.dram_tensor(name, list(shape), dt, kind="ExternalInput").ap()

    def dscr(name, shape, dt=F32):
        kind = "ExternalOutput" if debug else "Internal"
        return nc.dram_tensor(name, list(shape), dt, kind=kind).ap()

    xT = din("xT", [D, L])
    pvec_d = din("pvec", [128, NPV])
    ident_d = din("ident", [128, 128])
    mod_w = din("mod_w", [2, D, 6 * D])
    w_in0 = din("w_in0", [D, 2048])
    glu_w = din("glu_w", [512, 512])
    w_out0 = din("w_out0", [D, D])
    cm_w_in = din("cm_w_in", [D, 2048])
    cm_w_out = din("cm_w_out", [D, D])
    ffn_up = din("ffn_up", [2, D, 2816])
    ffn_gate = din("ffn_gate", [2, D, 2816])
    ffn_down = din("ffn_down", [2, 2816, D])
    s5p_d = din("s5p", [128, 48])
    s5bc_d = din("s5bc", [128, 4, 16, 16])
    biasg_d = din("biasg", [8, 128, 4, 256])
    maskc_d = din("maskc", [128, 2, 256])
    maskv_d = din("maskv", [128, 32, 16])
    notown_d = din("notown", [128, 32, 16])
    onehot_d = din("onehot", [16, L])
    outT = nc.dram_tensor("outT", [D, L], F32, kind="ExternalOutput").ap()

    uT_d = dscr("uT_d", [512, L], BF16)
    qT_d = dscr("qT_d", [512, L], BF16)
    kT_d = dscr("kT_d", [512, L], BF16)
    vaug_d = dscr("vaug_d", [L, 768], BF16)
    ycat_d = dscr("ycat_d", [D, L], BF16)
    xa_d = dscr("xa_d", [D, L])
    xb_d = dscr("xb_d", [D, L])

    def xview(ap):
        return ap.rearrange("(kc p) t -> p kc t", p=128)

    ones_bf = P.sb("ones_bf", [128, 128], BF16)
    P.memset("dve", ones_bf[:], 1.0, ["ones_bf"])
    identf = P.sb("identf", [128, 128])
    P.dma("sp", identf[:], ident_d, w=["identf"])
    identb = P.sb("identb", [128, 128], BF16)
    P.cp("dve", identb[:], identf[:], ["identf"], ["identb"])
    pv = P.sb("pv", [128, NPV])
    P.dma("sp", pv[:], pvec_d, w=["pv"])

    def pvs(name, a=0, b=None):
        o, w = PV[name]
        if b is None:
            b = w
        return pv[:, o + a:o + b]

    modT = P.sb("modT", [128, 2, 48])
    AB = P.sb("AB", [128, 2, 2, 8])
    rr = {"i": 0}

    cs2 = P.sb("cs2", [128, 8, 2])
    for dup in range(2):
        P.act(cs2[:, :, dup], pvs("c"), AF.Silu, ["pv"], ["cs2"])
    ng = pvs("normg").rearrange("p (l s k) -> p l s k", l=2, s=2)

    def mod_block(l, blk, mwbuf, mwname, mps, mpsname):
        mwv = mod_w[l].rearrange("(kc p) n -> p kc n", p=128)
        P.dma("sp", mwbuf[:], mwv[:, :, blk * 512:(blk + 1) * 512], w=[mwname])
        for jj in range(4):
            j = blk * 4 + jj
            for kc in range(8):
                P.mm(mps[:, j, :], mwbuf[:, kc, jj * 128:(jj + 1) * 128], cs2[:, kc, :],
                     kc == 0, kc == 7, [mwname, "cs2"], [mpsname])

    def mod_finish(l, mps, mpsname):
        P.tt("dve", modT[:, l, :], mps[:, :, 0], pvs("modb").rearrange("p (l j) -> p l j", l=2)[:, l, :], ALU.add,
             [mpsname, "pv"], ["modT"])
        for s_ in range(2):
            sc = modT[:, l, 8 + 24 * s_:16 + 24 * s_]
            P.stt(AB[:, l, s_, :], sc, 1.0, ng[:, l, s_, :], ALU.add, ALU.mult, ["modT", "pv"], ["AB"])

    P.push()
    mw = [P.sb("mw%d" % i, [128, 8, 512]) for i in range(6)]
    modps = P.ps("modps", [128, 48, 2])
    for blk in range(12):
        mod_block(0, blk, mw[blk % 6], "mw%d" % (blk % 6), modps, "modps")
    mod_finish(0, modps, "modps")
    P.pop()
    if stop_after == "M":
        dbgm = dscr("dbgm", [128, 96])
        P.dma("sp", dbgm, modT[:].rearrange("p l j -> p (l j)"), r=["modT"], w=[], key="st_dbgm")
        P.build()
        return nc

    def A_(l, s):
        return AB[:, l, s, :]

    def B_(l, s):
        return modT[:, l, 24 * s:24 * s + 8]

    def G_(l, s):
        return modT[:, l, 16 + 24 * s:24 + 24 * s]

    def rmsnorm(xt, xname, N, A, B, ht, hname, xsq, ssps, rstd, tmps, final=False):
        for kc in range(8):
            P.act(xsq[:, kc, :N], xt[:, kc, :N], AF.Square, [xname], ["xsq"])
        for kc in range(8):
            P.mm(ssps[:, :N], ones_bf[:], xsq[:, kc, :N], kc == 0, kc == 7, ["ones_bf", "xsq"], ["ssps"])
        P.act(rstd[:, :N], ssps[:, :N], AF.Ln, ["ssps", "epsb"], ["rstd"], scale=1.0 / D, bias=epsb[:, 0:1])
        P.act(rstd[:, :N], rstd[:, :N], AF.Exp, ["rstd"], ["rstd"], scale=-0.5)
        for kc in range(8):
            if final:
                P.stt(ht[:, kc, :N], xt[:, kc, :N], A[:, kc:kc + 1], rstd[:, :N], ALU.mult, ALU.mult,
                      [xname, "rstd", "AB", "pv"], [hname])
            else:
                ti = rr["i"] % len(tmps)
                rr["i"] += 1
                tn = "tmpn%d" % ti
                P.stt(tmps[ti][:, :N], xt[:, kc, :N], A[:, kc:kc + 1], rstd[:, :N], ALU.mult, ALU.mult,
                      [xname, "rstd", "AB"], [tn])
                P.act(ht[:, kc, :N], tmps[ti][:, :N], AF.Identity, [tn, "modT"], [hname], bias=B[:, kc:kc + 1])

    def load_w_bf(dst, name, src, ncols):
        v = src.rearrange("(kc p) n -> p kc n", p=128)
        c0 = 0
        while c0 < ncols:
            c1 = min(ncols, c0 + 2048)
            P.dma("pool", dst[:, :, c0:c1], v[:, :, c0:c1], w=[name], key="wld_" + name)
            c0 = c1

    def store_x(dst, tile_, name, sl, b):
        for oc in range(8):
            P.dma("sp", dst[oc * 128:(oc + 1) * 128, sl], tile_[:, oc, :], r=[name], w=[], key="st_x%d_%d" % (b, oc))

    epsb = P.sb("epsb", [128, 1])
    P.memset("dve", epsb[:], EPS, ["epsb"])

    P.push()
    win = P.sb("win", [128, 8, 2048], BF16)
    load_w_bf(win, "win", w_in0, 2048)
    xts = [P.sb("xt%d" % i, [128, 8, 512]) for i in range(2)]
    hts = [P.sb("ht%d" % i, [128, 8, 512], BF16) for i in range(2)]
    xsq = P.sb("xsq", [128, 8, 512], BF16)
    rstd = P.sb("rstd", [128, 512])
    tmps = [P.sb("tmpn%d" % i, [128, 512]) for i in range(3)]
    obs = [P.sb("ob%d" % i, [128, 512], BF16) for i in range(4)]
    vst = [P.sb("vst%d" % i, [128, 4, 768], BF16) for i in range(2)]
    ssps = P.ps("ssps", [128, 512])
    pjs = [P.ps("pj%d" % i, [128, 512]) for i in range(3)]
    pvp = [P.ps("pvp%d" % i, [128, 512]) for i in range(2)]
    for i in range(2):
        P.memset("pool", vst[i][:].rearrange("p s (r c) -> p s r c", c=192)[:, :, :, 64:128], 1.0, ["vst%d" % i])
    mwA = [P.sb("mwA%d" % i, [128, 8, 512]) for i in range(2)]
    modps1 = P.ps("modps1", [128, 48, 2])
    mblk = {"i": 0}
    oi = 0
    import os
    SK = os.environ.get("KSKIP", "")
    for tt in range(8 if "1" not in SK else 1):
        xt, xn = xts[tt % 2], "xt%d" % (tt % 2)
        ht, hn = hts[tt % 2], "ht%d" % (tt % 2)
        sl = slice(tt * 512, (tt + 1) * 512)
        if tt == 0:
            P.dma("sp", xt[:], xview(xT)[:, :, sl], w=[xn])
        if tt + 1 < 8:
            P.dma("sp", xts[(tt + 1) % 2][:], xview(xT)[:, :, (tt + 1) * 512:(tt + 2) * 512], w=["xt%d" % ((tt + 1) % 2)])
        rmsnorm(xt, xn, 512, A_(0, 0), B_(0, 0), ht, hn, xsq, ssps, rstd, tmps)
        for oc in range(12 if "p" not in SK else 0):
            pj, pn = pjs[oc % 3], "pj%d" % (oc % 3)
            for kc in range(8):
                P.mm(pj[:], win[:, kc, oc * 128:(oc + 1) * 128], ht[:, kc, :], kc == 0, kc == 7, ["win", hn], [pn])
            ob, on = obs[oi % 4], "ob%d" % (oi % 4)
            oi += 1
            P.act(ob[:], pj[:], AF.Identity, [pn], [on], scale=(0.125 if 4 <= oc < 8 else 1.0))
            dst = (uT_d, qT_d, kT_d)[oc // 4]
            P.dma("sp", dst[(oc % 4) * 128:(oc % 4 + 1) * 128, sl], ob[:], r=[on], w=["d_%d_%d" % (oc // 4, tt)], key="st_" + on)
        vs, vn = vst[tt % 2], "vst%d" % (tt % 2)
        for sub in range(4 if "v" not in SK else 0):
            pp, ppn = pvp[sub % 2], "pvp%d" % (sub % 2)
            for kc in range(8):
                P.mm(pp[:], ht[:, kc, sub * 128:(sub + 1) * 128], win[:, kc, 1536:2048], kc == 0, kc == 7, ["win", hn], [ppn])
            v4 = vs[:, sub, :].rearrange("p (r c) -> p r c", c=192)
            p4 = pp[:].rearrange("p (r c) -> p r c", c=128)
            ve = "act" if sub % 2 else "dve"
            P.cp(ve, v4[:, :, 0:64], p4[:, :, 0:64], [ppn], [vn])
            P.cp(ve, v4[:, :, 128:192], p4[:, :, 64:128], [ppn], [vn])
        for sub in range(4 if "s" not in SK else 0):
            P.dma("sp", vaug_d[tt * 512 + sub * 128:tt * 512 + (sub + 1) * 128, :], vs[:, sub, :], r=[vn], w=[], key="st_" + vn)
        for _ in range(2 if tt < 4 else 1):
            bi = mblk["i"]
            mblk["i"] += 1
            mod_block(1, bi, mwA[bi % 2], "mwA%d" % (bi % 2), modps1, "modps1")
    mod_finish(1, modps1, "modps1")
    P.pop()
    if stop_after == "A":
        P.build()
        return nc

    P.push()
    U = P.sb("U", [128, 4, L], BF16)
    for c in range(4):
        P.dma("sp", U[:, c, :], uT_d[c * 128:(c + 1) * 128, :], r=["d_0_%d" % t for t in range(8)], w=["U"], key="ldU")
    W1 = P.sb("W1", [128, 16, 8, 2, 128], BF16)
    CPb = P.sb("CPb", [128, 8, 2, 16, 32], BF16)
    BD = P.sb("BD", [128, 4, 8, 128], BF16)
    DBL = P.sb("DBL", [128, 9, 3, 16])
    gluw = P.sb("gluw", [128, 4, 512], BF16)
    load_w_bf(gluw, "gluw", glu_w, 512)

    P.push()
    s5p = P.sb("s5p", [128, 48])
    P.dma("sp", s5p[:], s5p_d, w=["s5p"])
    s5bc = P.sb("s5bc", [128, 4, 16, 16])
    P.dma("sp", s5bc[:], s5bc_d, w=["s5bc"])
    lamr, lami, logdt = s5p[:, 0:16], s5p[:, 16:32], s5p[:, 32:48]
    sc_ = P.sb("s5t", [128, 24, 16])
    S = "s5t"

    def T(i):
        return sc_[:, i, :]

    def vv(out, a, b, op):
        P.tt("dve", out, a, b, op, [S, "s5p", "PW"], [S])

    halfpi = P.sb("halfpi", [128, 1])
    P.memset("dve", halfpi[:], math.pi / 2, ["halfpi"])
    P.act(T(0), logdt, AF.Exp, ["s5p"], [S])
    vv(T(1), lami, T(0), ALU.mult)
    P.act(T(2), T(1), AF.Sin, [S], [S], scale=0.125)
    P.act(T(3), T(1), AF.Sin, [S, "halfpi"], [S], scale=-0.125, bias=halfpi[:, 0:1])
    for _ in range(3):
        vv(T(4), T(3), T(3), ALU.mult)
        vv(T(5), T(2), T(2), ALU.mult)
        P.stt(T(6), T(2), 2.0, T(3), ALU.mult, ALU.mult, [S], [S])
        vv(T(3), T(4), T(5), ALU.subtract)
        P.cp("dve", T(2), T(6), [S], [S])
    vv(T(4), lamr, T(0), ALU.mult)
    P.act(T(4), T(4), AF.Exp, [S], [S])
    vv(T(7), T(4), T(3), ALU.mult)
    vv(T(8), T(4), T(2), ALU.mult)
    vv(T(9), lamr, lamr, ALU.mult)
    vv(T(10), lami, lami, ALU.mult)
    vv(T(9), T(9), T(10), ALU.add)
    P.op("dve", lambda e: e.reciprocal(out=T(9), in_=T(9)), [S], [S])
    P.ts("dve", T(10), T(7), -1.0, ALU.add, [S], [S])
    vv(T(11), T(10), lamr, ALU.mult)
    vv(T(12), T(8), lami, ALU.mult)
    vv(T(11), T(11), T(12), ALU.add)
    vv(T(11), T(11), T(9), ALU.mult)
    vv(T(12), T(8), lamr, ALU.mult)
    vv(T(13), T(10), lami, ALU.mult)
    vv(T(12), T(12), T(13), ALU.subtract)
    vv(T(12), T(12), T(9), ALU.mult)
    PW = P.sb("PW", [128, 9, 2, 16])
    P.memset("dve", PW[:, 0, 0, :], 1.0, ["PW"])
    P.memset("dve", PW[:, 0, 1, :], 0.0, ["PW"])

    def cmul(o_r, o_i, a_r, a_i, b_r, b_i, names):
        P.tt("dve", T(14), a_r, b_r, ALU.mult, names, [S])
        P.tt("dve", T(15), a_i, b_i, ALU.mult, names, [S])
        P.tt("dve", T(16), a_r, b_i, ALU.mult, names, [S])
        P.tt("dve", T(17), a_i, b_r, ALU.mult, names, [S])
        P.tt("dve", o_r, T(14), T(15), ALU.subtract, [S], names)
        P.tt("dve", o_i, T(16), T(17), ALU.add, [S], names)

    for k in range(1, 9):
        cmul(PW[:, k, 0, :], PW[:, k, 1, :], PW[:, k - 1, 0, :], PW[:, k - 1, 1, :], T(7), T(8), ["PW", S])
    P.cp("dve", DBL[:, 0, 0, :], PW[:, 8, 0, :], ["PW"], ["DBL"])
    P.cp("dve", DBL[:, 0, 1, :], PW[:, 8, 1, :], ["PW"], ["DBL"])
    for j in range(1, 9):
        cmul(DBL[:, j, 0, :], DBL[:, j, 1, :], DBL[:, j - 1, 0, :], DBL[:, j - 1, 1, :],
             DBL[:, j - 1, 0, :], DBL[:, j - 1, 1, :], ["DBL", S])
    for j in range(9):
        P.ts("dve", DBL[:, j, 2, :], DBL[:, j, 1, :], -1.0, ALU.mult, ["DBL"], ["DBL"])

    if stop_after == "B00":
        d1 = dscr("dbg_pw", [128, 9 * 2 * 16]); P.dma("sp", d1, PW[:].rearrange("p a b c -> p (a b c)"), r=["PW"], w=[], key="st_d1")
        d2 = dscr("dbg_dbl", [128, 9 * 3 * 16]); P.dma("sp", d2, DBL[:].rearrange("p a b c -> p (a b c)"), r=["DBL"], w=[], key="st_d2")
        P.build()
        return nc
    wt = P.sb("s5w", [128, 8, 16, 16])
    WN = "s5w"

    def Wt(i):
        return wt[:, i, :, :]

    def bc(ap16):
        return ap16.unsqueeze(2).to_broadcast([128, 16, 16])

    def cmulw(o_r, o_i, s_r, s_i, b_r, b_i, rn, wn, neg_i=False):
        P.tt("dve", Wt(4), bc(s_r), b_r, ALU.mult, rn, [WN])
        P.tt("dve", Wt(5), bc(s_i), b_i, ALU.mult, rn, [WN])
        P.tt("dve", Wt(6), bc(s_r), b_i, ALU.mult, rn, [WN])
        P.tt("dve", Wt(7), bc(s_i), b_r, ALU.mult, rn, [WN])
        P.tt("dve", o_r, Wt(4), Wt(5), ALU.subtract, [WN], wn)
        P.tt("dve", o_i, Wt(6), Wt(7), ALU.add, [WN], wn)

    br, bi_, cr, ci = s5bc[:, 0], s5bc[:, 1], s5bc[:, 2], s5bc[:, 3]
    cmulw(Wt(0), Wt(1), T(11), T(12), br, bi_, [S, "s5bc", WN], [WN])

    BB128 = P.sb("BB128", [128, 2, 16, 128], BF16)
    P.memset("pool", BB128[:], 0.0, ["BB128"])
    PS_BB = 2 * 16 * 128

    def diag_ap(tensor, pstride, base_off, half):
        off = half * 64 * pstride + base_off + 16 * half
        return bass.AP(tensor, off, [[pstride, 64], [512, 4], [160, 4], [1, 16]])

    for ri in range(2):
        for half in range(2):
            hs = slice(64 * half, 64 * half + 64)
            for q in range(16):
                co = 32 * (q % 4) + 16 * half
                P.cp("dve", BB128[hs, ri, q, co:co + 16], Wt(ri)[hs, q, :], [WN], ["BB128"])

    CPf = P.sb("CPf", [128, 8, 2, 16, 32], BF16)
    P.memset("pool", CPf[:], 0.0, ["CPf"])
    P.memset("pool", CPb[:], 0.0, ["CPb"])
    for k in range(9):
        cmulw(Wt(2), Wt(3), PW[:, k, 0, :], PW[:, k, 1, :], cr, ci, ["PW", "s5bc", WN], [WN])
        P.ts("dve", Wt(3), Wt(3), -1.0, ALU.mult, [WN], [WN])
        for ri in range(2):
            for half in range(2):
                hs = slice(64 * half, 64 * half + 64)
                if k < 8:
                    P.cp("dve", CPf[hs, k, ri, :, 16 * half:16 * half + 16], Wt(2 + ri)[hs], [WN], ["CPf"])
                if k >= 1:
                    P.cp("act", CPb[hs, k - 1, ri, :, 16 * half:16 * half + 16], Wt(2 + ri)[hs], [WN], ["CPb"])

    bdps2 = [P.ps("bdps%d" % i, [128, 128]) for i in range(2)]
    P.memset("pool", BD[:], 0.0, ["BD"])
    for c in range(4 if "a" not in SK else 0):
        for tau in range(8):
            bp = bdps2[tau % 2]
            bpn = "bdps%d" % (tau % 2)
            for q4 in range(4):
                q = 4 * c + q4
                for ri in range(2):
                    P.mm(bp[:], BB128[:, ri, q, :], CPf[:, tau, ri, 4 * c:4 * c + 4, :], q4 == 0 and ri == 0, q4 == 3 and ri == 1,
                         ["BB128", "CPf"], [bpn])
            for q4 in range(4):
                P.cp("act", BD[32 * q4:32 * q4 + 32, c, tau, 32 * q4:32 * q4 + 32],
                     bp[32 * q4:32 * q4 + 32, 32 * q4:32 * q4 + 32], [bpn], ["BD"])

    srcs = [P.sb("w1src%d" % i, [128, 16, 128], BF16) for i in range(2)]
    for i in range(2):
        P.memset("pool", srcs[i][:], 0.0, ["w1src%d" % i])
    trps = [P.ps("trps%d" % i, [128, 4, 128], BF16) for i in range(2)]
    ti = 0
    for k in range(8 if "b" not in SK else 0):
        cmulw(Wt(2), Wt(3), PW[:, k, 0, :], PW[:, k, 1, :], Wt(0), Wt(1), ["PW", WN], [WN])
        for ri in range(2):
            sname = "w1src%d" % ri
            for half in range(2):
                hs = slice(64 * half, 64 * half + 64)
                for q in range(16):
                    co = 32 * (q % 4) + 16 * half
                    P.cp("dve" if q % 2 else "pool", srcs[ri][hs, q, co:co + 16], Wt(2 + ri)[hs, q, :], [WN], [sname])
            for c in range(4):
                tp, tpn = trps[ti % 2], "trps%d" % (ti % 2)
                ti += 1
                for q4 in range(4):
                    P.tr(tp[:, q4, :], srcs[ri][:, 4 * c + q4, :], identb[:], [sname, "identb"], [tpn])
                P.cp("act" if c % 2 else "dve", W1[:, 4 * c:4 * c + 4, k, ri, :], tp[:], [tpn], ["W1"])
    if stop_after == "B0":
        d1 = dscr("dbg_pw", [128, 9 * 2 * 16]); P.dma("sp", d1, PW[:].rearrange("p a b c -> p (a b c)"), r=["PW"], w=[], key="st_d1")
        d2 = dscr("dbg_dbl", [128, 9 * 3 * 16]); P.dma("sp", d2, DBL[:].rearrange("p a b c -> p (a b c)"), r=["DBL"], w=[], key="st_d2")
        d3 = dscr("dbg_bd", [128, 4 * 8 * 128], BF16); P.dma("sp", d3, BD[:].rearrange("p a b c -> p (a b c)"), r=["BD"], w=[], key="st_d3")
        d4 = dscr("dbg_w1", [128, 16, 8 * 2 * 128], BF16)
        for qq in range(16):
            P.dma("sp", d4[:, qq, :], W1[:, qq].rearrange("p b c d -> p (b c d)"), r=["W1"], w=[], key="st_d4")
        d5 = dscr("dbg_cpb", [128, 8 * 2 * 16 * 32], BF16); P.dma("sp", d5, CPb[:].rearrange("p a b c d -> p (a b c d)"), r=["CPb"], w=[], key="st_d5")
        P.build()
        return nc
    P.pop()

    ygb = P.sb("ygb", [128, 4, L], BF16)
    Xa = [[P.sb("Xa%d%d" % (b, ri), [128, 512]) for ri in range(2)] for b in range(2)]
    Xpr = [[P.sb("Xpr%d%d" % (q4, ri), [128, 512], BF16) for ri in range(2)] for q4 in range(4)]
    slp = [P.ps("slp%d" % ri, [128, 512]) for ri in range(2)]
    yps = [P.ps("yps%d" % i, [128, 512]) for i in range(2)]
    ypre = [P.sb("ypre%d" % i, [128, 512]) for i in range(2)]
    gtmp = [P.sb("gtmp%d" % i, [128, 512]) for i in range(2)]
    for q4 in range(4):
        for ri in range(2):
            P.memset("pool", Xpr[q4][ri][:, 0:1], 0.0, ["Xpr%d%d" % (q4, ri)])
    for c in range(4):
        Ust = U[:, c, :].rearrange("p (j s) -> p s j", s=8)
        for q4 in range(4):
            q = 4 * c + q4
            for ri in range(2):
                for s8 in range(8):
                    P.mm(slp[ri][:], W1[:, q, 7 - s8, ri, :], Ust[:, s8, :], s8 == 0, s8 == 7, ["W1", "U"], ["slp%d" % ri],
                         drain=(q4 == 0 and ri == 0 and s8 == 0))
                P.cp("act", Xa[0][ri][:], slp[ri][:], ["slp%d" % ri], ["Xa0%d" % ri])
            cur = 0
            for j in range(9):
                d = 1 << j
                nx = 1 - cur
                sr, si = Xa[cur][0], Xa[cur][1]
                dr, di = Xa[nx][0], Xa[nx][1]
                nsr, nsi, ndr, ndi = "Xa%d0" % cur, "Xa%d1" % cur, "Xa%d0" % nx, "Xa%d1" % nx
                ar_, ai_, nai_ = DBL[:, j, 0, q:q + 1], DBL[:, j, 1, q:q + 1], DBL[:, j, 2, q:q + 1]
                P.cp("pool", dr[:, 0:d], sr[:, 0:d], [nsr], [ndr])
                P.cp("pool", di[:, 0:d], si[:, 0:d], [nsi], [ndi])
                P.stt(dr[:, d:], sr[:, 0:512 - d], ar_, sr[:, d:], ALU.mult, ALU.add, [nsr, "DBL"], [ndr])
                P.stt(di[:, d:], si[:, 0:512 - d], ar_, si[:, d:], ALU.mult, ALU.add, [nsi, "DBL"], [ndi])
                P.stt(dr[:, d:], si[:, 0:512 - d], nai_, dr[:, d:], ALU.mult, ALU.add, [nsi, ndr, "DBL"], [ndr])
                P.stt(di[:, d:], sr[:, 0:512 - d], ai_, di[:, d:], ALU.mult, ALU.add, [nsr, ndi, "DBL"], [ndi])
                cur = nx
            for ri in range(2):
                P.cp("act", Xpr[q4][ri][:, 1:512], Xa[cur][ri][:, 0:511], ["Xa%d%d" % (cur, ri)], ["Xpr%d%d" % (q4, ri)])
        for t8 in range(8):
            yp, ypn = yps[t8 % 2], "yps%d" % (t8 % 2)
            for s8 in range(t8 + 1):
                P.mm(yp[:], BD[:, c, t8 - s8, :], Ust[:, s8, :], s8 == 0, False, ["BD", "U"], [ypn], drain=(s8 == 0))
            for q4 in range(4):
                q = 4 * c + q4
                for ri in range(2):
                    last = (q4 == 3 and ri == 1)
                    P.mm(yp[32 * q4:32 * q4 + 32, :], CPb[:, t8, ri, q, :], Xpr[q4][ri][:], False, last,
                         ["CPb", "Xpr%d%d" % (q4, ri)], [ypn], drain=(q4 == 0 and ri == 0), tile_position=(0, 32 * q4))
            ye, yen = ypre[t8 % 2], "ypre%d" % (t8 % 2)
            P.stt(ye[:], Ust[:, t8, :], pvs("ssmd", c, c + 1), yp[:], ALU.mult, ALU.add, ["U", "pv", ypn], [yen])
            gt, gtn = gtmp[t8 % 2], "gtmp%d" % (t8 % 2)
            P.act(gt[:], ye[:], AF.Square, [yen], [gtn])
            P.ts("dve", gt[:], gt[:], 0.044715, ALU.mult, [gtn], [gtn], s2=1.0, op1=ALU.add)
            P.tt("dve", gt[:], gt[:], ye[:], ALU.mult, [gtn, yen], [gtn])
            P.act(gt[:], gt[:], AF.Sigmoid, [gtn], [gtn], scale=1.5957691216057308)
            P.tt("dve", ygb[:, c, :].rearrange("p (j s) -> p s j", s=8)[:, t8, :], ye[:], gt[:], ALU.mult, [yen, gtn], ["ygb%d" % c])
    zps = [P.ps("zps%d" % i, [128, 512]) for i in range(2)]
    sgs = [P.sb("sg%d" % i, [128, 512]) for i in range(2)]
    obs = [P.sb("obB%d" % i, [128, 512], BF16) for i in range(2)]
    it = 0
    for tt in range(8):
        sl = slice(tt * 512, (tt + 1) * 512)
        for oc in range(4):
            zp, zn = zps[it % 2], "zps%d" % (it % 2)
            sg, sn = sgs[it % 2], "sg%d" % (it % 2)
            ob, on = obs[it % 2], "obB%d" % (it % 2)
            it += 1
            for kc in range(4):
                P.mm(zp[:], gluw[:, kc, oc * 128:(oc + 1) * 128], ygb[:, kc, sl], kc == 0, kc == 3,
                     ["gluw"] + ["ygb%d" % kk for kk in range(4)], [zn], drain=(tt == 0 and oc == 0 and kc == 0))
            P.act(sg[:], zp[:], AF.Sigmoid, [zn, "pv"], [sn], bias=pvs("glub", oc, oc + 1))
            P.tt("dve", ob[:], ygb[:, oc, sl], sg[:], ALU.mult, ["ygb%d" % oc, sn], [on])
            P.dma("sp", ycat_d[oc * 128:(oc + 1) * 128, sl], ob[:], r=[on], w=["d_y_%d_%d" % (oc, tt)], key="st_" + on)
    P.pop()
    if stop_after == "B":
        P.build()
        return nc

    P.push()
    qa2 = [[P.sb("qa%d%d" % (p_, i), [80, L], BF16) for i in range(2)] for p_ in range(2)]
    ka2 = [[P.sb("ka%d%d" % (p_, i), [80, L], BF16) for i in range(2)] for p_ in range(2)]
    vg2 = [P.sb("vg%d" % p_, [128, 32, 192], BF16) for p_ in range(2)]
    bt2 = [[P.sb("bt%d%d" % (p_, i), [128, 4, 256]) for i in range(2)] for p_ in range(2)]
    maskc = P.sb("maskc", [128, 2, 256])
    P.dma("sp", maskc[:], maskc_d, w=["maskc"])
    maskv = P.sb("maskv", [128, 32, 16])
    P.dma("sp", maskv[:], maskv_d, w=["maskv"])
    notown = P.sb("notown", [128, 32, 16])
    P.dma("sp", notown[:], notown_d, w=["notown"])
    kmb = [P.sb("kmb%d" % i, [80, 16], BF16) for i in range(2)]
    kmf = P.sb("kmf", [64, 16])
    Mext = P.sb("Mext", [128, 32, 80], BF16)
    gm = P.sb("gm", [128, 32, 16])
    msel = P.sb("msel", [128, 32, 16])
    t8a = P.sb("t8a", [128, 32, 8])
    P.memset("pool", Mext[:], 0.0, ["Mext"])
    for i in range(2):
        P.memset("pool", kmb[i][:], 0.0, ["kmb%d" % i])
    NS = 3
    gbank = P.ps("gbank", [128, 32, 16])
    sps = [P.ps("sps%d" % i, [128, 2, 256]) for i in range(NS)]
    OA = [P.ps("OA%d" % i, [128, 256]) for i in range(2)]
    OB = [P.ps("OB%d" % i, [128, 256]) for i in range(2)]
    tpsv = P.last_ps_raw[0:80, :].bitcast(BF16).rearrange("p (a b) -> p a b", a=8)
    sbb = [P.sb("sbb%d" % i, [128, 2, 256]) for i in range(3)]
    pTs = [P.sb("pT%d" % i, [128, 2, 256], BF16) for i in range(4)]
    Dt = P.sb("Dt", [128, 256])
    oba = [P.sb("oba%d" % i, [128, 256], BF16) for i in range(2)]
    cnt = {"p": 0, "o": 0, "b": 0}

    def att_loads(hp):
        pr = hp % 2
        P.dma("sp", vg2[pr][:], vaug_d[:, hp * 192:(hp + 1) * 192].rearrange("(kc p) c -> p kc c", p=128), w=["vg%d" % pr])
        for i in range(2):
            h = 2 * hp + i
            P.dma("sp", qa2[pr][i][0:64, :], qT_d[64 * h:64 * h + 64, :], w=["qa%d%d_q" % (pr, i)], key="ld_qa%d%d" % (pr, i))
            P.dma("sp", ka2[pr][i][0:64, :], kT_d[64 * h:64 * h + 64, :], w=["ka%d%d" % (pr, i)], key="ld_ka%d%d" % (pr, i))
            for hh in range(2):
                P.dma("pool", ka2[pr][i][64:80, hh * 2048:(hh + 1) * 2048], onehot_d[:, hh * 2048:(hh + 1) * 2048],
                      w=["ka%d%d" % (pr, i)], key="ld_ka%d%d" % (pr, i))
            P.dma("sp", bt2[pr][i][:], biasg_d[h], w=["bt%d%d" % (pr, i)])

    att_loads(0)
    for hp in range(4):
        pr = hp % 2
        qa, ka, vg, bt = qa2[pr], ka2[pr], vg2[pr], bt2[pr]
        nq = lambda i, pr=pr: "qa%d%d_q" % (pr, i)
        nm = lambda i, pr=pr: "qa%d%d_m" % (pr, i)
        nka = lambda i, pr=pr: "ka%d%d" % (pr, i)
        nbt = lambda i, pr=pr: "bt%d%d" % (pr, i)
        nvg = "vg%d" % pr
        if hp + 1 < 4:
            att_loads(hp + 1)
        for i in range(2):
            h = 2 * hp + i
            P.memset("pool", qa[i][64:80, :], 0.0, [nm(i)])
            P.ts("dve", bt[i][:], bt[i][:], pvs("b31", h, h + 1), ALU.subtract, [nbt(i), "pv"], [nbt(i)])
            P.tt("dve", bt[i][:, 0:2, :], bt[i][:, 0:2, :], maskc[:], ALU.add, [nbt(i), "maskc"], [nbt(i)])
            P.op("dve", lambda e, i=i, ka=ka: e.tensor_reduce(out=kmf[:], in_=ka[i][0:64, :].rearrange("p (j t) -> p j t", t=256),
                                                            axis=AX.X, op=ALU.add), [nka(i)], ["kmf"])
            P.ts("dve", kmb[i][0:64, :], kmf[:], 1.0 / 256, ALU.mult, ["kmf"], ["kmb%d" % i])
            for qt in range(32):
                P.mm(gbank[:, qt, :], qa[i][0:80, qt * 128:(qt + 1) * 128], kmb[i][:], True, True,
                     [nq(i), nm(i), "kmb%d" % i], ["gbank"])
            P.tt("dve", gm[:], gbank[:], maskv[:], ALU.add, ["gbank", "maskv"], ["gm"])
            for qt in range(32):
                P.op("dve", lambda e, qt=qt: e.max(out=t8a[:, qt, :], in_=gm[:, qt, :]), ["gm"], ["t8a"])
            P.tt("dve", msel[:], gm[:], t8a[:, :, 2:3].to_broadcast([128, 32, 16]), ALU.is_ge, ["gm", "t8a"], ["msel"])
            P.stt(Mext[:, :, 64:80], msel[:], -1.0, notown[:], ALU.add, ALU.mult, ["msel", "notown"], ["Mext"])
            for g4 in range(4):
                for t in range(8):
                    P.tr(tpsv[:, t, :], Mext[:, 8 * g4 + t, :], identb[:], ["Mext", "identb"], ["OB1"])
                P.cp("act", qa[i][64:80, g4 * 1024:(g4 + 1) * 1024].rearrange("p (a b) -> p a b", a=8), tpsv[64:80, :, :],
                     ["OB1"], [nm(i)])
        tasks = [(n, i, j) for n in range(16) for i in range(2) for j in range(n + 1)]
        LOOK = NS - 1

        def emit_qk(t):
            n, i, j = tasks[t]
            s = t % NS
            for cc in range(2):
                kc = 2 * j + cc
                P.mm(sps[s][:, cc, :], ka[i][0:80, kc * 128:(kc + 1) * 128], qa[i][0:80, n * 256:(n + 1) * 256], True, True,
                     [nka(i), nq(i), nm(i)], ["sps%d" % s])

        def emit_rest(t):
            n, i, j = tasks[t]
            s = t % NS
            sp_, spn = sps[s][:], "sps%d" % s
            pi_ = cnt["p"] % 4
            cnt["p"] += 1
            pT, pTn = pTs[pi_], "pT%d" % pi_
            if j >= n - 1:
                base = 0 if j == n else 2
                bi_ = cnt["b"] % 3
                cnt["b"] += 1
                sb_, sbn = sbb[bi_], "sbb%d" % bi_
                P.tt("dve", sb_[:], sp_, bt[i][:, base:base + 2, :], ALU.add, [spn, nbt(i)], [sbn])
                P.act(pT[:], sb_[:], AF.Exp, [sbn], [pTn])
            else:
                P.act(pT[:], sp_, AF.Exp, [spn], [pTn])
            for cc in range(2):
                kc = 2 * j + cc
                first = (j == 0 and cc == 0)
                last = (j == n and cc == 1)
                P.mm(OA[i][:], vg[:, kc, 0:128], pT[:, cc, :], first, last, [nvg, pTn], ["OA%d" % i])
                P.mm(OB[i][:], vg[:, kc, 64:192], pT[:, cc, :], first, last, [nvg, pTn], ["OB%d" % i])
            if i == 1 and j == n:
                qs = slice(n * 256, (n + 1) * 256)
                P.cp("act", Dt[0:64, :], OB[0][0:64, :], ["OB0"], ["Dt"])
                P.cp("act", Dt[64:128, :], OA[1][64:128, :], ["OA1"], ["Dt"])
                P.op("dve", lambda e: e.reciprocal(out=Dt[:], in_=Dt[:]), ["Dt"], ["Dt"])
                o = cnt["o"] % 2
                cnt["o"] += 1
                P.tt("dve", oba[o][0:64, :], OA[0][0:64, :], Dt[0:64, :], ALU.mult, ["OA0", "Dt"], ["oba%d" % o])
                P.tt("dve", oba[o][64:128, :], OB[1][64:128, :], Dt[64:128, :], ALU.mult, ["OB1", "Dt"], ["oba%d" % o])
                P.dma("sp", ycat_d[512 + hp * 128:512 + (hp + 1) * 128, qs], oba[o][:], r=["oba%d" % o],
                      w=[], key="st_oba%d" % o)

        for t in range(len(tasks) + LOOK):
            if t < len(tasks):
                emit_qk(t)
            if t - LOOK >= 0:
                emit_rest(t - LOOK)
    P.pop()
    if stop_after == "C":
        P.build()
        return nc

    def ycat_reads(tt):
        r = ["d_y_%d_%d" % (oc, tt) for oc in range(4)]
        r += ["d_ya_%d_%d" % (hp, n) for hp in range(4) for n in (2 * tt, 2 * tt + 1)]
        return r

    P.push()
    wout = P.sb("wout", [128, 8, D], BF16)
    load_w_bf(wout, "wout", w_out0, D)
    xts = [P.sb("xtD%d" % i, [128, 8, 512]) for i in range(2)]
    ycs = [P.sb("yc%d" % i, [128, 8, 512], BF16) for i in range(2)]
    xos = [P.sb("xo%d" % i, [128, 8, 512]) for i in range(2)]
    pjs = [P.ps("pjD%d" % i, [128, 512]) for i in range(3)]
    for tt in range(8):
        sl = slice(tt * 512, (tt + 1) * 512)
        b = tt % 2
        if tt == 0:
            P.dma("sp", xts[b][:], xview(xT)[:, :, sl], w=["xtD%d" % b])
            P.dma("sp", ycs[b][:], xview(ycat_d)[:, :, sl], w=["yc%d" % b])
        if tt + 1 < 8:
            sl2 = slice((tt + 1) * 512, (tt + 2) * 512)
            P.dma("sp", xts[1 - b][:], xview(xT)[:, :, sl2], w=["xtD%d" % (1 - b)])
            P.dma("sp", ycs[1 - b][:], xview(ycat_d)[:, :, sl2], w=["yc%d" % (1 - b)])
        for oc in range(8):
            pj, pn = pjs[oc % 3], "pjD%d" % (oc % 3)
            for kc in range(8):
                P.mm(pj[:], wout[:, kc, oc * 128:(oc + 1) * 128], ycs[b][:, kc, :], kc == 0, kc == 7, ["wout", "yc%d" % b], [pn])
            P.stt(xos[b][:, oc, :], pj[:], G_(0, 0)[:, oc:oc + 1], xts[b][:, oc, :], ALU.mult, ALU.add,
                  [pn, "modT", "xtD%d" % b], ["xo%d" % b])
        store_x(xa_d, xos[b], "xo%d" % b, sl, b)
    P.pop()
    if stop_after == "D":
        P.build()
        return nc

    def ffn_phase(l, xin_d, xout_d, final=False):
        P.push()
        wg = P.sb("wg", [128, 8, 2816], BF16)
        wu = P.sb("wu", [128, 8, 2816], BF16)
        wd = P.sb("wd", [128, 22, D], BF16)
        load_w_bf(wg, "wg", ffn_gate[l], 2816)
        load_w_bf(wu, "wu", ffn_up[l], 2816)
        load_w_bf(wd, "wd", ffn_down[l], D)
        N = 256
        NT = L // N
        xts = [P.sb("xtF%d" % i, [128, 8, N]) for i in range(3)]
        hts = [P.sb("htF%d" % i, [128, 8, N], BF16) for i in range(2)]
        hids = [P.sb("hid%d" % i, [128, 22, N], BF16) for i in range(2)]
        xsq = P.sb("xsq", [128, 8, N], BF16)
        rstd = P.sb("rstd", [128, N])
        tmps = [P.sb("tmpn%d" % i, [128, N]) for i in range(2)]
        halo = P.sb("halo", [128, 22, 2])
        P.memset("pool", halo[:], 0.0, ["halo%d" % j for j in range(22)])
        gbs = [P.sb("gb%d" % i, [128, N + 2]) for i in range(2)]
        accs = [P.sb("acc%d" % i, [128, N]) for i in range(2)]
        ssps = P.ps("ssps", [128, 512])
        gpss = [P.ps("gpsF%d" % i, [128, N]) for i in range(2)]
        upss = [P.ps("upsF%d" % i, [128, N]) for i in range(2)]
        dpss = [P.ps("dpsF%d" % i, [128, N]) for i in range(2)]
        dww = pvs("ffdww").rearrange("p (l j k) -> p l j k", l=2, j=22)
        dwb = pvs("ffdwb").rearrange("p (l j) -> p l j", l=2)
        cnt = {"it": 0}

        def emit_load(t):
            P.dma("sp", xts[t % 3][:], xview(xin_d)[:, :, t * N:(t + 1) * N], w=["xtF%d" % (t % 3)])

        def emit_norm(t):
            rmsnorm(xts[t % 3], "xtF%d" % (t % 3), N, A_(l, 1), B_(l, 1), hts[t % 2], "htF%d" % (t % 2), xsq, ssps, rstd, tmps)

        def emit_gu(t, j):
            ht, hn = hts[t % 2], "htF%d" % (t % 2)
            hid, hidn = hids[t % 2], "hid%d" % (t % 2)
            it = cnt["it"]
            cnt["it"] += 1
            gb, gbn = gbs[it % 2], "gb%d" % (it % 2)
            acc, acn = accs[it % 2], "acc%d" % (it % 2)
            gp, gpn = gpss[it % 2], "gpsF%d" % (it % 2)
            up, upn = upss[it % 2], "upsF%d" % (it % 2)
            hl = "halo%d" % j
            for kc in range(8):
                P.mm(gp[:], wg[:, kc, j * 128:(j + 1) * 128], ht[:, kc, :], kc == 0, kc == 7, ["wg", hn], [gpn])
            for kc in range(8):
                P.mm(up[:], wu[:, kc, j * 128:(j + 1) * 128], ht[:, kc, :], kc == 0, kc == 7, ["wu", hn], [upn])
            P.cp("act", gb[:, 2:N + 2], gp[:], [gpn], [gbn])
            P.cp("pool", gb[:, 0:2], halo[:, j, :], [hl], [gbn])
            P.act(acc[:], gp[:], AF.Identity, [gpn, "pv"], [acn], scale=dww[:, l, j, 2:3], bias=dwb[:, l, j:j + 1])
            P.stt(acc[:], gb[:, 1:N + 1], dww[:, l, j, 1:2], acc[:], ALU.mult, ALU.add, [gbn, "pv", acn], [acn])
            P.stt(acc[:], gb[:, 0:N], dww[:, l, j, 0:1], acc[:], ALU.mult, ALU.add, [gbn, "pv", acn], [acn])
            P.cp("pool", halo[:, j, :], gb[:, N:N + 2], [gbn], [hl])
            P.act(acc[:], acc[:], AF.Silu, [acn], [acn])
            P.tt("dve", hid[:, j, :], acc[:], up[:], ALU.mult, [acn, upn], [hidn])

        def emit_down(t):
            xt, xn = xts[t % 3], "xtF%d" % (t % 3)
            hid, hidn = hids[t % 2], "hid%d" % (t % 2)
            for oc in range(8):
                dp, dpn = dpss[oc % 2], "dpsF%d" % (oc % 2)
                for j in range(22):
                    P.mm(dp[:], wd[:, j, oc * 128:(oc + 1) * 128], hid[:, j, :], j == 0, j == 21, ["wd", hidn], [dpn])
                P.stt(xt[:, oc, :], dp[:], G_(l, 1)[:, oc:oc + 1], xt[:, oc, :], ALU.mult, ALU.add,
                      [dpn, "modT", xn], [xn])
            if final:
                rmsnorm(xt, xn, N, pvs("finalg"), None, xt, xn, xsq, ssps, rstd, None, final=True)
            store_x(xout_d, xt, xn, slice(t * N, (t + 1) * N), t % 2)

        emit_load(0)
        emit_load(1)
        emit_norm(0)
        for t in range(NT):
            for j in range(22):
                emit_gu(t, j)
                if j == 2 and t >= 1:
                    emit_down(t - 1)
                if j == 8 and t + 2 < NT:
                    emit_load(t + 2)
                if j == 10 and t + 1 < NT:
                    emit_norm(t + 1)
        emit_down(NT - 1)
        P.pop()

    ffn_phase(0, xa_d, xb_d)
    if stop_after == "E":
        P.build()
        return nc

    P.push()
    cwin = P.sb("cwin", [128, 8, 2048], BF16)
    cwout = P.sb("cwout", [128, 8, D], BF16)
    load_w_bf(cwin, "cwin", cm_w_in, 2048)
    load_w_bf(cwout, "cwout", cm_w_out, D)
    dg = P.sb("dg", [128, 8, 31, 128], BF16)
    dwv = pvs("cmdww").rearrange("p (c k) -> p c k", c=8)
    for oc in range(8):
        P.tt("pool" if oc % 2 else "dve", dg[:, oc, :, :], identf[:].unsqueeze(1).to_broadcast([128, 31, 128]),
             dwv[:, oc, :].unsqueeze(2).to_broadcast([128, 31, 128]), ALU.mult, ["identf", "pv"], ["dg%d" % oc])
    N = 256
    NT = L // N
    xts = [P.sb("xtG%d" % i, [128, 8, N]) for i in range(3)]
    ht = P.sb("htG", [128, 8, N], BF16)
    xsq = P.sb("xsq", [128, 8, N], BF16)
    rstd = P.sb("rstd", [128, N])
    rstd2 = P.sb("rstd2", [128, N])
    tmps = [P.sb("tmpn%d" % i, [128, N]) for i in range(3)]
    ags = [P.sb("ag%d" % i, [128, 8, N + 30], BF16) for i in range(2)]
    P.memset("pool", ags[1][:, :, N:N + 30], 0.0, ["ag1_%d" % oc for oc in range(8)])
    cvt = P.sb("cvt", [128, 8, N])
    cvb = P.sb("cvb", [128, 8, N], BF16)
    cvq = P.sb("cvq", [128, 8, N], BF16)
    mu = P.sb("mu", [128, N])
    msq = P.sb("msq", [128, N])
    lnb = P.sb("lnb", [128, 8, N], BF16)
    sgs = [P.sb("sgG%d" % i, [128, N]) for i in range(4)]
    ssps = P.ps("ssps", [128, N])
    a1p = [P.ps("a1p%d" % i, [128, N]) for i in range(2)]
    a2p = [P.ps("a2p%d" % i, [128, N]) for i in range(2)]
    cps = [P.ps("cps%d" % i, [128, N]) for i in range(2)]
    sqps = P.ps("sqps", [128, N])
    bin_ = pvs("cmbin")
    hbin = P.sb("hbin", [128, 8])
    P.ts("dve", hbin[:], bin_[:, 8:16], 0.5, ALU.mult, ["pv"], ["hb"])

    def c_load(t):
        P.dma("sp", xts[t % 3][:], xview(xb_d)[:, :, t * N:(t + 1) * N], w=["xtG%d" % (t % 3)])

    def c_norm(t):
        rmsnorm(xts[t % 3], "xtG%d" % (t % 3), N, A_(1, 0), B_(1, 0), ht, "htG", xsq, ssps, rstd, tmps)

    def c_halo(t):
        ag, pag = ags[t % 2], ags[(t + 1) % 2]
        P.cp("pool", ag[:, :, 0:30], pag[:, :, N:N + 30], ["ag%d_%d" % ((t + 1) % 2, oc) for oc in range(8)],
             ["ag%d_%d" % (t % 2, oc) for oc in range(8)])

    def c_s1(t, oc):
        ag, agn = ags[t % 2], "ag%d_%d" % (t % 2, oc)
        p1, p1n = a1p[oc % 2], "a1p%d" % (oc % 2)
        p2, p2n = a2p[oc % 2], "a2p%d" % (oc % 2)
        sg, sgn = sgs[oc % 4], "sgG%d" % (oc % 4)
        for kc in range(8):
            P.mm(p1[:], cwin[:, kc, oc * 128:(oc + 1) * 128], ht[:, kc, :], kc == 0, kc == 7, ["cwin", "htG"], [p1n])
        for kc in range(8):
            P.mm(p2[:], cwin[:, kc, 1024 + oc * 128:1024 + (oc + 1) * 128], ht[:, kc, :], kc == 0, kc == 7, ["cwin", "htG"], [p2n])
        P.act(sg[:], p2[:], AF.Tanh, [p2n, "hb"], [sgn], scale=0.5, bias=hbin[:, oc:oc + 1])
        P.ts("pool", sg[:], sg[:], 0.5, ALU.mult, [sgn], [sgn], s2=0.5, op1=ALU.add)
        P.stt(ag[:, oc, 30:N + 30], p1[:], bin_[:, oc:oc + 1], sg[:], ALU.add, ALU.mult, [p1n, "pv", sgn], [agn])

    def c_s2(t):
        ag = ags[t % 2]
        for oc in range(8):
            agn = "ag%d_%d" % (t % 2, oc)
            cp_, cpn = cps[oc % 2], "cps%d" % (oc % 2)
            for k in range(31):
                P.mm(cp_[:], dg[:, oc, k, :], ag[:, oc, k:k + N], k == 0, k == 30, ["dg%d" % oc, agn], [cpn])
            P.act(cvt[:, oc, :], cp_[:], AF.Identity, [cpn, "pv"], ["cvt%d" % oc], bias=pvs("cmdwb", oc, oc + 1))
            P.cp("dve", cvb[:, oc, :], cvt[:, oc, :], ["cvt%d" % oc], ["cvb"])
            P.act(cvq[:, oc, :], cvt[:, oc, :], AF.Square, ["cvt%d" % oc], ["cvq"])

    def c_stats(t):
        for oc in range(8):
            P.mm(ssps[:], ones_bf[:], cvb[:, oc, :], oc == 0, oc == 7, ["ones_bf", "cvb"], ["ssps"])
        for oc in range(8):
            P.mm(sqps[:], ones_bf[:], cvq[:, oc, :], oc == 0, oc == 7, ["ones_bf", "cvq"], ["sqps"])
        P.act(mu[:], ssps[:], AF.Identity, ["ssps"], ["mu"], scale=1.0 / D)
        P.tt("dve", msq[:], mu[:], mu[:], ALU.mult, ["mu"], ["msq"])
        P.stt(msq[:], sqps[:], 1.0 / D, msq[:], ALU.mult, ALU.subtract, ["sqps", "msq"], ["msq"])
        P.act(rstd2[:], msq[:], AF.Ln, ["msq", "epsb"], ["rstd2"], bias=epsb[:, 0:1])
        P.act(rstd2[:], rstd2[:], AF.Exp, ["rstd2"], ["rstd2"], scale=-0.5)

    def c_normalize(t, oc):
        cn = "cvt%d" % oc
        P.tt("dve", cvt[:, oc, :], cvt[:, oc, :], mu[:], ALU.subtract, [cn, "mu"], [cn])
        P.tt("dve", cvt[:, oc, :], cvt[:, oc, :], rstd2[:], ALU.mult, [cn, "rstd2"], [cn])
        P.act(lnb[:, oc, :], cvt[:, oc, :], AF.Silu, [cn, "pv"], ["lnb"],
              scale=pvs("cmlng", oc, oc + 1), bias=pvs("cmlnb", oc, oc + 1))

    def c_s4(t):
        xt, xn = xts[t % 3], "xtG%d" % (t % 3)
        for oc in range(8):
            p1, p1n = a1p[oc % 2], "a1p%d" % (oc % 2)
            sg, sgn = sgs[oc % 2], "sgG%d" % (oc % 2)
            for kc in range(8):
                P.mm(p1[:], cwout[:, kc, oc * 128:(oc + 1) * 128], lnb[:, kc, :], kc == 0, kc == 7, ["cwout", "lnb"], [p1n])
            P.act(sg[:], p1[:], AF.Identity, [p1n, "pv"], [sgn], bias=pvs("cmbout", oc, oc + 1))
            P.stt(xt[:, oc, :], sg[:], G_(1, 0)[:, oc:oc + 1], xt[:, oc, :], ALU.mult, ALU.add,
                  [sgn, "modT", xn], [xn])
        store_x(xa_d, xt, xn, slice(t * N, (t + 1) * N), t % 2)

    c_load(0)
    c_load(1)
    c_norm(0)
    c_halo(0)
    for oc in range(8):
        c_s1(0, oc)
    for t in range(NT):
        if t + 2 < NT:
            c_load(t + 2)
        if t + 1 < NT:
            c_norm(t + 1)
        c_s2(t)
        c_stats(t)
        if t + 1 < NT:
            c_halo(t + 1)
        for oc in range(8):
            c_normalize(t, oc)
            if t + 1 < NT:
                c_s1(t + 1, oc)
        c_s4(t)
    P.pop()
    if stop_after == "F":
        P.build()
        return nc

    ffn_phase(1, xa_d, outT, final=True)
    P.build()
    return nc


def _rel_bucket(dist):
    n = np.maximum(dist, 0)
    max_exact = 16
    nf = np.maximum(n, 1).astype(np.float32)
    large = max_exact + (np.log(nf / max_exact) / math.log(128 / max_exact) * (32 - max_exact)).astype(np.int32)
    large = np.minimum(large, 31)
    return np.where(n < max_exact, n, large)


def _t128(v):
    v = np.asarray(v, np.float32)
    return np.ascontiguousarray(v.reshape(-1, 128).T)


def prepare_shared(inp):
    f = lambda a: np.ascontiguousarray(np.asarray(a, np.float32))
    sh = {}
    sh["ident"] = np.eye(128, dtype=np.float32)
    sh["mod_w"] = f(inp["mod_w"])
    sh["w_in0"] = f(inp["ab_w_in"][0])
    sh["glu_w"] = f(inp["ssm_glu_w"][0])
    sh["w_out0"] = f(inp["ab_w_out"][0])
    sh["cm_w_in"] = f(inp["cm_w_in"][0])
    sh["cm_w_out"] = f(inp["cm_w_out"][0])
    sh["ffn_up"] = f(inp["ffn_w_up"])
    sh["ffn_gate"] = f(inp["ffn_w_gate"])
    sh["ffn_down"] = f(inp["ffn_w_down"])

    def gq(a):
        a = np.asarray(a, np.float32)
        rest = a.shape[2:]
        a = a.reshape((16, 2, 64) + rest)
        a = np.moveaxis(a, 0, 2)
        return np.ascontiguousarray(a.reshape((128, 16) + rest))

    lamr = gq(inp["ssm_a_re"][0])
    lami = gq(inp["ssm_a_im"][0])
    logdt = gq(np.repeat(np.asarray(inp["ssm_log_dt"][0], np.float32)[:, None], 64, axis=1))
    sh["s5p"] = np.ascontiguousarray(np.concatenate([lamr, lami, logdt], axis=1))
    br = gq(inp["ssm_b_re"][0])
    bi = gq(inp["ssm_b_im"][0])
    cr = gq(np.transpose(np.asarray(inp["ssm_c_re"][0], np.float32), (0, 2, 1)))
    ci = gq(np.transpose(np.asarray(inp["ssm_c_im"][0], np.float32), (0, 2, 1)))
    sh["s5bc"] = np.ascontiguousarray(np.stack([br, bi, cr, ci], axis=1))
    rb = np.asarray(inp["rel_bias"], np.float32)
    kp = np.arange(128)[:, None]
    qp = np.arange(256)[None, :]
    tabs = []
    for t in range(4):
        if t < 2:
            dist = qp - (128 * t + kp)
        else:
            dist = qp + 256 - (128 * (t - 2) + kp)
        tabs.append(_rel_bucket(dist))
    idx = np.stack(tabs, axis=1)
    sh["biasg"] = np.ascontiguousarray(np.transpose(rb[idx], (3, 0, 1, 2)))
    mc = np.zeros((128, 2, 256), np.float32)
    for t in range(2):
        mc[:, t, :] = np.where(qp - (128 * t + kp) >= 0, 0.0, NEG)
    sh["maskc"] = mc
    mv = np.zeros((128, 32, 16), np.float32)
    no = np.ones((128, 32, 16), np.float32)
    for qt in range(32):
        mv[:, qt, qt // 2:] = -1e30
        no[:, qt, qt // 2] = 0.0
    sh["maskv"] = mv
    sh["notown"] = no
    oh = np.zeros((16, L), np.float32)
    for j in range(16):
        oh[j, 256 * j:256 * (j + 1)] = -NEG
    sh["onehot"] = oh
    return sh


def prepare_pvec(inp, b):
    pvv = np.zeros((128, NPV), np.float32)

    def put(name, arr):
        o, w = PV[name]
        arr = np.asarray(arr, np.float32)
        assert arr.shape == (128, w), (name, arr.shape, w)
        pvv[:, o:o + w] = arr

    put("c", _t128(inp["c"][b]))
    put("modb", np.concatenate([_t128(inp["mod_b"][l]) for l in range(2)], axis=1))
    put("normg", np.concatenate([_t128(inp["norm_g"][l, s]) for l in range(2) for s in range(2)], axis=1))
    put("finalg", _t128(inp["final_g"]))
    put("glub", _t128(inp["ssm_glu_b"][0]))
    put("cmbin", _t128(inp["cm_b_in"][0]))
    dw = np.asarray(inp["cm_dw_w"][0], np.float32)
    put("cmdww", np.transpose(dw.reshape(31, 8, 128), (2, 1, 0)).reshape(128, 248))
    put("cmdwb", _t128(inp["cm_dw_b"][0]))
    put("cmlng", _t128(inp["cm_ln_g"][0]))
    put("cmlnb", _t128(inp["cm_ln_b"][0]))
    put("cmbout", _t128(inp["cm_b_out"][0]))
    fw = np.asarray(inp["ffn_dw_w"], np.float32)
    put("ffdww", np.transpose(fw.reshape(2, 3, 22, 128), (3, 0, 2, 1)).reshape(128, 132))
    put("ffdwb", np.concatenate([_t128(inp["ffn_dw_b"][l]) for l in range(2)], axis=1))
    put("ssmd", _t128(np.asarray(inp["ssm_d"][0], np.float32).reshape(-1)))
    put("b31", np.repeat(np.asarray(inp["rel_bias"], np.float32)[31][None, :], 128, axis=0))
    return pvv


_CACHE = {}


def kernel(**inputs):
    x = np.asarray(inputs["x"], np.float32)
    nb = x.shape[0]
    if "nc" not in _CACHE:
        _CACHE["nc"] = build_program()
    nc = _CACHE["nc"]
    sh = prepare_shared(inputs)
    in_maps = []
    for b in range(nb):
        m = dict(sh)
        m["xT"] = np.ascontiguousarray(x[b].T)
        m["pvec"] = prepare_pvec(inputs, b)
        in_maps.append(m)
    res = run_bass_kernel_spmd(nc, in_maps, core_ids=list(range(nb)))
    out = np.stack([np.ascontiguousarray(r["outT"].T) for r in res.results], axis=0)
    return out.astype(np.float32)
```
